# Optimizing a Trainium2 kernel written in Bass

```python
import jax, jax.numpy as jnp
from jax import lax
import numpy as np

D_MODEL = 1024
BATCH = 8
SEQ = 4096
DEPTH = 1

W_A = 1024
W_B = 1024
K_A = 3
K_B = 31
N_GROUPS_A = 16
N_GROUPS_B = 16
EPS = 1e-6
SPLIT_SIZES = (W_A, W_A, W_A, W_A, W_B, W_B, W_B, D_MODEL, D_MODEL)
D_IN = sum(SPLIT_SIZES)

kernel_name = "hybrid_shortconv_conformer_gated_block"


def rms_norm(x, gain):
    xf = x.astype(jnp.float32)
    y = xf * lax.rsqrt(jnp.mean(xf * xf, axis=-1, keepdims=True) + EPS)
    return (y * gain.astype(jnp.float32)).astype(x.dtype)


def layer_norm(x, gain, bias):
    xf = x.astype(jnp.float32)
    mu = jnp.mean(xf, axis=-1, keepdims=True)
    var = jnp.mean(jnp.square(xf - mu), axis=-1, keepdims=True)
    y = (xf - mu) * lax.rsqrt(var + EPS)
    return (y * gain.astype(jnp.float32) + bias.astype(jnp.float32)).astype(x.dtype)


def depthwise_conv_centred(u, w):
    k, ch = w.shape
    pad = (k - 1) // 2
    return lax.conv_general_dilated(
        u, w[:, None, :].astype(u.dtype), window_strides=(1,), padding=[(pad, pad)],
        dimension_numbers=("NWC", "WIO", "NWC"), feature_group_count=ch)


def split_columns(p):
    idx = np.cumsum(np.array(SPLIT_SIZES))[:-1].tolist()
    return jnp.split(p, idx, axis=-1)


def setup_inputs(seed: int = 0) -> dict:
    key = jax.random.key(seed)
    ks = jax.random.split(key, 20)
    f32 = jnp.float32

    def nrm(k, shape, scale):
        return jax.random.normal(k, shape, f32) * scale

    L, D = DEPTH, D_MODEL
    return {
        "x": nrm(ks[0], (BATCH, SEQ, D), 1.0),
        "c": nrm(ks[1], (BATCH, D), 1.0),
        "norm_gain": 1.0 + nrm(ks[2], (L, D), 0.02),
        "w_ada": nrm(ks[3], (L, D, 3 * D), 0.5 * D ** -0.5),
        "b_ada": nrm(ks[4], (L, 3 * D), 0.02),
        "w_in": nrm(ks[5], (L, D, D_IN), D ** -0.5),
        "b_merge": nrm(ks[6], (L, 2 * D), 0.02),
        "conv_a_w": nrm(ks[7], (L, K_A, W_A), K_A ** -0.5),
        "w_out_a": nrm(ks[8], (L, W_A, D), W_A ** -0.5),
        "conv_b_w": nrm(ks[9], (L, K_B, W_B), K_B ** -0.5),
        "conv_b_bias": nrm(ks[10], (L, W_B), 0.02),
        "ln_b_gain": 1.0 + nrm(ks[11], (L, W_B), 0.02),
        "ln_b_bias": nrm(ks[12], (L, W_B), 0.02),
        "w_out_b": nrm(ks[13], (L, W_B, D), W_B ** -0.5),
        "b_out_b": nrm(ks[14], (L, D), 0.02),
        "w_o": nrm(ks[15], (L, D, D), D ** -0.5),
        "final_gain": 1.0 + nrm(ks[16], (D,), 0.02),
    }


def reference(x, c, norm_gain, w_ada, b_ada, w_in, b_merge, conv_a_w, w_out_a,
              conv_b_w, conv_b_bias, ln_b_gain, ln_b_bias, w_out_b, b_out_b, w_o,
              final_gain):
    c_act = jax.nn.silu(c)
    for l in range(DEPTH):
        mod = c_act @ w_ada[l] + b_ada[l]
        shift, scale, gate = jnp.split(mod, 3, axis=-1)
        h = rms_norm(x, norm_gain[l]) * (1.0 + scale[:, None, :]) + shift[:, None, :]

        proj = h @ w_in[l]
        b_a, c_a, v_a, z_a, a_b, g_b, z_b, m_a, m_b = split_columns(proj)
        bm_a, bm_b = jnp.split(b_merge[l], 2, axis=-1)

        y_a = b_a * depthwise_conv_centred(c_a * v_a, conv_a_w[l])
        y_a = (y_a * jax.nn.silu(z_a)) @ w_out_a[l]

        u_b = a_b * jax.nn.sigmoid(g_b)
        u_b = depthwise_conv_centred(u_b, conv_b_w[l]) + conv_b_bias[l]
        u_b = jax.nn.silu(layer_norm(u_b, ln_b_gain[l], ln_b_bias[l]))
        y_b = (u_b * jax.nn.silu(z_b)) @ w_out_b[l] + b_out_b[l]

        merged = jax.nn.sigmoid(m_a + bm_a) * y_a + jax.nn.sigmoid(m_b + bm_b) * y_b
        x = x + gate[:, None, :] * (merged @ w_o[l])
    return rms_norm(x, final_gain)
```

```python
import numpy as np
from contextlib import ExitStack

import concourse.bass as bass
import concourse.mybir as mybir
from concourse.bass_utils import run_bass_kernel_spmd

F32 = mybir.dt.float32
BF16 = mybir.dt.bfloat16
AF = mybir.ActivationFunctionType
ALU = mybir.AluOpType

D = 1024
S = 4096
NCORES = 8
DIN = 9216
EPS = 1e-6
TT = 512
NT = S // TT
NJ = 8
NPRM = 41
N_PE_TAPS = 19
R_BMA, R_BMB, R_CAW, R_CBW, R_CBB, R_LNG, R_LNB, R_BOB, R_C = 0, 1, 2, 5, 36, 37, 38, 39, 40
BLK_BA, BLK_CA, BLK_VA, BLK_ZA, BLK_AB, BLK_GB, BLK_ZB, BLK_MA, BLK_MB = range(9)


BLOB_SHAPES = [
    ("x", (S, D)), ("c", (D,)), ("norm_gain", (D,)), ("w_ada", (D, 3 * D)), ("b_ada", (3 * D,)),
    ("w_in", (D, DIN)), ("b_merge", (2 * D,)), ("conv_a_w", (3, D)), ("w_out_a", (D, D)),
    ("conv_b_w", (31, D)), ("conv_b_bias", (D,)), ("ln_b_gain", (D,)), ("ln_b_bias", (D,)),
    ("w_out_b", (D, D)), ("b_out_b", (D,)), ("w_o", (D, D)), ("final_gain", (D,)),
]
BLOB_OFF = {}
_o = 0
for _n, _s in BLOB_SHAPES:
    BLOB_OFF[_n] = (_o, _s)
    _o += int(np.prod(_s))
BLOB_N = _o


class Buf:
    __slots__ = ("w", "r")

    def __init__(self):
        self.w = None
        self.r = []


class Slot:
    __slots__ = ("sem", "count")

    def __init__(self, sem):
        self.sem = sem
        self.count = 0


class Sched:
    ENG = ("pe", "act", "dve", "pool", "sp")

    def __init__(self, nc, stack):
        self.nc = nc
        self.stack = stack
        self.ops = {e: [] for e in self.ENG}
        self.tl = {e: stack.enter_context(nc.semaphore("tl_" + e)) for e in ("pe", "act", "dve", "pool")}
        self.cnt = {e: 0 for e in self.tl}
        self.waited = {e: {} for e in self.ENG}
        self.slots = []
        self.nsem = 4

    def slot(self, name):
        s = Slot(self.stack.enter_context(self.nc.semaphore(name)))
        self.slots.append(s)
        self.nsem += 1
        return s

    def _waits(self, engine, deps):
        best = {}
        for tok in deps:
            if tok is None:
                continue
            sem, val, prod = tok
            if prod == engine and engine == "pe":
                continue
            key = sem.num
            if key not in best or best[key][1] < val:
                best[key] = (sem, val)
        for key, (sem, val) in best.items():
            if self.waited[engine].get(key, 0) >= val:
                continue
            self.waited[engine][key] = val
            self.ops[engine].append(lambda eng, sem=sem, val=val: eng.wait_ge(sem, val))

    def _collect(self, reads, writes, deps):
        out = list(deps)
        for b in reads:
            out.append(b.w)
        for b in writes:
            out.extend(b.r)
            out.append(b.w)
        return out

    def _update(self, tok, reads, writes):
        for b in reads:
            b.r.append(tok)
        for b in writes:
            b.w = tok
            b.r = []

    def op(self, engine, fn, reads=(), writes=(), deps=()):
        self._waits(engine, self._collect(reads, writes, deps))
        self.cnt[engine] += 1
        sem = self.tl[engine]
        tok = (sem, self.cnt[engine], engine)
        self.ops[engine].append(lambda eng, fn=fn, sem=sem: fn(eng).then_inc(sem, 1))
        self._update(tok, reads, writes)
        return tok

    def dma(self, queue, slot, out, in_, reads=(), writes=(), deps=(), noncontig=False):
        self._waits(queue, self._collect(reads, writes, deps))
        slot.count += 16
        tok = (slot.sem, slot.count, None)
        nc = self.nc

        def fn(eng, out=out, in_=in_, sem=slot.sem):
            if noncontig:
                with nc.allow_non_contiguous_dma(reason="small one-time parameter load"):
                    eng.dma_start(out=out, in_=in_).then_inc(sem, 16)
            else:
                eng.dma_start(out=out, in_=in_).then_inc(sem, 16)

        self.ops[queue].append(fn)
        self._update(tok, reads, writes)
        return tok

    def barrier(self, scratch_ap):
        toks = [(self.tl[e], self.cnt[e], e) for e in self.tl if self.cnt[e] > 0]
        toks += [(s.sem, s.count, None) for s in self.slots if s.count > 0]
        t = self.op("act", lambda a: a.activation(out=scratch_ap, in_=scratch_ap, func=AF.Identity), deps=toks)
        for e in self.ENG:
            if e != "act":
                self._waits(e, [t])
        return t

    def finish(self):
        toks = [(s.sem, s.count, None) for s in self.slots if s.count > 0]
        self._waits("sp", toks)


def build_nc():
    nc = bass.Bass("TRN2", target_bir_lowering=False)

    blob = nc.dram_tensor("blob", [BLOB_N], F32, kind="ExternalInput").ap()

    def view(name):
        off, shape = BLOB_OFF[name]
        n = int(np.prod(shape))
        v = blob[off:off + n]
        if len(shape) == 2:
            v = v.rearrange("(a b) -> a b", b=shape[1])
        return v

    x = view("x")
    c = view("c")
    norm_gain = view("norm_gain")
    w_ada = view("w_ada")
    b_ada = view("b_ada")
    w_in = view("w_in")
    b_merge = view("b_merge")
    conv_a_w = view("conv_a_w")
    w_out_a = view("w_out_a")
    conv_b_w = view("conv_b_w")
    conv_b_bias = view("conv_b_bias")
    ln_b_gain = view("ln_b_gain")
    ln_b_bias = view("ln_b_bias")
    w_out_b = view("w_out_b")
    b_out_b = view("b_out_b")
    w_o = view("w_o")
    final_gain = view("final_gain")
    y = nc.dram_tensor("y", [S, D], F32, kind="ExternalOutput").ap()

    sp_ya = nc.dram_tensor("sp_ya", [NT, 128, NJ, TT], BF16).ap()
    sp_uc = nc.dram_tensor("sp_uc", [NT, 128, NJ, TT], BF16).ap()
    sp_gz = nc.dram_tensor("sp_gz", [NT, 128, NJ, TT], BF16).ap()
    sp_sa = nc.dram_tensor("sp_sa", [NT, 128, NJ, TT], BF16).ap()
    sp_sb = nc.dram_tensor("sp_sb", [NT, 128, NJ, TT], BF16).ap()

    with ExitStack() as G:
        S_ = Sched(nc, G)

        def sb(stack, name, shape, dt):
            return stack.enter_context(nc.sbuf_tensor(name, shape, dt))

        ident_bf = sb(G, "ident_bf", [128, 128], BF16)
        ident_f = sb(G, "ident_f", [128, 128], F32)
        ones_bf = sb(G, "ones_bf", [128, 128], BF16)
        cols = sb(G, "cols", [128, NJ * NPRM], F32)
        gate_bc = sb(G, "gate_bc", [128, D], F32)
        ss = sb(G, "ss", [128, 32], F32)
        rstd = sb(G, "rstd", [128, 32], F32)
        ss2 = sb(G, "ss2", [128, 32], F32)
        rstd2 = sb(G, "rstd2", [128, 32], F32)
        bar = sb(G, "bar", [128, 2], F32)
        epsc = sb(G, "epsc", [128, 1], F32)
        banks = [nc.alloc_psum_tensor("bank%d" % i, [128, 512], F32) for i in range(8)]
        bankB = [Buf() for _ in range(8)]

        def col(j, r):
            return cols[:, j * NPRM + r: j * NPRM + r + 1]

        b_ident_bf, b_ident_f, b_ones, b_cols, b_gate = Buf(), Buf(), Buf(), Buf(), Buf()
        b_ss, b_ss2 = Buf(), Buf()

        def mk_ident(t):
            def fn(g):
                g.memset(t[:, :], 1.0)
                return g.affine_select(out=t[:, :], in_=t[:, :], pattern=[[-1, 128]], compare_op=ALU.is_equal,
                                       fill=0.0, base=0, channel_multiplier=1)
            return fn

        S_.op("pool", mk_ident(ident_bf), writes=[b_ident_bf])
        S_.op("pool", mk_ident(ident_f), writes=[b_ident_f])
        S_.op("pool", lambda g: g.memset(ones_bf[:, :], 1.0 / 1024.0), writes=[b_ones])
        S_.op("pool", lambda g: g.memset(ss[:, :], 0.0), writes=[b_ss])
        S_.op("pool", lambda g: g.memset(ss2[:, :], 0.0), writes=[b_ss2])
        S_.op("pool", lambda g: g.memset(bar[:, :], 0.0))
        b_eps = Buf()
        S_.op("pool", lambda g: g.memset(epsc[:, :], EPS), writes=[b_eps])

        A = ExitStack()
        hT = sb(A, "hT", [128, NJ, S], BF16)
        wj = [sb(A, "wj%d" % i, [128, 9, NJ, 128], BF16) for i in range(2)]
        b_wj = [Buf(), Buf()]
        sl_wj = [S_.slot("sl_wj%d" % i) for i in range(2)]
        b_hT = [Buf() for _ in range(32)]

        def load_wj(j):
            p = j % 2
            src = w_in.rearrange("(kc p) (blk j m) -> j p blk kc m", p=128, blk=9, j=NJ, m=128)
            t = None
            for blk in range(9):
                t = S_.dma("pool", sl_wj[p], wj[p][:, blk, :, :], src[j, :, blk, :, :],
                           writes=[b_wj[p]] if blk == 0 else [], deps=[] if blk == 0 else [])
            b_wj[p].w = t
            return t

        A0 = ExitStack()
        prm = sb(A0, "prm", [NPRM, D], F32)
        onesF = sb(A0, "onesF", [128, 128], F32)
        lhsT_bc = sb(A0, "lhsT_bc", [128, NJ, 128], F32)
        c_act = sb(A0, "c_act", [128, NJ], F32)
        wada = [sb(A0, "wada%d" % i, [128, 1536], F32) for i in range(2)]
        mod_sb = sb(A0, "mod_sb", [128, 3 * D], F32)
        ng_bc = sb(A0, "ng_bc", [128, D], F32)
        gprime = sb(A0, "gprime", [128, D], F32)
        xt = [sb(A0, "xt%d" % i, [128, D], F32) for i in range(4)]
        t1 = [sb(A0, "t1_%d" % i, [128, D], F32) for i in range(2)]
        hb = [sb(A0, "hb%d" % i, [128, D], BF16) for i in range(2)]
        junk = sb(A0, "junk", [128, D], F32)

        sl_prm = S_.slot("sl_prm")
        b_prm = Buf()
        rows = [
            (R_BMA, 2, b_merge.rearrange("(r n) -> r n", r=2)),
            (R_CAW, 3, conv_a_w),
            (R_CBW, 31, conv_b_w),
            (R_CBB, 1, conv_b_bias.rearrange("(r n) -> r n", r=1)),
            (R_LNG, 1, ln_b_gain.rearrange("(r n) -> r n", r=1)),
            (R_LNB, 1, ln_b_bias.rearrange("(r n) -> r n", r=1)),
            (R_BOB, 1, b_out_b.rearrange("(r n) -> r n", r=1)),
            (R_C, 1, c.rearrange("(r n) -> r n", r=1)),
        ]
        t = None
        for (r0, n, src) in rows:
            t = S_.dma("sp", sl_prm, prm[r0:r0 + n, :], src)
        b_prm.w = t

        def tr_prm(pe):
            ins = None
            for kc in range(NJ):
                ins = pe.transpose(banks[0][:, kc * NPRM:(kc + 1) * NPRM], prm[0:NPRM, kc * 128:(kc + 1) * 128],
                                   ident_f[0:NPRM, 0:NPRM])
            return ins

        S_.op("pe", tr_prm, reads=[b_prm, b_ident_f], writes=[bankB[0]])
        S_.op("dve", lambda v: v.tensor_copy(out=cols[:, :], in_=banks[0][:, 0:NJ * NPRM]),
              reads=[bankB[0]], writes=[b_cols])

        b_cact, b_lhs, b_mod, b_ng, b_gp = Buf(), Buf(), Buf(), Buf(), Buf()
        cols3 = cols[:, :].rearrange("p (k r) -> p k r", r=NPRM)
        S_.op("act", lambda a: a.activation(out=c_act[:, :], in_=cols3[:, :, R_C], func=AF.Silu),
              reads=[b_cols], writes=[b_cact])
        S_.op("pool", lambda g: g.memset(onesF[:, :], 1.0))
        b_onesF = Buf()
        b_onesF.w = (S_.tl["pool"], S_.cnt["pool"], "pool")

        def mk_lhs(v):
            ins = None
            for kc in range(NJ):
                ins = v.tensor_scalar(out=lhsT_bc[:, kc, :], in0=onesF[:, :], scalar1=c_act[:, kc:kc + 1],
                                      scalar2=None, op0=ALU.mult)
            return ins

        S_.op("dve", mk_lhs, reads=[b_cact, b_onesF], writes=[b_lhs])
        sl_misc = S_.slot("sl_misc")
        S_.dma("sp", sl_misc, mod_sb[:, :], b_ada.partition_broadcast(128), writes=[b_mod])
        S_.dma("sp", sl_misc, ng_bc[:, :], norm_gain.partition_broadcast(128), writes=[b_ng])
        b_mod.w = (sl_misc.sem, sl_misc.count, None)
        b_ng.w = (sl_misc.sem, sl_misc.count, None)

        sl_wada = [S_.slot("sl_wada%d" % i) for i in range(2)]
        b_wada = [Buf(), Buf()]
        q = 0
        for kc in range(NJ):
            for h in range(2):
                p = q % 2
                q += 1
                S_.dma("sp", sl_wada[p], wada[p][:, :], w_ada[kc * 128:(kc + 1) * 128, h * 1536:(h + 1) * 1536],
                       writes=[b_wada[p]])

                def mm(pe, kc=kc, h=h, p=p):
                    ins = None
                    for n in range(3):
                        ins = pe.matmul(banks[1 + h * 3 + n][:, :], lhsT=lhsT_bc[:, kc, :],
                                        rhs=wada[p][:, n * 512:(n + 1) * 512], start=(kc == 0), stop=(kc == NJ - 1))
                    return ins

                S_.op("pe", mm, reads=[b_wada[p], b_lhs], writes=[bankB[1 + h * 3 + n] for n in range(3)])
        for n in range(6):
            S_.op("dve", lambda v, n=n: v.tensor_tensor(out=mod_sb[:, n * 512:(n + 1) * 512],
                                                        in0=banks[1 + n][:, :], in1=mod_sb[:, n * 512:(n + 1) * 512],
                                                        op=ALU.add),
                  reads=[bankB[1 + n]], writes=[b_mod])
        S_.op("dve", lambda v: v.scalar_tensor_tensor(out=gprime[:, :], in0=mod_sb[:, D:2 * D], scalar=1.0,
                                                      in1=ng_bc[:, :], op0=ALU.add, op1=ALU.mult),
              reads=[b_mod, b_ng], writes=[b_gp])
        S_.op("dve", lambda v: v.tensor_copy(out=gate_bc[:, :], in_=mod_sb[:, 2 * D:3 * D]),
              reads=[b_mod], writes=[b_gate])

        load_wj(0)

        NXT = 4
        sl_xt = [S_.slot("sl_xt%d" % i) for i in range(NXT)]
        b_xt = [Buf() for _ in range(NXT)]
        b_t1 = [Buf(), Buf()]
        b_hb = [Buf(), Buf()]
        b_rstd = [Buf() for _ in range(32)]

        def p0_load(tc):
            s4 = tc % NXT
            S_.dma("sp", sl_xt[s4], xt[s4][:, :], x[tc * 128:(tc + 1) * 128, :], writes=[b_xt[s4]])

        def p0_stats_act(tc):
            s4 = tc % NXT
            tss = S_.op("act", lambda a, s4=s4, tc=tc: a.activation(out=junk[:, :], in_=xt[s4][:, :], func=AF.Square,
                                                                    accum_out=ss[:, tc:tc + 1]),
                        reads=[b_xt[s4], b_ss], writes=[])
            return S_.op("act", lambda a, tc=tc: a.activation(out=rstd[:, tc:tc + 1], in_=ss[:, tc:tc + 1],
                                                              func=AF.Sqrt, scale=1.0 / D, bias=epsc[:, 0:1]),
                         reads=[b_eps], deps=[tss])

        def p0_recip(tc, tsq):
            S_.op("dve", lambda v, tc=tc: v.reciprocal(out=rstd[:, tc:tc + 1], in_=rstd[:, tc:tc + 1]),
                  deps=[tsq], writes=[b_rstd[tc]])

        def p0_main(tc):
            s4 = tc % NXT
            s2 = tc % 2
            S_.op("dve", lambda v, s4=s4, s2=s2, tc=tc: v.scalar_tensor_tensor(
                out=t1[s2][:, :], in0=xt[s4][:, :], scalar=rstd[:, tc:tc + 1], in1=gprime[:, :],
                op0=ALU.mult, op1=ALU.mult),
                reads=[b_xt[s4], b_rstd[tc], b_gp], writes=[b_t1[s2]])
            S_.op("dve", lambda g, s2=s2: g.tensor_tensor(out=hb[s2][:, :], in0=t1[s2][:, :], in1=mod_sb[:, 0:D],
                                                          op=ALU.add),
                  reads=[b_t1[s2], b_mod], writes=[b_hb[s2]])
            bk = 6 + s2
            bbf = banks[bk][:, :].bitcast(BF16)

            def trs(pe, s2=s2, bbf=bbf):
                ins = None
                for kc in range(NJ):
                    ins = pe.transpose(bbf[:, kc * 128:(kc + 1) * 128], hb[s2][:, kc * 128:(kc + 1) * 128],
                                       ident_bf[:, :])
                return ins

            S_.op("pe", trs, reads=[b_hb[s2], b_ident_bf], writes=[bankB[bk]])

        def p0_evac(tc):
            bk = 6 + tc % 2
            bbf = banks[bk][:, :].bitcast(BF16)
            S_.op("act", lambda a, tc=tc, bbf=bbf: a.activation(
                out=hT[:, :, tc * 128:(tc + 1) * 128], in_=bbf.rearrange("p (k t) -> p k t", k=NJ),
                func=AF.Identity),
                reads=[bankB[bk]], writes=[b_hT[tc]])

        p0_load(0)
        p0_load(1)
        tsqs = {0: p0_stats_act(0)}
        p0_recip(0, tsqs[0])
        for tc in range(33):
            if tc + 2 < 32:
                p0_load(tc + 2)
            if tc + 1 < 32:
                tsqs[tc + 1] = p0_stats_act(tc + 1)
            if tc < 32:
                p0_main(tc)
            if tc + 1 < 32:
                p0_recip(tc + 1, tsqs[tc + 1])
            if tc >= 1:
                p0_evac(tc - 1)

        S_.barrier(bar[:, 0:1])
        A0.close()

        A1 = ExitStack()
        diag = [sb(A1, "diag%d" % i, [128, 31, 128], BF16) for i in range(2)]
        cvb = sb(A1, "cvb", [128, S + 2], F32)
        ub = sb(A1, "ub", [128, S + 30], BF16)
        sz_sb = [sb(A1, "sz%d" % i, [128, TT], F32) for i in range(2)]
        ca_sb = [sb(A1, "ca%d" % i, [128, TT], F32) for i in range(2)]
        sg_sb = [sb(A1, "sg%d" % i, [128, TT], F32) for i in range(2)]
        ga = [sb(A1, "ga%d" % i, [128, TT], F32) for i in range(3)]
        acc = [sb(A1, "acc%d" % i, [128, TT], F32) for i in range(2)]
        cacc = [sb(A1, "cacc%d" % i, [128, TT], F32) for i in range(2)]
        b_cacc = [Buf(), Buf()]
        st_names = ("gz", "sa", "sb", "ya", "uc")
        stg = {n: [sb(A1, "st_%s%d" % (n, i), [128, TT], BF16) for i in range(2)] for n in st_names}
        b_stg = {n: [Buf(), Buf()] for n in st_names}
        sl_stg = {n: [S_.slot("sl_%s%d" % (n, i)) for i in range(2)] for n in st_names}
        sp_of = {"gz": sp_gz, "sa": sp_sa, "sb": sp_sb, "ya": sp_ya, "uc": sp_uc}
        b_sp = {n: [Buf() for _ in range(NT)] for n in st_names}
        b_diag = [Buf(), Buf()]
        b_cv = [Buf() for _ in range(NT)]
        b_u = [Buf() for _ in range(NT)]
        b_sz, b_ca, b_sg = [Buf(), Buf()], [Buf(), Buf()], [Buf(), Buf()]
        b_ga = [Buf(), Buf(), Buf()]
        b_acc = [Buf(), Buf()]

        S_.op("pool", lambda g: g.memset(cvb[:, 0:1], 0.0))
        S_.op("pool", lambda g: g.memset(cvb[:, S + 1:S + 2], 0.0))
        S_.op("pool", lambda g: g.memset(ub[:, 0:15], 0.0))
        S_.op("pool", lambda g: g.memset(ub[:, S + 15:S + 30], 0.0))
        tpad = (S_.tl["pool"], S_.cnt["pool"], "pool")

        ring = [0]
        convring = [0]

        def next_bank():
            b = ring[0] % 6
            ring[0] += 1
            return b

        def proj(j, i, blk):
            p = j % 2
            bk = next_bank()

            def fn(pe, p=p, blk=blk, i=i, bk=bk):
                ins = None
                for kc in range(NJ):
                    ins = pe.matmul(banks[bk][:, :], lhsT=wj[p][:, blk, kc, :], rhs=hT[:, kc, i * TT:(i + 1) * TT],
                                    start=(kc == 0), stop=(kc == NJ - 1))
                return ins

            S_.op("pe", fn, reads=[b_wj[p]] + b_hT[i * 4:(i + 1) * 4], writes=[bankB[bk]])
            return bk

        def store(name, slot_i, il, j):
            S_.dma("sp", sl_stg[name][slot_i], sp_of[name][il, :, j, :], stg[name][slot_i][:, :],
                   reads=[b_stg[name][slot_i]], writes=[b_sp[name][il]])

        stc = [0]

        def lagged(j, il):
            p = j % 2
            c0 = il * TT
            s = stc[0] % 2
            stc[0] += 1
            cb = 6 + (convring[0] % 2)
            convring[0] += 1

            def cfn(pe, p=p, c0=c0, cb=cb):
                ins = None
                for k in range(N_PE_TAPS):
                    ins = pe.matmul(banks[cb][:, :], lhsT=diag[p][:, k, :], rhs=ub[:, c0 + k:c0 + k + TT],
                                    start=(k == 0), stop=(k == N_PE_TAPS - 1))
                return ins

            ureads = [b_u[t] for t in (il - 1, il, il + 1) if 0 <= t < NT]
            S_.op("pe", cfn, reads=[b_diag[p]] + ureads, writes=[bankB[cb]], deps=[tpad])
            S_.op("act", lambda a, s=s, cb=cb, j=j: a.activation(out=cacc[s][:, :], in_=banks[cb][:, :],
                                                                 func=AF.Identity, bias=col(j, R_CBB)),
                  reads=[bankB[cb], b_cols], writes=[b_cacc[s]])
            for k in range(N_PE_TAPS, 31):
                last = (k == 30)
                dst = stg["uc"][s] if last else cacc[s]
                S_.op("dve", lambda v, s=s, c0=c0, j=j, k=k, dst=dst: v.scalar_tensor_tensor(
                    out=dst[:, :], in0=ub[:, c0 + k:c0 + k + TT], scalar=col(j, R_CBW + k), in1=cacc[s][:, :],
                    op0=ALU.mult, op1=ALU.add),
                    reads=ureads + [b_cacc[s]] + ([b_cols] if k == N_PE_TAPS else []),
                    writes=[b_stg["uc"][s]] if last else [b_cacc[s]])
            store("uc", s, il, j)
            a_ = s
            cvreads = [b_cv[t] for t in (il - 1, il, il + 1) if 0 <= t < NT]
            S_.op("dve", lambda g, a_=a_, c0=c0, j=j: g.tensor_scalar(
                out=acc[a_][:, :], in0=cvb[:, c0:c0 + TT], scalar1=col(j, R_CAW), scalar2=None, op0=ALU.mult),
                reads=cvreads + [b_cols], writes=[b_acc[a_]], deps=[tpad])
            for k in (1, 2):
                S_.op("dve", lambda g, a_=a_, c0=c0, j=j, k=k: g.scalar_tensor_tensor(
                    out=acc[a_][:, :], in0=cvb[:, c0 + k:c0 + k + TT], scalar=col(j, R_CAW + k), in1=acc[a_][:, :],
                    op0=ALU.mult, op1=ALU.add),
                    reads=cvreads, writes=[b_acc[a_]])
            S_.op("dve", lambda g, a_=a_, s=s, il=il: g.tensor_tensor(
                out=stg["ya"][s][:, :], in0=acc[a_][:, :], in1=ga[il % 3][:, :], op=ALU.mult),
                reads=[b_acc[a_], b_ga[il % 3]], writes=[b_stg["ya"][s]])
            store("ya", s, il, j)

        def build_diag(j):
            p = j % 2
            S_.op("dve", lambda v, p=p, j=j: v.tensor_tensor(
                out=diag[p][:, :, :],
                in0=ident_bf[:, :].unsqueeze(1).to_broadcast([128, 31, 128]),
                in1=cols[:, j * NPRM + R_CBW:j * NPRM + R_CBW + 31].unsqueeze(2).to_broadcast([128, 31, 128]),
                op=ALU.mult),
                reads=[b_ident_bf, b_cols], writes=[b_diag[p]])

        build_diag(0)
        for j in range(NJ):
            p = j % 2
            if j + 1 < NJ:
                load_wj(j + 1)

            if j + 1 < NJ:
                build_diag(j + 1)

            for i in range(NT):
                c0 = i * TT
                s = (j * NT + i) % 2
                bk_za = proj(j, i, BLK_ZA)
                bk_zb = proj(j, i, BLK_ZB)
                bk_ba = proj(j, i, BLK_BA)
                bk_gb = proj(j, i, BLK_GB)
                bk_ma = proj(j, i, BLK_MA)
                bk_mb = proj(j, i, BLK_MB)
                S_.op("act", lambda a, s=s, bk=bk_za: a.activation(out=sz_sb[s][:, :], in_=banks[bk][:, :],
                                                                   func=AF.Silu),
                      reads=[bankB[bk_za]], writes=[b_sz[s]])
                S_.op("act", lambda a, s=s, bk=bk_zb: a.activation(out=stg["gz"][s][:, :], in_=banks[bk][:, :],
                                                                   func=AF.Silu),
                      reads=[bankB[bk_zb]], writes=[b_stg["gz"][s]])
                store("gz", s, i, j)
                S_.op("dve", lambda v, s=s, bk=bk_ba, i=i: v.tensor_tensor(
                    out=ga[i % 3][:, :], in0=banks[bk][:, :], in1=sz_sb[s][:, :], op=ALU.mult),
                    reads=[bankB[bk_ba], b_sz[s]], writes=[b_ga[i % 3]])
                S_.op("act", lambda a, s=s, bk=bk_gb: a.activation(out=sg_sb[s][:, :], in_=banks[bk][:, :],
                                                                   func=AF.Sigmoid),
                      reads=[bankB[bk_gb]], writes=[b_sg[s]])
                S_.op("act", lambda a, s=s, bk=bk_ma, j=j: a.activation(
                    out=stg["sa"][s][:, :], in_=banks[bk][:, :], func=AF.Sigmoid, bias=col(j, R_BMA)),
                    reads=[bankB[bk_ma], b_cols], writes=[b_stg["sa"][s]])
                store("sa", s, i, j)
                S_.op("act", lambda a, s=s, bk=bk_mb, j=j: a.activation(
                    out=stg["sb"][s][:, :], in_=banks[bk][:, :], func=AF.Sigmoid, bias=col(j, R_BMB)),
                    reads=[bankB[bk_mb], b_cols], writes=[b_stg["sb"][s]])
                store("sb", s, i, j)
                bk_ab = proj(j, i, BLK_AB)
                bk_ca = proj(j, i, BLK_CA)
                bk_va = proj(j, i, BLK_VA)
                S_.op("dve", lambda v, s=s, bk=bk_ab, c0=c0: v.tensor_tensor(
                    out=ub[:, 15 + c0:15 + c0 + TT], in0=banks[bk][:, :], in1=sg_sb[s][:, :], op=ALU.mult),
                    reads=[bankB[bk_ab], b_sg[s]], writes=[b_u[i]])
                S_.op("act", lambda a, s=s, bk=bk_ca: a.activation(out=ca_sb[s][:, :], in_=banks[bk][:, :],
                                                                   func=AF.Identity),
                      reads=[bankB[bk_ca]], writes=[b_ca[s]])
                S_.op("dve", lambda v, s=s, bk=bk_va, c0=c0: v.tensor_tensor(
                    out=cvb[:, 1 + c0:1 + c0 + TT], in0=banks[bk][:, :], in1=ca_sb[s][:, :], op=ALU.mult),
                    reads=[bankB[bk_va], b_ca[s]], writes=[b_cv[i]])
                if i >= 1:
                    lagged(j, i - 1)
                if i == NT - 1:
                    lagged(j, i)

        S_.barrier(bar[:, 0:1])
        A1.close()
        A.close()

        B = ExitStack()
        woa_bf = sb(B, "woa_bf", [128, NJ, D], BF16)
        wob_bf = sb(B, "wob_bf", [128, NJ, D], BF16)
        wo_bf = sb(B, "wo_bf", [128, NJ, D], BF16)
        fg_bc = sb(B, "fg_bc", [128, D], F32)
        ld_names = ("uc", "ya", "gz", "sa", "sb")
        nring = {"uc": 2, "ya": 2, "gz": 1, "sa": 1, "sb": 1}
        CHUNKED = ("gz", "sa", "sb")
        ldt = {n: [sb(B, "ld_%s%d" % (n, i), [128, NJ, TT], BF16) for i in range(nring[n])] for n in ld_names}
        b_ldt = {n: [Buf() for _ in range(nring[n])] for n in ("uc", "ya")}
        sl_ldt = {n: [S_.slot("sl_ld_%s%d" % (n, i)) for i in range(nring[n])] for n in ("uc", "ya")}
        b_ldc = {n: [Buf() for _ in range(NJ)] for n in CHUNKED}
        sl_ldc = {n: [S_.slot("sl_ldc_%s%d" % (n, k)) for k in range(NJ)] for n in CHUNKED}
        sq = [sb(B, "sq%d" % i, [128, TT], BF16) for i in range(2)]
        mean_sb = [sb(B, "mean_sb%d" % i, [128, TT], F32) for i in range(2)]
        m2 = sb(B, "m2", [128, TT], F32)
        rstd_t = [sb(B, "rstd_t%d" % i, [128, TT], F32) for i in range(2)]
        tA = [sb(B, "tA%d" % i, [128, TT], F32) for i in range(2)]
        tB = [sb(B, "tB%d" % i, [128, TT], F32) for i in range(2)]
        slu = [sb(B, "slu%d" % i, [128, TT], F32) for i in range(2)]
        ubp = [sb(B, "ubp%d" % i, [128, NJ, TT], BF16) for i in range(2)]
        u1 = [sb(B, "u1_%d" % i, [128, TT], F32) for i in range(2)]
        u2 = [sb(B, "u2_%d" % i, [128, TT], F32) for i in range(2)]
        mg = sb(B, "mg", [128, NJ, TT], BF16)
        NX2 = 3
        xt2 = [sb(B, "xt2_%d" % i, [128, D], F32) for i in range(NX2)]
        xn = [sb(B, "xn%d" % i, [128, D], F32) for i in range(2)]
        ot = [sb(B, "ot%d" % i, [128, D], F32) for i in range(2)]
        junk2 = sb(B, "junk2", [128, D], BF16)

        b_woa, b_wob, b_wo, b_fg = Buf(), Buf(), Buf(), Buf()
        sl_wos = [S_.slot("sl_wos%d" % i) for i in range(2)]
        sl_w2 = S_.slot("sl_w2")
        b_sq = [Buf(), Buf()]
        b_mean, b_rstdt = [Buf(), Buf()], [Buf(), Buf()]
        b_m2 = Buf()
        b_tA, b_tB, b_slu = [Buf(), Buf()], [Buf(), Buf()], [Buf(), Buf()]
        b_ubp = [[Buf() for _ in range(NJ)] for _ in range(2)]
        b_u1, b_u2 = [Buf(), Buf()], [Buf(), Buf()]
        b_mg = [Buf() for _ in range(NJ)]
        b_xt2 = [Buf() for _ in range(NX2)]
        sl_xt2 = [S_.slot("sl_xt2_%d" % i) for i in range(NX2)]
        b_xn, b_ot = [Buf(), Buf()], [Buf(), Buf()]
        sl_ot = [S_.slot("sl_ot%d" % i) for i in range(2)]
        b_rstd2 = [Buf() for _ in range(32)]

        def load_tile(i, names):
            for n in names:
                if n in CHUNKED:
                    for k in range(NJ):
                        load_chunk(n, i, k)
                    continue
                r = i % nring[n]
                S_.dma("sp", sl_ldt[n][r], ldt[n][r][:, :, :], sp_of[n][i, :, :, :],
                       reads=[b_sp[n][i]], writes=[b_ldt[n][r]])

        def load_chunk(n, i, k):
            S_.dma("sp", sl_ldc[n][k], ldt[n][0][:, k, :], sp_of[n][i, :, k, :],
                   reads=[b_sp[n][i]], writes=[b_ldc[n][k]])

        def T(n, i):
            return ldt[n][i % nring[n]]

        def BT(n, i):
            return b_ldt[n][i % nring[n]]

        REST = ("gz", "ya", "sa", "sb")
        load_tile(0, ("uc",))
        load_tile(1, ("uc",))
        load_tile(0, REST)
        S_.dma("sp", sl_w2, fg_bc[:, :], final_gain.partition_broadcast(128), writes=[b_fg])
        S_.dma("pool", sl_w2, woa_bf[:, :, :], w_out_a.rearrange("(kc p) n -> p kc n", p=128), writes=[b_woa])
        S_.dma("pool", sl_w2, wob_bf[:, :, :], w_out_b.rearrange("(kc p) n -> p kc n", p=128), writes=[b_wob])
        tw2 = (sl_w2.sem, sl_w2.count, None)
        b_fg.w = tw2
        b_woa.w = tw2
        b_wob.w = tw2
        wo_stage, b_wos = xn, b_xn
        for kc in range(NJ):
            p = kc % 2
            S_.dma("sp", sl_wos[p], wo_stage[p][:, :], w_o[kc * 128:(kc + 1) * 128, :], writes=[b_wos[p]])
            S_.op("dve", lambda v, kc=kc, p=p: v.tensor_tensor(out=wo_bf[:, kc, :], in0=wo_stage[p][:, :],
                                                               in1=gate_bc[:, :], op=ALU.mult),
                  reads=[b_wos[p], b_gate], writes=[b_wo])

        BK_MEAN, BK_MSQ = 0, 1
        yring = [0]
        oring = [0]

        def s1_sq(i, k):
            s = k % 2
            S_.op("act", lambda a, s=s, k=k, t=T("uc", i): a.activation(out=sq[s][:, :], in_=t[:, k, :],
                                                                        func=AF.Square),
                  reads=[BT("uc", i)], writes=[b_sq[s]])

        def s1_mm(i, k):
            s = k % 2

            def stat(pe, s=s, k=k, t=T("uc", i)):
                pe.matmul(banks[BK_MEAN][:, :], lhsT=ones_bf[:, :], rhs=t[:, k, :], start=(k == 0),
                          stop=(k == NJ - 1))
                return pe.matmul(banks[BK_MSQ][:, :], lhsT=ones_bf[:, :], rhs=sq[s][:, :], start=(k == 0),
                                 stop=(k == NJ - 1))

            S_.op("pe", stat, reads=[BT("uc", i), b_sq[s], b_ones], writes=[bankB[BK_MEAN], bankB[BK_MSQ]])

        def s1_fin(i):
            r = i % 2
            S_.op("act", lambda a, r=r: a.activation(out=mean_sb[r][:, :], in_=banks[BK_MEAN][:, :],
                                                     func=AF.Identity),
                  reads=[bankB[BK_MEAN]], writes=[b_mean[r]])
            S_.op("dve", lambda v, r=r: v.tensor_tensor(out=m2[:, :], in0=mean_sb[r][:, :], in1=mean_sb[r][:, :],
                                                        op=ALU.mult),
                  reads=[b_mean[r]], writes=[b_m2])
            S_.op("dve", lambda v: v.tensor_tensor(out=m2[:, :], in0=banks[BK_MSQ][:, :], in1=m2[:, :],
                                                   op=ALU.subtract),
                  reads=[bankB[BK_MSQ], b_m2], writes=[b_m2])
            S_.op("dve", lambda v: v.tensor_scalar(out=m2[:, :], in0=m2[:, :], scalar1=0.0, scalar2=None,
                                                   op0=ALU.max),
                  reads=[b_m2], writes=[b_m2])
            S_.op("act", lambda a, r=r: a.activation(out=rstd_t[r][:, :], in_=m2[:, :], func=AF.Sqrt,
                                                     bias=epsc[:, 0:1]),
                  reads=[b_m2, b_eps], writes=[b_rstdt[r]])
            S_.op("dve", lambda v, r=r: v.reciprocal(out=rstd_t[r][:, :], in_=rstd_t[r][:, :]),
                  reads=[b_rstdt[r]], writes=[b_rstdt[r]])

        def s2_a(i, k):
            s = k % 2
            r = i % 2
            S_.op("dve", lambda v, s=s, k=k, r=r, t=T("uc", i): v.tensor_tensor(
                out=tA[s][:, :], in0=t[:, k, :], in1=mean_sb[r][:, :], op=ALU.subtract),
                reads=[BT("uc", i), b_mean[r]], writes=[b_tA[s]])
            S_.op("dve", lambda g, s=s, r=r: g.tensor_tensor(out=tB[s][:, :], in0=tA[s][:, :], in1=rstd_t[r][:, :],
                                                             op=ALU.mult),
                  reads=[b_tA[s], b_rstdt[r]], writes=[b_tB[s]])
            S_.op("act", lambda a, s=s, k=k: a.activation(out=slu[s][:, :], in_=tB[s][:, :], func=AF.Silu,
                                                          scale=col(k, R_LNG), bias=col(k, R_LNB)),
                  reads=[b_tB[s], b_cols], writes=[b_slu[s]])

        def s2_b(i, k):
            s = k % 2
            r = i % 2
            S_.op("dve", lambda v, s=s, k=k, r=r, t=T("gz", i): v.tensor_tensor(
                out=ubp[r][:, k, :], in0=slu[s][:, :], in1=t[:, k, :], op=ALU.mult),
                reads=[b_slu[s], b_ldc["gz"][k]], writes=[b_ubp[r][k]])

        def s3(i, dj):
            s = dj % 2
            r = i % 2
            bka = 2 + (yring[0] % 4)
            yring[0] += 1
            bkb = 2 + (yring[0] % 4)
            yring[0] += 1

            def mma(pe, dj=dj, bka=bka, t=T("ya", i)):
                ins = None
                for kc in range(NJ):
                    ins = pe.matmul(banks[bka][:, :], lhsT=woa_bf[:, kc, dj * 128:(dj + 1) * 128],
                                    rhs=t[:, kc, :], start=(kc == 0), stop=(kc == NJ - 1))
                return ins

            def mmb(pe, dj=dj, bkb=bkb, r=r):
                ins = None
                for kc in range(NJ):
                    ins = pe.matmul(banks[bkb][:, :], lhsT=wob_bf[:, kc, dj * 128:(dj + 1) * 128],
                                    rhs=ubp[r][:, kc, :], start=(kc == 0), stop=(kc == NJ - 1))
                return ins

            S_.op("pe", mma, reads=[b_woa, BT("ya", i)], writes=[bankB[bka]])
            S_.op("pe", mmb, reads=[b_wob] + b_ubp[r], writes=[bankB[bkb]])
            S_.op("dve", lambda v, s=s, dj=dj, bka=bka, t=T("sa", i): v.tensor_tensor(
                out=u1[s][:, :], in0=banks[bka][:, :], in1=t[:, dj, :], op=ALU.mult),
                reads=[bankB[bka], b_ldc["sa"][dj]], writes=[b_u1[s]])
            S_.op("dve", lambda v, s=s, dj=dj, bkb=bkb, t=T("sb", i): v.scalar_tensor_tensor(
                out=u2[s][:, :], in0=banks[bkb][:, :], scalar=col(dj, R_BOB), in1=t[:, dj, :],
                op0=ALU.add, op1=ALU.mult),
                reads=[bankB[bkb], b_ldc["sb"][dj], b_cols], writes=[b_u2[s]])
            S_.op("dve", lambda g, s=s, dj=dj: g.tensor_tensor(out=mg[:, dj, :], in0=u1[s][:, :],
                                                               in1=u2[s][:, :], op=ALU.add),
                  reads=[b_u1[s], b_u2[s]], writes=[b_mg[dj]])

        def s4_load(i, tq):
            tcg = i * 4 + tq
            s4 = tcg % NX2
            S_.dma("sp", sl_xt2[s4], xt2[s4][:, :], x[tcg * 128:(tcg + 1) * 128, :], writes=[b_xt2[s4]])

        def s4_a(i, tq):
            tcg = i * 4 + tq
            s4 = tcg % NX2
            s2 = tcg % 2
            for dh in range(2):
                bko = 6 + (oring[0] % 2)
                oring[0] += 1

                def mmo(pe, tq=tq, dh=dh, bko=bko):
                    ins = None
                    for kc in range(NJ):
                        ins = pe.matmul(banks[bko][:, :], lhsT=mg[:, kc, tq * 128:(tq + 1) * 128],
                                        rhs=wo_bf[:, kc, dh * 512:(dh + 1) * 512], start=(kc == 0),
                                        stop=(kc == NJ - 1))
                    return ins

                S_.op("pe", mmo, reads=[b_wo] + b_mg, writes=[bankB[bko]])
                S_.op("dve", lambda v, s2=s2, s4=s4, dh=dh, bko=bko: v.tensor_tensor(
                    out=xn[s2][:, dh * 512:(dh + 1) * 512], in0=banks[bko][:, :],
                    in1=xt2[s4][:, dh * 512:(dh + 1) * 512], op=ALU.add),
                    reads=[bankB[bko], b_xt2[s4]], writes=[b_xn[s2]])
            tss = S_.op("act", lambda a, s2=s2, tcg=tcg: a.activation(out=junk2[:, :], in_=xn[s2][:, :],
                                                                      func=AF.Square, accum_out=ss2[:, tcg:tcg + 1]),
                        reads=[b_xn[s2], b_ss2], writes=[])
            return S_.op("act", lambda a, tcg=tcg: a.activation(out=rstd2[:, tcg:tcg + 1], in_=ss2[:, tcg:tcg + 1],
                                                                func=AF.Sqrt, scale=1.0 / D, bias=epsc[:, 0:1]),
                         reads=[b_eps], deps=[tss])

        def s4_b(i, tq, tsq):
            tcg = i * 4 + tq
            s2 = tcg % 2
            S_.op("dve", lambda v, tcg=tcg: v.reciprocal(out=rstd2[:, tcg:tcg + 1], in_=rstd2[:, tcg:tcg + 1]),
                  deps=[tsq], writes=[b_rstd2[tcg]])
            S_.op("dve", lambda v, s2=s2, tcg=tcg: v.scalar_tensor_tensor(
                out=ot[s2][:, :], in0=xn[s2][:, :], scalar=rstd2[:, tcg:tcg + 1], in1=fg_bc[:, :],
                op0=ALU.mult, op1=ALU.mult),
                reads=[b_xn[s2], b_rstd2[tcg], b_fg], writes=[b_ot[s2]])
            S_.dma("sp", sl_ot[s2], y[tcg * 128:(tcg + 1) * 128, :], ot[s2][:, :], reads=[b_ot[s2]])

        for k in range(NJ):
            s1_sq(0, k)
            s1_mm(0, k)
        s1_fin(0)
        for k in range(NJ):
            s2_a(0, k)
            if k >= 1:
                s2_b(0, k - 1)
        s2_b(0, NJ - 1)
        load_tile(1, ("gz",))
        for k in range(NJ):
            s1_sq(1, k)
            s1_mm(1, k)
        s1_fin(1)

        for i in range(NT):
            if i + 2 < NT:
                load_tile(i + 2, ("uc",))
            if i + 1 < NT:
                load_tile(i + 1, ("ya",))
            for tq in range(3):
                s4_load(i, tq)
            for dj in range(NJ):
                s3(i, dj)
                if i + 1 < NT:
                    load_chunk("sa", i + 1, dj)
                    load_chunk("sb", i + 1, dj)
                    s2_a(i + 1, dj)
                    if dj >= 1:
                        s2_b(i + 1, dj - 1)
                        if i + 2 < NT:
                            load_chunk("gz", i + 2, dj - 1)
            if i + 1 < NT:
                s2_b(i + 1, NJ - 1)
                if i + 2 < NT:
                    load_chunk("gz", i + 2, NJ - 1)
            if i + 2 < NT:
                for k in range(NJ):
                    s1_sq(i + 2, k)
                    s1_mm(i + 2, k)
            prev = None
            for tq in range(4):
                tsq = s4_a(i, tq)
                if tq == 0:
                    s4_load(i, 3)
                if prev is not None:
                    s4_b(i, tq - 1, prev)
                prev = tsq
            s4_b(i, 3, prev)
            if i + 2 < NT:
                s1_fin(i + 2)

        S_.finish()
        B.close()

        with nc.Block() as block:
            @block.tensor
            def _(e):
                for f in S_.ops["pe"]:
                    f(e)

            @block.scalar
            def _(e):
                for f in S_.ops["act"]:
                    f(e)

            @block.vector
            def _(e):
                for f in S_.ops["dve"]:
                    f(e)

            @block.gpsimd
            def _(e):
                for f in S_.ops["pool"]:
                    f(e)

            @block.sync
            def _(e):
                for f in S_.ops["sp"]:
                    f(e)
    return nc


_NC = None


def kernel(x, c, norm_gain, w_ada, b_ada, w_in, b_merge, conv_a_w, w_out_a, conv_b_w, conv_b_bias,
           ln_b_gain, ln_b_bias, w_out_b, b_out_b, w_o, final_gain):
    global _NC
    if _NC is None:
        _NC = build_nc()
    nc = _NC

    vals = {
        "norm_gain": norm_gain[0], "w_ada": w_ada[0], "b_ada": b_ada[0], "w_in": w_in[0],
        "b_merge": b_merge[0], "conv_a_w": conv_a_w[0], "w_out_a": w_out_a[0],
        "conv_b_w": conv_b_w[0], "conv_b_bias": conv_b_bias[0], "ln_b_gain": ln_b_gain[0],
        "ln_b_bias": ln_b_bias[0], "w_out_b": w_out_b[0], "b_out_b": b_out_b[0], "w_o": w_o[0],
        "final_gain": final_gain,
    }
    x = np.asarray(x)
    c = np.asarray(c)
    base = np.empty((BLOB_N,), dtype=np.float32)
    for name, shape in BLOB_SHAPES:
        if name in ("x", "c"):
            continue
        off, _ = BLOB_OFF[name]
        a = np.asarray(vals[name], dtype=np.float32).reshape(-1)
        base[off:off + a.size] = a
    in_maps = []
    ox, _ = BLOB_OFF["x"]
    oc, _ = BLOB_OFF["c"]
    for b in range(NCORES):
        m = base.copy()
        m[ox:ox + S * D] = np.asarray(x[b], dtype=np.float32).reshape(-1)
        m[oc:oc + D] = np.asarray(c[b], dtype=np.float32).reshape(-1)
        in_maps.append({"blob": m})
    res = run_bass_kernel_spmd(nc, in_maps, core_ids=list(range(NCORES)))
    out = np.stack([np.asarray(res.results[b]["y"], dtype=np.float32) for b in range(NCORES)], axis=0)
    return out
```

```python
import numpy as np
from contextlib import ExitStack

import concourse.bass as bass
import concourse.mybir as mybir
from concourse.bass_utils import run_bass_kernel_spmd

F32 = mybir.dt.float32
BF16 = mybir.dt.bfloat16
AF = mybir.ActivationFunctionType
ALU = mybir.AluOpType

D = 1024
S = 4096
NCORES = 8
DIN = 9216
EPS = 1e-6
TT = 512
NT = S // TT
NJ = 8
NPRM = 41
N_PE_TAPS = 19
R_BMA, R_BMB, R_CAW, R_CBW, R_CBB, R_LNG, R_LNB, R_BOB, R_C = 0, 1, 2, 5, 36, 37, 38, 39, 40
BLK_BA, BLK_CA, BLK_VA, BLK_ZA, BLK_AB, BLK_GB, BLK_ZB, BLK_MA, BLK_MB = range(9)


BLOB_SHAPES = [
    ("x", (S, D)), ("c", (D,)), ("norm_gain", (D,)), ("w_ada", (D, 3 * D)), ("b_ada", (3 * D,)),
    ("w_in", (D, DIN)), ("b_merge", (2 * D,)), ("conv_a_w", (3, D)), ("w_out_a", (D, D)),
    ("conv_b_w", (31, D)), ("conv_b_bias", (D,)), ("ln_b_gain", (D,)), ("ln_b_bias", (D,)),
    ("w_out_b", (D, D)), ("b_out_b", (D,)), ("w_o", (D, D)), ("final_gain", (D,)),
]
BLOB_OFF = {}
_o = 0
for _n, _s in BLOB_SHAPES:
    BLOB_OFF[_n] = (_o, _s)
    _o += int(np.prod(_s))
BLOB_N = _o


class Buf:
    __slots__ = ("w", "r")

    def __init__(self):
        self.w = None
        self.r = []


class Slot:
    __slots__ = ("sem", "count")

    def __init__(self, sem):
        self.sem = sem
        self.count = 0


class Sched:
    ENG = ("pe", "act", "dve", "pool", "sp")

    def __init__(self, nc, stack):
        self.nc = nc
        self.stack = stack
        self.ops = {e: [] for e in self.ENG}
        self.tl = {e: stack.enter_context(nc.semaphore("tl_" + e)) for e in ("pe", "act", "dve", "pool")}
        self.cnt = {e: 0 for e in self.tl}
        self.waited = {e: {} for e in self.ENG}
        self.slots = []
        self.nsem = 4

    def slot(self, name):
        s = Slot(self.stack.enter_context(self.nc.semaphore(name)))
        self.slots.append(s)
        self.nsem += 1
        return s

    def _waits(self, engine, deps):
        best = {}
        for tok in deps:
            if tok is None:
                continue
            sem, val, prod = tok
            if prod == engine and (engine == "pe" or val < self.cnt[engine]):
                continue
            key = sem.num
            if key not in best or best[key][1] < val:
                best[key] = (sem, val)
        for key, (sem, val) in best.items():
            if self.waited[engine].get(key, 0) >= val:
                continue
            self.waited[engine][key] = val
            self.ops[engine].append(lambda eng, sem=sem, val=val: eng.wait_ge(sem, val))

    def _collect(self, reads, writes, deps):
        out = list(deps)
        for b in reads:
            out.append(b.w)
        for b in writes:
            out.extend(b.r)
            out.append(b.w)
        return out

    def _update(self, tok, reads, writes):
        for b in reads:
            b.r.append(tok)
        for b in writes:
            b.w = tok
            b.r = []

    def op(self, engine, fn, reads=(), writes=(), deps=()):
        self._waits(engine, self._collect(reads, writes, deps))
        self.cnt[engine] += 1
        sem = self.tl[engine]
        tok = (sem, self.cnt[engine], engine)
        self.ops[engine].append(lambda eng, fn=fn, sem=sem: fn(eng).then_inc(sem, 1))
        self._update(tok, reads, writes)
        return tok

    def dma(self, queue, slot, out, in_, reads=(), writes=(), deps=(), noncontig=False):
        self._waits(queue, self._collect(reads, writes, deps))
        slot.count += 16
        tok = (slot.sem, slot.count, None)
        nc = self.nc

        def fn(eng, out=out, in_=in_, sem=slot.sem):
            if noncontig:
                with nc.allow_non_contiguous_dma(reason="small one-time parameter load"):
                    eng.dma_start(out=out, in_=in_).then_inc(sem, 16)
            else:
                eng.dma_start(out=out, in_=in_).then_inc(sem, 16)

        self.ops[queue].append(fn)
        self._update(tok, reads, writes)
        return tok

    def barrier(self, scratch_ap):
        toks = [(self.tl[e], self.cnt[e], e) for e in self.tl if self.cnt[e] > 0]
        toks += [(s.sem, s.count, None) for s in self.slots if s.count > 0]
        t = self.op("act", lambda a: a.activation(out=scratch_ap, in_=scratch_ap, func=AF.Identity), deps=toks)
        for e in self.ENG:
            if e != "act":
                self._waits(e, [t])
        return t

    def finish(self):
        toks = [(s.sem, s.count, None) for s in self.slots if s.count > 0]
        self._waits("sp", toks)


def build_nc():
    nc = bass.Bass("TRN2", target_bir_lowering=False)

    blob = nc.dram_tensor("blob", [BLOB_N], F32, kind="ExternalInput").ap()

    def view(name):
        off, shape = BLOB_OFF[name]
        n = int(np.prod(shape))
        v = blob[off:off + n]
        if len(shape) == 2:
            v = v.rearrange("(a b) -> a b", b=shape[1])
        return v

    x = view("x")
    c = view("c")
    norm_gain = view("norm_gain")
    w_ada = view("w_ada")
    b_ada = view("b_ada")
    w_in = view("w_in")
    b_merge = view("b_merge")
    conv_a_w = view("conv_a_w")
    w_out_a = view("w_out_a")
    conv_b_w = view("conv_b_w")
    conv_b_bias = view("conv_b_bias")
    ln_b_gain = view("ln_b_gain")
    ln_b_bias = view("ln_b_bias")
    w_out_b = view("w_out_b")
    b_out_b = view("b_out_b")
    w_o = view("w_o")
    final_gain = view("final_gain")
    y = nc.dram_tensor("y", [S, D], F32, kind="ExternalOutput").ap()

    sp_ya = nc.dram_tensor("sp_ya", [NT, 128, NJ, TT], BF16).ap()
    sp_uc = nc.dram_tensor("sp_uc", [NT, 128, NJ, TT], BF16).ap()
    sp_gz = nc.dram_tensor("sp_gz", [NT, 128, NJ, TT], BF16).ap()
    sp_sa = nc.dram_tensor("sp_sa", [NT, 128, NJ, TT], BF16).ap()
    sp_sb = nc.dram_tensor("sp_sb", [NT, 128, NJ, TT], BF16).ap()

    with ExitStack() as G:
        S_ = Sched(nc, G)

        def sb(stack, name, shape, dt):
            return stack.enter_context(nc.sbuf_tensor(name, shape, dt))

        ident_bf = sb(G, "ident_bf", [128, 128], BF16)
        ident_f = sb(G, "ident_f", [128, 128], F32)
        ones_bf = sb(G, "ones_bf", [128, 128], BF16)
        cols = sb(G, "cols", [128, NJ * NPRM], F32)
        gate_bc = sb(G, "gate_bc", [128, D], F32)
        ss = sb(G, "ss", [128, 32], F32)
        rstd = sb(G, "rstd", [128, 32], F32)
        ss2 = sb(G, "ss2", [128, 32], F32)
        rstd2 = sb(G, "rstd2", [128, 32], F32)
        bar = sb(G, "bar", [128, 2], F32)
        epsc = sb(G, "epsc", [128, 1], F32)
        banks = [nc.alloc_psum_tensor("bank%d" % i, [128, 512], F32) for i in range(8)]
        bankB = [Buf() for _ in range(8)]

        def col(j, r):
            return cols[:, j * NPRM + r: j * NPRM + r + 1]

        b_ident_bf, b_ident_f, b_ones, b_cols, b_gate = Buf(), Buf(), Buf(), Buf(), Buf()
        b_ss, b_ss2 = Buf(), Buf()

        def mk_ident(t):
            def fn(g):
                g.memset(t[:, :], 1.0)
                return g.affine_select(out=t[:, :], in_=t[:, :], pattern=[[-1, 128]], compare_op=ALU.is_equal,
                                       fill=0.0, base=0, channel_multiplier=1)
            return fn

        S_.op("pool", mk_ident(ident_bf), writes=[b_ident_bf])
        S_.op("pool", mk_ident(ident_f), writes=[b_ident_f])
        S_.op("pool", lambda g: g.memset(ones_bf[:, :], 1.0 / 1024.0), writes=[b_ones])
        S_.op("pool", lambda g: g.memset(ss[:, :], 0.0), writes=[b_ss])
        S_.op("pool", lambda g: g.memset(ss2[:, :], 0.0), writes=[b_ss2])
        S_.op("pool", lambda g: g.memset(bar[:, :], 0.0))
        b_eps = Buf()
        S_.op("pool", lambda g: g.memset(epsc[:, :], EPS), writes=[b_eps])

        A = ExitStack()
        hT = sb(A, "hT", [128, NJ, S], BF16)
        wj = [sb(A, "wj%d" % i, [128, 9, NJ, 128], BF16) for i in range(2)]
        b_wj = [Buf(), Buf()]
        sl_wj = [S_.slot("sl_wj%d" % i) for i in range(2)]
        b_hT = [Buf() for _ in range(32)]

        def load_wj(j):
            p = j % 2
            src = w_in.rearrange("(kc p) (blk j m) -> j p blk kc m", p=128, blk=9, j=NJ, m=128)
            t = None
            for blk in range(9):
                t = S_.dma("pool", sl_wj[p], wj[p][:, blk, :, :], src[j, :, blk, :, :],
                           writes=[b_wj[p]] if blk == 0 else [], deps=[] if blk == 0 else [])
            b_wj[p].w = t
            return t

        A0 = ExitStack()
        prm = sb(A0, "prm", [NPRM, D], F32)
        onesF = sb(A0, "onesF", [128, 128], F32)
        lhsT_bc = sb(A0, "lhsT_bc", [128, NJ, 128], F32)
        c_act = sb(A0, "c_act", [128, NJ], F32)
        wada = [sb(A0, "wada%d" % i, [128, 1536], F32) for i in range(2)]
        mod_sb = sb(A0, "mod_sb", [128, 3 * D], F32)
        ng_bc = sb(A0, "ng_bc", [128, D], F32)
        gprime = sb(A0, "gprime", [128, D], F32)
        xt = [sb(A0, "xt%d" % i, [128, D], F32) for i in range(4)]
        t1 = [sb(A0, "t1_%d" % i, [128, D], F32) for i in range(2)]
        hb = [sb(A0, "hb%d" % i, [128, D], BF16) for i in range(2)]
        junk = sb(A0, "junk", [128, D], F32)

        sl_prm = S_.slot("sl_prm")
        b_prm = Buf()
        rows = [
            (R_BMA, 2, b_merge.rearrange("(r n) -> r n", r=2)),
            (R_CAW, 3, conv_a_w),
            (R_CBW, 31, conv_b_w),
            (R_CBB, 1, conv_b_bias.rearrange("(r n) -> r n", r=1)),
            (R_LNG, 1, ln_b_gain.rearrange("(r n) -> r n", r=1)),
            (R_LNB, 1, ln_b_bias.rearrange("(r n) -> r n", r=1)),
            (R_BOB, 1, b_out_b.rearrange("(r n) -> r n", r=1)),
            (R_C, 1, c.rearrange("(r n) -> r n", r=1)),
        ]
        t = None
        for (r0, n, src) in rows:
            t = S_.dma("sp", sl_prm, prm[r0:r0 + n, :], src)
        b_prm.w = t

        def tr_prm(pe):
            ins = None
            for kc in range(NJ):
                ins = pe.transpose(banks[0][:, kc * NPRM:(kc + 1) * NPRM], prm[0:NPRM, kc * 128:(kc + 1) * 128],
                                   ident_f[0:NPRM, 0:NPRM])
            return ins

        S_.op("pe", tr_prm, reads=[b_prm, b_ident_f], writes=[bankB[0]])
        S_.op("dve", lambda v: v.tensor_copy(out=cols[:, :], in_=banks[0][:, 0:NJ * NPRM]),
              reads=[bankB[0]], writes=[b_cols])

        b_cact, b_lhs, b_mod, b_ng, b_gp = Buf(), Buf(), Buf(), Buf(), Buf()
        cols3 = cols[:, :].rearrange("p (k r) -> p k r", r=NPRM)
        S_.op("act", lambda a: a.activation(out=c_act[:, :], in_=cols3[:, :, R_C], func=AF.Silu),
              reads=[b_cols], writes=[b_cact])
        S_.op("pool", lambda g: g.memset(onesF[:, :], 1.0))
        b_onesF = Buf()
        b_onesF.w = (S_.tl["pool"], S_.cnt["pool"], "pool")

        def mk_lhs(v):
            ins = None
            for kc in range(NJ):
                ins = v.tensor_scalar(out=lhsT_bc[:, kc, :], in0=onesF[:, :], scalar1=c_act[:, kc:kc + 1],
                                      scalar2=None, op0=ALU.mult)
            return ins

        S_.op("dve", mk_lhs, reads=[b_cact, b_onesF], writes=[b_lhs])
        sl_misc = S_.slot("sl_misc")
        S_.dma("sp", sl_misc, mod_sb[:, :], b_ada.partition_broadcast(128), writes=[b_mod])
        S_.dma("sp", sl_misc, ng_bc[:, :], norm_gain.partition_broadcast(128), writes=[b_ng])
        b_mod.w = (sl_misc.sem, sl_misc.count, None)
        b_ng.w = (sl_misc.sem, sl_misc.count, None)

        sl_wada = [S_.slot("sl_wada%d" % i) for i in range(2)]
        b_wada = [Buf(), Buf()]
        q = 0
        for kc in range(NJ):
            for h in range(2):
                p = q % 2
                q += 1
                S_.dma("sp", sl_wada[p], wada[p][:, :], w_ada[kc * 128:(kc + 1) * 128, h * 1536:(h + 1) * 1536],
                       writes=[b_wada[p]])

                def mm(pe, kc=kc, h=h, p=p):
                    ins = None
                    for n in range(3):
                        ins = pe.matmul(banks[1 + h * 3 + n][:, :], lhsT=lhsT_bc[:, kc, :],
                                        rhs=wada[p][:, n * 512:(n + 1) * 512], start=(kc == 0), stop=(kc == NJ - 1))
                    return ins

                S_.op("pe", mm, reads=[b_wada[p], b_lhs], writes=[bankB[1 + h * 3 + n] for n in range(3)])
        for n in range(6):
            S_.op("dve", lambda v, n=n: v.tensor_tensor(out=mod_sb[:, n * 512:(n + 1) * 512],
                                                        in0=banks[1 + n][:, :], in1=mod_sb[:, n * 512:(n + 1) * 512],
                                                        op=ALU.add),
                  reads=[bankB[1 + n]], writes=[b_mod])
        S_.op("dve", lambda v: v.scalar_tensor_tensor(out=gprime[:, :], in0=mod_sb[:, D:2 * D], scalar=1.0,
                                                      in1=ng_bc[:, :], op0=ALU.add, op1=ALU.mult),
              reads=[b_mod, b_ng], writes=[b_gp])
        S_.op("dve", lambda v: v.tensor_copy(out=gate_bc[:, :], in_=mod_sb[:, 2 * D:3 * D]),
              reads=[b_mod], writes=[b_gate])

        load_wj(0)

        NXT = 4
        sl_xt = [S_.slot("sl_xt%d" % i) for i in range(NXT)]
        b_xt = [Buf() for _ in range(NXT)]
        b_t1 = [Buf(), Buf()]
        b_hb = [Buf(), Buf()]
        b_rstd = [Buf() for _ in range(32)]

        def p0_load(tc):
            s4 = tc % NXT
            S_.dma("sp", sl_xt[s4], xt[s4][:, :], x[tc * 128:(tc + 1) * 128, :], writes=[b_xt[s4]])

        def p0_stats_act(tc):
            s4 = tc % NXT
            tss = S_.op("act", lambda a, s4=s4, tc=tc: a.activation(out=junk[:, :], in_=xt[s4][:, :], func=AF.Square,
                                                                    accum_out=ss[:, tc:tc + 1]),
                        reads=[b_xt[s4], b_ss], writes=[])
            return S_.op("act", lambda a, tc=tc: a.activation(out=rstd[:, tc:tc + 1], in_=ss[:, tc:tc + 1],
                                                              func=AF.Sqrt, scale=1.0 / D, bias=epsc[:, 0:1]),
                         reads=[b_eps], deps=[tss])

        def p0_recip(tc, tsq):
            S_.op("dve", lambda v, tc=tc: v.reciprocal(out=rstd[:, tc:tc + 1], in_=rstd[:, tc:tc + 1]),
                  deps=[tsq], writes=[b_rstd[tc]])

        def p0_main(tc):
            s4 = tc % NXT
            s2 = tc % 2
            S_.op("dve", lambda v, s4=s4, s2=s2, tc=tc: v.scalar_tensor_tensor(
                out=t1[s2][:, :], in0=xt[s4][:, :], scalar=rstd[:, tc:tc + 1], in1=gprime[:, :],
                op0=ALU.mult, op1=ALU.mult),
                reads=[b_xt[s4], b_rstd[tc], b_gp], writes=[b_t1[s2]])
            S_.op("dve", lambda g, s2=s2: g.tensor_tensor(out=hb[s2][:, :], in0=t1[s2][:, :], in1=mod_sb[:, 0:D],
                                                          op=ALU.add),
                  reads=[b_t1[s2], b_mod], writes=[b_hb[s2]])
            bk = 6 + s2
            bbf = banks[bk][:, :].bitcast(BF16)

            def trs(pe, s2=s2, bbf=bbf):
                ins = None
                for kc in range(NJ):
                    ins = pe.transpose(bbf[:, kc * 128:(kc + 1) * 128], hb[s2][:, kc * 128:(kc + 1) * 128],
                                       ident_bf[:, :])
                return ins

            S_.op("pe", trs, reads=[b_hb[s2], b_ident_bf], writes=[bankB[bk]])

        def p0_evac(tc):
            bk = 6 + tc % 2
            bbf = banks[bk][:, :].bitcast(BF16)
            S_.op("act", lambda a, tc=tc, bbf=bbf: a.activation(
                out=hT[:, :, tc * 128:(tc + 1) * 128], in_=bbf.rearrange("p (k t) -> p k t", k=NJ),
                func=AF.Identity),
                reads=[bankB[bk]], writes=[b_hT[tc]])

        p0_load(0)
        p0_load(1)
        tsqs = {0: p0_stats_act(0)}
        p0_recip(0, tsqs[0])
        for tc in range(33):
            if tc + 2 < 32:
                p0_load(tc + 2)
            if tc + 1 < 32:
                tsqs[tc + 1] = p0_stats_act(tc + 1)
            if tc < 32:
                p0_main(tc)
            if tc + 1 < 32:
                p0_recip(tc + 1, tsqs[tc + 1])
            if tc >= 1:
                p0_evac(tc - 1)

        S_.barrier(bar[:, 0:1])
        A0.close()

        A1 = ExitStack()
        diag = [sb(A1, "diag%d" % i, [128, 31, 128], BF16) for i in range(2)]
        cvb = sb(A1, "cvb", [128, S + 2], F32)
        ub = sb(A1, "ub", [128, S + 30], BF16)
        sz_sb = [sb(A1, "sz%d" % i, [128, TT], F32) for i in range(2)]
        ca_sb = [sb(A1, "ca%d" % i, [128, TT], F32) for i in range(2)]
        sg_sb = [sb(A1, "sg%d" % i, [128, TT], F32) for i in range(2)]
        ga = [sb(A1, "ga%d" % i, [128, TT], F32) for i in range(3)]
        acc = [sb(A1, "acc%d" % i, [128, TT], F32) for i in range(2)]
        cacc = [sb(A1, "cacc%d" % i, [128, TT], F32) for i in range(2)]
        b_cacc = [Buf(), Buf()]
        st_names = ("gz", "sa", "sb", "ya", "uc")
        stg = {n: [sb(A1, "st_%s%d" % (n, i), [128, TT], BF16) for i in range(2)] for n in st_names}
        b_stg = {n: [Buf(), Buf()] for n in st_names}
        sl_stg = {n: [S_.slot("sl_%s%d" % (n, i)) for i in range(2)] for n in st_names}
        sp_of = {"gz": sp_gz, "sa": sp_sa, "sb": sp_sb, "ya": sp_ya, "uc": sp_uc}
        b_sp = {n: [Buf() for _ in range(NT)] for n in st_names}
        b_diag = [Buf(), Buf()]
        b_cv = [Buf() for _ in range(NT)]
        b_u = [Buf() for _ in range(NT)]
        b_sz, b_ca, b_sg = [Buf(), Buf()], [Buf(), Buf()], [Buf(), Buf()]
        b_ga = [Buf(), Buf(), Buf()]
        b_acc = [Buf(), Buf()]

        S_.op("pool", lambda g: g.memset(cvb[:, 0:1], 0.0))
        S_.op("pool", lambda g: g.memset(cvb[:, S + 1:S + 2], 0.0))
        S_.op("pool", lambda g: g.memset(ub[:, 0:15], 0.0))
        S_.op("pool", lambda g: g.memset(ub[:, S + 15:S + 30], 0.0))
        tpad = (S_.tl["pool"], S_.cnt["pool"], "pool")

        ring = [0]
        convring = [0]

        def next_bank():
            b = ring[0] % 6
            ring[0] += 1
            return b

        def proj(j, i, blk):
            p = j % 2
            bk = next_bank()

            def fn(pe, p=p, blk=blk, i=i, bk=bk):
                ins = None
                for kc in range(NJ):
                    ins = pe.matmul(banks[bk][:, :], lhsT=wj[p][:, blk, kc, :], rhs=hT[:, kc, i * TT:(i + 1) * TT],
                                    start=(kc == 0), stop=(kc == NJ - 1))
                return ins

            S_.op("pe", fn, reads=[b_wj[p]] + b_hT[i * 4:(i + 1) * 4], writes=[bankB[bk]])
            return bk

        def store(name, slot_i, il, j):
            S_.dma("sp", sl_stg[name][slot_i], sp_of[name][il, :, j, :], stg[name][slot_i][:, :],
                   reads=[b_stg[name][slot_i]], writes=[b_sp[name][il]])

        stc = [0]

        def lagged_parts(j, il, n):
            p = j % 2
            c0 = il * TT
            s = stc[0] % 2
            stc[0] += 1
            cb = 6 + (convring[0] % 2)
            convring[0] += 1
            ureads = [b_u[t] for t in (il - 1, il, il + 1) if 0 <= t < NT]
            cvreads = [b_cv[t] for t in (il - 1, il, il + 1) if 0 <= t < NT]
            gslot = n % 3

            def part_pe():
                def cfn(pe):
                    ins = None
                    for k in range(N_PE_TAPS):
                        ins = pe.matmul(banks[cb][:, :], lhsT=diag[p][:, k, :], rhs=ub[:, c0 + k:c0 + k + TT],
                                        start=(k == 0), stop=(k == N_PE_TAPS - 1))
                    return ins

                S_.op("pe", cfn, reads=[b_diag[p]] + ureads, writes=[bankB[cb]], deps=[tpad])
                S_.op("act", lambda a: a.activation(out=cacc[s][:, :], in_=banks[cb][:, :],
                                                    func=AF.Identity, bias=col(j, R_CBB)),
                      reads=[bankB[cb], b_cols], writes=[b_cacc[s]])

            def taps(k0, k1):
                def emit():
                    for k in range(k0, k1):
                        last = (k == 30)
                        dst = stg["uc"][s] if last else cacc[s]
                        S_.op("dve", lambda v, k=k, dst=dst: v.scalar_tensor_tensor(
                            out=dst[:, :], in0=ub[:, c0 + k:c0 + k + TT], scalar=col(j, R_CBW + k),
                            in1=cacc[s][:, :], op0=ALU.mult, op1=ALU.add),
                            reads=ureads + [b_cacc[s], b_cols],
                            writes=[b_stg["uc"][s]] if last else [b_cacc[s]])
                    if k1 == 31:
                        store("uc", s, il, j)
                return emit

            def part_a():
                S_.op("dve", lambda g: g.tensor_scalar(
                    out=acc[s][:, :], in0=cvb[:, c0:c0 + TT], scalar1=col(j, R_CAW), scalar2=None, op0=ALU.mult),
                    reads=cvreads + [b_cols], writes=[b_acc[s]], deps=[tpad])
                for k in (1, 2):
                    S_.op("dve", lambda g, k=k: g.scalar_tensor_tensor(
                        out=acc[s][:, :], in0=cvb[:, c0 + k:c0 + k + TT], scalar=col(j, R_CAW + k),
                        in1=acc[s][:, :], op0=ALU.mult, op1=ALU.add),
                        reads=cvreads, writes=[b_acc[s]])
                S_.op("dve", lambda g: g.tensor_tensor(
                    out=stg["ya"][s][:, :], in0=acc[s][:, :], in1=ga[gslot][:, :], op=ALU.mult),
                    reads=[b_acc[s], b_ga[gslot]], writes=[b_stg["ya"][s]])
                store("ya", s, il, j)

            nd = 31 - N_PE_TAPS
            c1 = N_PE_TAPS + nd // 3
            c2 = N_PE_TAPS + (2 * nd) // 3
            return [part_pe, taps(N_PE_TAPS, c1), taps(c1, c2), taps(c2, 31), part_a]

        def build_diag(j):
            p = j % 2
            S_.op("dve", lambda v, p=p, j=j: v.tensor_tensor(
                out=diag[p][:, :, :],
                in0=ident_bf[:, :].unsqueeze(1).to_broadcast([128, 31, 128]),
                in1=cols[:, j * NPRM + R_CBW:j * NPRM + R_CBW + 31].unsqueeze(2).to_broadcast([128, 31, 128]),
                op=ALU.mult),
                reads=[b_ident_bf, b_cols], writes=[b_diag[p]])

        build_diag(0)
        NIT = NJ * NT
        for n in range(NIT):
            j, i = divmod(n, NT)
            p = j % 2
            if i == 0 and j + 1 < NJ:
                load_wj(j + 1)
            if i == 2 and j + 1 < NJ:
                build_diag(j + 1)
            c0 = i * TT
            s = n % 2
            parts = None
            if n >= 2:
                jl, il = divmod(n - 2, NT)
                parts = lagged_parts(jl, il, n - 2)
                parts[0]()
            bk_za = proj(j, i, BLK_ZA)
            bk_zb = proj(j, i, BLK_ZB)
            bk_ba = proj(j, i, BLK_BA)
            bk_gb = proj(j, i, BLK_GB)
            bk_ma = proj(j, i, BLK_MA)
            bk_mb = proj(j, i, BLK_MB)
            S_.op("act", lambda a, s=s, bk=bk_za: a.activation(out=sz_sb[s][:, :], in_=banks[bk][:, :],
                                                               func=AF.Silu),
                  reads=[bankB[bk_za]], writes=[b_sz[s]])
            S_.op("act", lambda a, s=s, bk=bk_zb: a.activation(out=stg["gz"][s][:, :], in_=banks[bk][:, :],
                                                               func=AF.Silu),
                  reads=[bankB[bk_zb]], writes=[b_stg["gz"][s]])
            store("gz", s, i, j)
            S_.op("dve", lambda v, s=s, bk=bk_ba, n=n: v.tensor_tensor(
                out=ga[n % 3][:, :], in0=banks[bk][:, :], in1=sz_sb[s][:, :], op=ALU.mult),
                reads=[bankB[bk_ba], b_sz[s]], writes=[b_ga[n % 3]])
            if parts:
                parts[1]()
            S_.op("act", lambda a, s=s, bk=bk_gb: a.activation(out=sg_sb[s][:, :], in_=banks[bk][:, :],
                                                               func=AF.Sigmoid),
                  reads=[bankB[bk_gb]], writes=[b_sg[s]])
            S_.op("act", lambda a, s=s, bk=bk_ma, j=j: a.activation(
                out=stg["sa"][s][:, :], in_=banks[bk][:, :], func=AF.Sigmoid, bias=col(j, R_BMA)),
                reads=[bankB[bk_ma], b_cols], writes=[b_stg["sa"][s]])
            store("sa", s, i, j)
            S_.op("act", lambda a, s=s, bk=bk_mb, j=j: a.activation(
                out=stg["sb"][s][:, :], in_=banks[bk][:, :], func=AF.Sigmoid, bias=col(j, R_BMB)),
                reads=[bankB[bk_mb], b_cols], writes=[b_stg["sb"][s]])
            store("sb", s, i, j)
            bk_ab = proj(j, i, BLK_AB)
            bk_ca = proj(j, i, BLK_CA)
            bk_va = proj(j, i, BLK_VA)
            S_.op("dve", lambda v, s=s, bk=bk_ab, c0=c0: v.tensor_tensor(
                out=ub[:, 15 + c0:15 + c0 + TT], in0=banks[bk][:, :], in1=sg_sb[s][:, :], op=ALU.mult),
                reads=[bankB[bk_ab], b_sg[s]], writes=[b_u[i]])
            if parts:
                parts[2]()
            S_.op("act", lambda a, s=s, bk=bk_ca: a.activation(out=ca_sb[s][:, :], in_=banks[bk][:, :],
                                                               func=AF.Identity),
                  reads=[bankB[bk_ca]], writes=[b_ca[s]])
            S_.op("dve", lambda v, s=s, bk=bk_va, c0=c0: v.tensor_tensor(
                out=cvb[:, 1 + c0:1 + c0 + TT], in0=banks[bk][:, :], in1=ca_sb[s][:, :], op=ALU.mult),
                reads=[bankB[bk_va], b_ca[s]], writes=[b_cv[i]])
            if parts:
                parts[3]()
                parts[4]()
        for n in (NIT - 2, NIT - 1):
            jl, il = divmod(n, NT)
            for part in lagged_parts(jl, il, n):
                part()

        S_.barrier(bar[:, 0:1])
        A1.close()
        A.close()

        B = ExitStack()
        woa_bf = sb(B, "woa_bf", [128, NJ, D], BF16)
        wob_bf = sb(B, "wob_bf", [128, NJ, D], BF16)
        wo_bf = sb(B, "wo_bf", [128, NJ, D], BF16)
        fg_bc = sb(B, "fg_bc", [128, D], F32)
        ld_names = ("uc", "ya", "gz", "sa", "sb")
        nring = {"uc": 2, "ya": 2, "gz": 1, "sa": 1, "sb": 1}
        CHUNKED = ("gz", "sa", "sb")
        ldt = {n: [sb(B, "ld_%s%d" % (n, i), [128, NJ, TT], BF16) for i in range(nring[n])] for n in ld_names}
        b_ldt = {n: [Buf() for _ in range(nring[n])] for n in ("uc", "ya")}
        sl_ldt = {n: [S_.slot("sl_ld_%s%d" % (n, i)) for i in range(nring[n])] for n in ("uc", "ya")}
        b_ldc = {n: [Buf() for _ in range(NJ)] for n in CHUNKED}
        sl_ldc = {n: [S_.slot("sl_ldc_%s%d" % (n, k)) for k in range(NJ)] for n in CHUNKED}
        sq = [sb(B, "sq%d" % i, [128, TT], BF16) for i in range(2)]
        mean_sb = [sb(B, "mean_sb%d" % i, [128, TT], F32) for i in range(2)]
        m2 = sb(B, "m2", [128, TT], F32)
        rstd_t = [sb(B, "rstd_t%d" % i, [128, TT], F32) for i in range(2)]
        tA = [sb(B, "tA%d" % i, [128, TT], F32) for i in range(2)]
        tB = [sb(B, "tB%d" % i, [128, TT], F32) for i in range(2)]
        slu = [sb(B, "slu%d" % i, [128, TT], F32) for i in range(2)]
        ubp = [sb(B, "ubp%d" % i, [128, NJ, TT], BF16) for i in range(2)]
        u1 = [sb(B, "u1_%d" % i, [128, TT], F32) for i in range(2)]
        u2 = [sb(B, "u2_%d" % i, [128, TT], F32) for i in range(2)]
        mg = sb(B, "mg", [128, NJ, TT], BF16)
        NX2 = 3
        xt2 = [sb(B, "xt2_%d" % i, [128, D], F32) for i in range(NX2)]
        xn = [sb(B, "xn%d" % i, [128, D], F32) for i in range(2)]
        ot = [sb(B, "ot%d" % i, [128, D], F32) for i in range(2)]
        junk2 = sb(B, "junk2", [128, D], BF16)

        b_woa, b_wob, b_wo, b_fg = Buf(), Buf(), Buf(), Buf()
        sl_wos = [S_.slot("sl_wos%d" % i) for i in range(2)]
        sl_w2 = S_.slot("sl_w2")
        b_sq = [Buf(), Buf()]
        b_mean, b_rstdt = [Buf(), Buf()], [Buf(), Buf()]
        b_m2 = Buf()
        b_tA, b_tB, b_slu = [Buf(), Buf()], [Buf(), Buf()], [Buf(), Buf()]
        b_ubp = [[Buf() for _ in range(NJ)] for _ in range(2)]
        b_u1, b_u2 = [Buf(), Buf()], [Buf(), Buf()]
        b_mg = [Buf() for _ in range(NJ)]
        b_xt2 = [Buf() for _ in range(NX2)]
        sl_xt2 = [S_.slot("sl_xt2_%d" % i) for i in range(NX2)]
        b_xn, b_ot = [Buf(), Buf()], [Buf(), Buf()]
        sl_ot = [S_.slot("sl_ot%d" % i) for i in range(2)]
        b_rstd2 = [Buf() for _ in range(32)]

        def load_tile(i, names):
            for n in names:
                if n in CHUNKED:
                    for k in range(NJ):
                        load_chunk(n, i, k)
                    continue
                r = i % nring[n]
                S_.dma("sp", sl_ldt[n][r], ldt[n][r][:, :, :], sp_of[n][i, :, :, :],
                       reads=[b_sp[n][i]], writes=[b_ldt[n][r]])

        def load_chunk(n, i, k):
            S_.dma("sp", sl_ldc[n][k], ldt[n][0][:, k, :], sp_of[n][i, :, k, :],
                   reads=[b_sp[n][i]], writes=[b_ldc[n][k]])

        def T(n, i):
            return ldt[n][i % nring[n]]

        def BT(n, i):
            return b_ldt[n][i % nring[n]]

        REST = ("gz", "ya", "sa", "sb")
        load_tile(0, ("uc",))
        load_tile(1, ("uc",))
        load_tile(0, REST)
        S_.dma("sp", sl_w2, fg_bc[:, :], final_gain.partition_broadcast(128), writes=[b_fg])
        S_.dma("pool", sl_w2, woa_bf[:, :, :], w_out_a.rearrange("(kc p) n -> p kc n", p=128), writes=[b_woa])
        S_.dma("pool", sl_w2, wob_bf[:, :, :], w_out_b.rearrange("(kc p) n -> p kc n", p=128), writes=[b_wob])
        tw2 = (sl_w2.sem, sl_w2.count, None)
        b_fg.w = tw2
        b_woa.w = tw2
        b_wob.w = tw2
        wo_stage, b_wos = xn, b_xn
        for kc in range(NJ):
            p = kc % 2
            S_.dma("sp", sl_wos[p], wo_stage[p][:, :], w_o[kc * 128:(kc + 1) * 128, :], writes=[b_wos[p]])
            S_.op("dve", lambda v, kc=kc, p=p: v.tensor_tensor(out=wo_bf[:, kc, :], in0=wo_stage[p][:, :],
                                                               in1=gate_bc[:, :], op=ALU.mult),
                  reads=[b_wos[p], b_gate], writes=[b_wo])

        BK_MEAN, BK_MSQ = 0, 1
        yring = [0]
        oring = [0]

        def s1_sq(i, k):
            s = k % 2
            S_.op("act", lambda a, s=s, k=k, t=T("uc", i): a.activation(out=sq[s][:, :], in_=t[:, k, :],
                                                                        func=AF.Square),
                  reads=[BT("uc", i)], writes=[b_sq[s]])

        def s1_mm(i, k):
            s = k % 2

            def stat(pe, s=s, k=k, t=T("uc", i)):
                pe.matmul(banks[BK_MEAN][:, :], lhsT=ones_bf[:, :], rhs=t[:, k, :], start=(k == 0),
                          stop=(k == NJ - 1))
                return pe.matmul(banks[BK_MSQ][:, :], lhsT=ones_bf[:, :], rhs=sq[s][:, :], start=(k == 0),
                                 stop=(k == NJ - 1))

            S_.op("pe", stat, reads=[BT("uc", i), b_sq[s], b_ones], writes=[bankB[BK_MEAN], bankB[BK_MSQ]])

        def s1_fin(i):
            r = i % 2
            S_.op("act", lambda a, r=r: a.activation(out=mean_sb[r][:, :], in_=banks[BK_MEAN][:, :],
                                                     func=AF.Identity),
                  reads=[bankB[BK_MEAN]], writes=[b_mean[r]])
            S_.op("dve", lambda v, r=r: v.tensor_tensor(out=m2[:, :], in0=mean_sb[r][:, :], in1=mean_sb[r][:, :],
                                                        op=ALU.mult),
                  reads=[b_mean[r]], writes=[b_m2])
            S_.op("dve", lambda v: v.tensor_tensor(out=m2[:, :], in0=banks[BK_MSQ][:, :], in1=m2[:, :],
                                                   op=ALU.subtract),
                  reads=[bankB[BK_MSQ], b_m2], writes=[b_m2])
            S_.op("dve", lambda v: v.tensor_scalar(out=m2[:, :], in0=m2[:, :], scalar1=0.0, scalar2=None,
                                                   op0=ALU.max),
                  reads=[b_m2], writes=[b_m2])
            S_.op("act", lambda a, r=r: a.activation(out=rstd_t[r][:, :], in_=m2[:, :], func=AF.Sqrt,
                                                     bias=epsc[:, 0:1]),
                  reads=[b_m2, b_eps], writes=[b_rstdt[r]])
            S_.op("dve", lambda v, r=r: v.reciprocal(out=rstd_t[r][:, :], in_=rstd_t[r][:, :]),
                  reads=[b_rstdt[r]], writes=[b_rstdt[r]])

        def s2_a(i, k):
            s = k % 2
            r = i % 2
            S_.op("dve", lambda v, s=s, k=k, r=r, t=T("uc", i): v.tensor_tensor(
                out=tA[s][:, :], in0=t[:, k, :], in1=mean_sb[r][:, :], op=ALU.subtract),
                reads=[BT("uc", i), b_mean[r]], writes=[b_tA[s]])
            S_.op("dve", lambda g, s=s, r=r: g.tensor_tensor(out=tB[s][:, :], in0=tA[s][:, :], in1=rstd_t[r][:, :],
                                                             op=ALU.mult),
                  reads=[b_tA[s], b_rstdt[r]], writes=[b_tB[s]])
            S_.op("act", lambda a, s=s, k=k: a.activation(out=slu[s][:, :], in_=tB[s][:, :], func=AF.Silu,
                                                          scale=col(k, R_LNG), bias=col(k, R_LNB)),
                  reads=[b_tB[s], b_cols], writes=[b_slu[s]])

        def s2_b(i, k):
            s = k % 2
            r = i % 2
            S_.op("dve", lambda v, s=s, k=k, r=r, t=T("gz", i): v.tensor_tensor(
                out=ubp[r][:, k, :], in0=slu[s][:, :], in1=t[:, k, :], op=ALU.mult),
                reads=[b_slu[s], b_ldc["gz"][k]], writes=[b_ubp[r][k]])

        def s3(i, dj):
            s = dj % 2
            r = i % 2
            bka = 2 + (yring[0] % 4)
            yring[0] += 1
            bkb = 2 + (yring[0] % 4)
            yring[0] += 1

            def mma(pe, dj=dj, bka=bka, t=T("ya", i)):
                ins = None
                for kc in range(NJ):
                    ins = pe.matmul(banks[bka][:, :], lhsT=woa_bf[:, kc, dj * 128:(dj + 1) * 128],
                                    rhs=t[:, kc, :], start=(kc == 0), stop=(kc == NJ - 1))
                return ins

            def mmb(pe, dj=dj, bkb=bkb, r=r):
                ins = None
                for kc in range(NJ):
                    ins = pe.matmul(banks[bkb][:, :], lhsT=wob_bf[:, kc, dj * 128:(dj + 1) * 128],
                                    rhs=ubp[r][:, kc, :], start=(kc == 0), stop=(kc == NJ - 1))
                return ins

            S_.op("pe", mma, reads=[b_woa, BT("ya", i)], writes=[bankB[bka]])
            S_.op("pe", mmb, reads=[b_wob] + b_ubp[r], writes=[bankB[bkb]])
            S_.op("dve", lambda v, s=s, dj=dj, bka=bka, t=T("sa", i): v.tensor_tensor(
                out=u1[s][:, :], in0=banks[bka][:, :], in1=t[:, dj, :], op=ALU.mult),
                reads=[bankB[bka], b_ldc["sa"][dj]], writes=[b_u1[s]])
            S_.op("dve", lambda v, s=s, dj=dj, bkb=bkb, t=T("sb", i): v.scalar_tensor_tensor(
                out=u2[s][:, :], in0=banks[bkb][:, :], scalar=col(dj, R_BOB), in1=t[:, dj, :],
                op0=ALU.add, op1=ALU.mult),
                reads=[bankB[bkb], b_ldc["sb"][dj], b_cols], writes=[b_u2[s]])
            S_.op("dve", lambda g, s=s, dj=dj: g.tensor_tensor(out=mg[:, dj, :], in0=u1[s][:, :],
                                                               in1=u2[s][:, :], op=ALU.add),
                  reads=[b_u1[s], b_u2[s]], writes=[b_mg[dj]])

        def s4_load(i, tq):
            tcg = i * 4 + tq
            s4 = tcg % NX2
            S_.dma("sp", sl_xt2[s4], xt2[s4][:, :], x[tcg * 128:(tcg + 1) * 128, :], writes=[b_xt2[s4]])

        def s4_a(i, tq):
            tcg = i * 4 + tq
            s4 = tcg % NX2
            s2 = tcg % 2
            for dh in range(2):
                bko = 6 + (oring[0] % 2)
                oring[0] += 1

                def mmo(pe, tq=tq, dh=dh, bko=bko):
                    ins = None
                    for kc in range(NJ):
                        ins = pe.matmul(banks[bko][:, :], lhsT=mg[:, kc, tq * 128:(tq + 1) * 128],
                                        rhs=wo_bf[:, kc, dh * 512:(dh + 1) * 512], start=(kc == 0),
                                        stop=(kc == NJ - 1))
                    return ins

                S_.op("pe", mmo, reads=[b_wo] + b_mg, writes=[bankB[bko]])
                S_.op("dve", lambda v, s2=s2, s4=s4, dh=dh, bko=bko: v.tensor_tensor(
                    out=xn[s2][:, dh * 512:(dh + 1) * 512], in0=banks[bko][:, :],
                    in1=xt2[s4][:, dh * 512:(dh + 1) * 512], op=ALU.add),
                    reads=[bankB[bko], b_xt2[s4]], writes=[b_xn[s2]])
            tss = S_.op("act", lambda a, s2=s2, tcg=tcg: a.activation(out=junk2[:, :], in_=xn[s2][:, :],
                                                                      func=AF.Square, accum_out=ss2[:, tcg:tcg + 1]),
                        reads=[b_xn[s2], b_ss2], writes=[])
            return S_.op("act", lambda a, tcg=tcg: a.activation(out=rstd2[:, tcg:tcg + 1], in_=ss2[:, tcg:tcg + 1],
                                                                func=AF.Sqrt, scale=1.0 / D, bias=epsc[:, 0:1]),
                         reads=[b_eps], deps=[tss])

        def s4_b(i, tq, tsq):
            tcg = i * 4 + tq
            s2 = tcg % 2
            S_.op("dve", lambda v, tcg=tcg: v.reciprocal(out=rstd2[:, tcg:tcg + 1], in_=rstd2[:, tcg:tcg + 1]),
                  deps=[tsq], writes=[b_rstd2[tcg]])
            S_.op("dve", lambda v, s2=s2, tcg=tcg: v.scalar_tensor_tensor(
                out=ot[s2][:, :], in0=xn[s2][:, :], scalar=rstd2[:, tcg:tcg + 1], in1=fg_bc[:, :],
                op0=ALU.mult, op1=ALU.mult),
                reads=[b_xn[s2], b_rstd2[tcg], b_fg], writes=[b_ot[s2]])
            S_.dma("sp", sl_ot[s2], y[tcg * 128:(tcg + 1) * 128, :], ot[s2][:, :], reads=[b_ot[s2]])

        for k in range(NJ):
            s1_sq(0, k)
            s1_mm(0, k)
        s1_fin(0)
        for k in range(NJ):
            s2_a(0, k)
            if k >= 1:
                s2_b(0, k - 1)
        s2_b(0, NJ - 1)
        load_tile(1, ("gz",))
        for k in range(NJ):
            s1_sq(1, k)
            s1_mm(1, k)
        s1_fin(1)

        for i in range(NT):
            if i + 2 < NT:
                load_tile(i + 2, ("uc",))
            if i + 1 < NT:
                load_tile(i + 1, ("ya",))
            for tq in range(3):
                s4_load(i, tq)
            for dj in range(NJ):
                s3(i, dj)
                if i + 1 < NT:
                    load_chunk("sa", i + 1, dj)
                    load_chunk("sb", i + 1, dj)
                    s2_a(i + 1, dj)
                    if dj >= 1:
                        s2_b(i + 1, dj - 1)
                        if i + 2 < NT:
                            load_chunk("gz", i + 2, dj - 1)
            if i + 1 < NT:
                s2_b(i + 1, NJ - 1)
                if i + 2 < NT:
                    load_chunk("gz", i + 2, NJ - 1)
            if i + 2 < NT:
                for k in range(NJ):
                    s1_sq(i + 2, k)
                    s1_mm(i + 2, k)
            prev = None
            for tq in range(4):
                tsq = s4_a(i, tq)
                if tq == 0:
                    s4_load(i, 3)
                if prev is not None:
                    s4_b(i, tq - 1, prev)
                prev = tsq
            s4_b(i, 3, prev)
            if i + 2 < NT:
                s1_fin(i + 2)

        S_.finish()
        B.close()

        with nc.Block() as block:
            @block.tensor
            def _(e):
                for f in S_.ops["pe"]:
                    f(e)

            @block.scalar
            def _(e):
                for f in S_.ops["act"]:
                    f(e)

            @block.vector
            def _(e):
                for f in S_.ops["dve"]:
                    f(e)

            @block.gpsimd
            def _(e):
                for f in S_.ops["pool"]:
                    f(e)

            @block.sync
            def _(e):
                for f in S_.ops["sp"]:
                    f(e)
    return nc


_NC = None


def kernel(x, c, norm_gain, w_ada, b_ada, w_in, b_merge, conv_a_w, w_out_a, conv_b_w, conv_b_bias,
           ln_b_gain, ln_b_bias, w_out_b, b_out_b, w_o, final_gain):
    global _NC
    if _NC is None:
        _NC = build_nc()
    nc = _NC

    vals = {
        "norm_gain": norm_gain[0], "w_ada": w_ada[0], "b_ada": b_ada[0], "w_in": w_in[0],
        "b_merge": b_merge[0], "conv_a_w": conv_a_w[0], "w_out_a": w_out_a[0],
        "conv_b_w": conv_b_w[0], "conv_b_bias": conv_b_bias[0], "ln_b_gain": ln_b_gain[0],
        "ln_b_bias": ln_b_bias[0], "w_out_b": w_out_b[0], "b_out_b": b_out_b[0], "w_o": w_o[0],
        "final_gain": final_gain,
    }
    x = np.asarray(x)
    c = np.asarray(c)
    base = np.empty((BLOB_N,), dtype=np.float32)
    for name, shape in BLOB_SHAPES:
        if name in ("x", "c"):
            continue
        off, _ = BLOB_OFF[name]
        a = np.asarray(vals[name], dtype=np.float32).reshape(-1)
        base[off:off + a.size] = a
    in_maps = []
    ox, _ = BLOB_OFF["x"]
    oc, _ = BLOB_OFF["c"]
    for b in range(NCORES):
        m = base.copy()
        m[ox:ox + S * D] = np.asarray(x[b], dtype=np.float32).reshape(-1)
        m[oc:oc + D] = np.asarray(c[b], dtype=np.float32).reshape(-1)
        in_maps.append({"blob": m})
    res = run_bass_kernel_spmd(nc, in_maps, core_ids=list(range(NCORES)))
    out = np.stack([np.asarray(res.results[b]["y"], dtype=np.float32) for b in range(NCORES)], axis=0)
    return out
```

```python
import numpy as np
from contextlib import ExitStack

import concourse.bass as bass
import concourse.mybir as mybir
from concourse.bass_utils import run_bass_kernel_spmd

F32 = mybir.dt.float32
BF16 = mybir.dt.bfloat16
AF = mybir.ActivationFunctionType
ALU = mybir.AluOpType

D = 1024
S = 4096
NCORES = 8
DIN = 9216
EPS = 1e-6
TT = 512
NT = S // TT
NJ = 8
NPRM = 41
N_PE_TAPS = 19
R_BMA, R_BMB, R_CAW, R_CBW, R_CBB, R_LNG, R_LNB, R_BOB, R_C = 0, 1, 2, 5, 36, 37, 38, 39, 40
BLK_BA, BLK_CA, BLK_VA, BLK_ZA, BLK_AB, BLK_GB, BLK_ZB, BLK_MA, BLK_MB = range(9)


BLOB_SHAPES = [
    ("x", (S, D)), ("c", (D,)), ("norm_gain", (D,)), ("w_ada", (D, 3 * D)), ("b_ada", (3 * D,)),
    ("w_in", (D, DIN)), ("b_merge", (2 * D,)), ("conv_a_w", (3, D)), ("w_out_a", (D, D)),
    ("conv_b_w", (31, D)), ("conv_b_bias", (D,)), ("ln_b_gain", (D,)), ("ln_b_bias", (D,)),
    ("w_out_b", (D, D)), ("b_out_b", (D,)), ("w_o", (D, D)), ("final_gain", (D,)),
]
BLOB_OFF = {}
_o = 0
for _n, _s in BLOB_SHAPES:
    BLOB_OFF[_n] = (_o, _s)
    _o += int(np.prod(_s))
BLOB_N = _o


class Buf:
    __slots__ = ("w", "r")

    def __init__(self):
        self.w = None
        self.r = []


class Slot:
    __slots__ = ("sem", "count")

    def __init__(self, sem):
        self.sem = sem
        self.count = 0


class Sched:
    ENG = ("pe", "act", "dve", "pool", "sp")

    def __init__(self, nc, stack):
        self.nc = nc
        self.stack = stack
        self.ops = {e: [] for e in self.ENG}
        self.tl = {e: stack.enter_context(nc.semaphore("tl_" + e)) for e in ("pe", "act", "dve", "pool")}
        self.cnt = {e: 0 for e in self.tl}
        self.waited = {e: {} for e in self.ENG}
        self.slots = []
        self.nsem = 4

    def slot(self, name):
        s = Slot(self.stack.enter_context(self.nc.semaphore(name)))
        self.slots.append(s)
        self.nsem += 1
        return s

    def _waits(self, engine, deps):
        best = {}
        for tok in deps:
            if tok is None:
                continue
            sem, val, prod = tok
            if prod == engine and engine == "pe":
                continue
            key = sem.num
            if key not in best or best[key][1] < val:
                best[key] = (sem, val)
        for key, (sem, val) in best.items():
            if self.waited[engine].get(key, 0) >= val:
                continue
            self.waited[engine][key] = val
            self.ops[engine].append(lambda eng, sem=sem, val=val: eng.wait_ge(sem, val))

    def _collect(self, reads, writes, deps):
        out = list(deps)
        for b in reads:
            out.append(b.w)
        for b in writes:
            out.extend(b.r)
            out.append(b.w)
        return out

    def _update(self, tok, reads, writes):
        for b in reads:
            b.r.append(tok)
        for b in writes:
            b.w = tok
            b.r = []

    def op(self, engine, fn, reads=(), writes=(), deps=()):
        self._waits(engine, self._collect(reads, writes, deps))
        self.cnt[engine] += 1
        sem = self.tl[engine]
        tok = (sem, self.cnt[engine], engine)
        self.ops[engine].append(lambda eng, fn=fn, sem=sem: fn(eng).then_inc(sem, 1))
        self._update(tok, reads, writes)
        return tok

    def dma(self, queue, slot, out, in_, reads=(), writes=(), deps=(), noncontig=False):
        self._waits(queue, self._collect(reads, writes, deps))
        slot.count += 16
        tok = (slot.sem, slot.count, None)
        nc = self.nc

        def fn(eng, out=out, in_=in_, sem=slot.sem):
            if noncontig:
                with nc.allow_non_contiguous_dma(reason="small one-time parameter load"):
                    eng.dma_start(out=out, in_=in_).then_inc(sem, 16)
            else:
                eng.dma_start(out=out, in_=in_).then_inc(sem, 16)

        self.ops[queue].append(fn)
        self._update(tok, reads, writes)
        return tok

    def barrier(self, scratch_ap):
        toks = [(self.tl[e], self.cnt[e], e) for e in self.tl if self.cnt[e] > 0]
        toks += [(s.sem, s.count, None) for s in self.slots if s.count > 0]
        t = self.op("act", lambda a: a.activation(out=scratch_ap, in_=scratch_ap, func=AF.Identity), deps=toks)
        for e in self.ENG:
            if e != "act":
                self._waits(e, [t])
        return t

    def finish(self):
        toks = [(s.sem, s.count, None) for s in self.slots if s.count > 0]
        self._waits("sp", toks)


def build_nc():
    nc = bass.Bass("TRN2", target_bir_lowering=False)

    blob = nc.dram_tensor("blob", [BLOB_N], F32, kind="ExternalInput").ap()

    def view(name):
        off, shape = BLOB_OFF[name]
        n = int(np.prod(shape))
        v = blob[off:off + n]
        if len(shape) == 2:
            v = v.rearrange("(a b) -> a b", b=shape[1])
        return v

    x = view("x")
    c = view("c")
    norm_gain = view("norm_gain")
    w_ada = view("w_ada")
    b_ada = view("b_ada")
    w_in = view("w_in")
    b_merge = view("b_merge")
    conv_a_w = view("conv_a_w")
    w_out_a = view("w_out_a")
    conv_b_w = view("conv_b_w")
    conv_b_bias = view("conv_b_bias")
    ln_b_gain = view("ln_b_gain")
    ln_b_bias = view("ln_b_bias")
    w_out_b = view("w_out_b")
    b_out_b = view("b_out_b")
    w_o = view("w_o")
    final_gain = view("final_gain")
    y = nc.dram_tensor("y", [S, D], F32, kind="ExternalOutput").ap()

    sp_ya = nc.dram_tensor("sp_ya", [NT, 128, NJ, TT], BF16).ap()
    sp_uc = nc.dram_tensor("sp_uc", [NT, 128, NJ, TT], BF16).ap()
    sp_gz = nc.dram_tensor("sp_gz", [NT, 128, NJ, TT], BF16).ap()
    sp_sa = nc.dram_tensor("sp_sa", [NT, 128, NJ, TT], BF16).ap()
    sp_sb = nc.dram_tensor("sp_sb", [NT, 128, NJ, TT], BF16).ap()

    sc_woa = nc.dram_tensor("sc_woa", [128, NJ, D], BF16).ap()
    sc_wob = nc.dram_tensor("sc_wob", [128, NJ, D], BF16).ap()
    sc_wo = nc.dram_tensor("sc_wo", [128, NJ, D], BF16).ap()

    with ExitStack() as G:
        S_ = Sched(nc, G)

        def sb(stack, name, shape, dt):
            return stack.enter_context(nc.sbuf_tensor(name, shape, dt))

        ident_bf = sb(G, "ident_bf", [128, 128], BF16)
        ident_f = sb(G, "ident_f", [128, 128], F32)
        ones_bf = sb(G, "ones_bf", [128, 128], BF16)
        cols = sb(G, "cols", [128, NJ * NPRM], F32)
        gate_bc = sb(G, "gate_bc", [128, D], F32)
        ss = sb(G, "ss", [128, 32], F32)
        rstd = sb(G, "rstd", [128, 32], F32)
        ss2 = sb(G, "ss2", [128, 32], F32)
        rstd2 = sb(G, "rstd2", [128, 32], F32)
        bar = sb(G, "bar", [128, 2], F32)
        epsc = sb(G, "epsc", [128, 1], F32)
        banks = [nc.alloc_psum_tensor("bank%d" % i, [128, 512], F32) for i in range(8)]
        bankB = [Buf() for _ in range(8)]

        def col(j, r):
            return cols[:, j * NPRM + r: j * NPRM + r + 1]

        b_ident_bf, b_ident_f, b_ones, b_cols, b_gate = Buf(), Buf(), Buf(), Buf(), Buf()
        b_ss, b_ss2 = Buf(), Buf()

        def mk_ident(t, b):
            S_.op("pool", lambda g: g.memset(t[:, :], 1.0), writes=[b])
            S_.op("pool", lambda g: g.affine_select(out=t[:, :], in_=t[:, :], pattern=[[-1, 128]],
                                                    compare_op=ALU.is_equal, fill=0.0, base=0,
                                                    channel_multiplier=1),
                  reads=[b], writes=[b])

        mk_ident(ident_bf, b_ident_bf)
        mk_ident(ident_f, b_ident_f)
        S_.op("pool", lambda g: g.memset(ones_bf[:, :], 1.0 / 1024.0), writes=[b_ones])
        S_.op("pool", lambda g: g.memset(ss[:, :], 0.0), writes=[b_ss])
        S_.op("pool", lambda g: g.memset(ss2[:, :], 0.0), writes=[b_ss2])
        S_.op("pool", lambda g: g.memset(bar[:, :], 0.0))
        b_eps = Buf()
        S_.op("pool", lambda g: g.memset(epsc[:, :], EPS), writes=[b_eps])

        A = ExitStack()
        hT = sb(A, "hT", [128, NJ, S], BF16)
        wj = [sb(A, "wj%d" % i, [128, 9, NJ, 128], BF16) for i in range(2)]
        b_wj = [Buf(), Buf()]
        sl_wj = [S_.slot("sl_wj%d" % i) for i in range(2)]
        b_hT = [Buf() for _ in range(32)]

        def load_wj(j):
            p = j % 2
            src = w_in.rearrange("(kc p) (blk j m) -> j p blk kc m", p=128, blk=9, j=NJ, m=128)
            t = None
            for blk in range(9):
                t = S_.dma("pool", sl_wj[p], wj[p][:, blk, :, :], src[j, :, blk, :, :],
                           writes=[b_wj[p]] if blk == 0 else [], deps=[] if blk == 0 else [])
            b_wj[p].w = t
            return t

        A0 = ExitStack()
        prm = sb(A0, "prm", [NPRM, D], F32)
        onesF = sb(A0, "onesF", [128, 128], F32)
        lhsT_bc = sb(A0, "lhsT_bc", [128, NJ, 128], BF16)
        c_act = sb(A0, "c_act", [128, NJ], F32)
        wada = [sb(A0, "wada%d" % i, [128, 1536], BF16) for i in range(4)]
        mod_sb = sb(A0, "mod_sb", [128, 3 * D], F32)
        ng_bc = sb(A0, "ng_bc", [128, D], F32)
        gprime = sb(A0, "gprime", [128, D], F32)
        xt = [sb(A0, "xt%d" % i, [128, D], F32) for i in range(4)]
        t1 = [sb(A0, "t1_%d" % i, [128, D], F32) for i in range(2)]
        hb = [sb(A0, "hb%d" % i, [128, D], BF16) for i in range(2)]
        junk = sb(A0, "junk", [128, D], F32)

        sl_prm = S_.slot("sl_prm")
        b_prm = Buf()
        rows = [
            (R_BMA, 2, b_merge.rearrange("(r n) -> r n", r=2)),
            (R_CAW, 3, conv_a_w),
            (R_CBW, 31, conv_b_w),
            (R_CBB, 1, conv_b_bias.rearrange("(r n) -> r n", r=1)),
            (R_LNG, 1, ln_b_gain.rearrange("(r n) -> r n", r=1)),
            (R_LNB, 1, ln_b_bias.rearrange("(r n) -> r n", r=1)),
            (R_BOB, 1, b_out_b.rearrange("(r n) -> r n", r=1)),
            (R_C, 1, c.rearrange("(r n) -> r n", r=1)),
        ]
        t = None
        for (r0, n, src) in rows:
            t = S_.dma("sp", sl_prm, prm[r0:r0 + n, :], src)
        b_prm.w = t

        def tr_prm(pe):
            ins = None
            for kc in range(NJ):
                ins = pe.transpose(banks[0][:, kc * NPRM:(kc + 1) * NPRM], prm[0:NPRM, kc * 128:(kc + 1) * 128],
                                   ident_f[0:NPRM, 0:NPRM])
            return ins

        S_.op("pe", tr_prm, reads=[b_prm, b_ident_f], writes=[bankB[0]])
        S_.op("dve", lambda v: v.tensor_copy(out=cols[:, :], in_=banks[0][:, 0:NJ * NPRM]),
              reads=[bankB[0]], writes=[b_cols])

        b_cact, b_lhs, b_mod, b_ng, b_gp = Buf(), Buf(), Buf(), Buf(), Buf()
        cols3 = cols[:, :].rearrange("p (k r) -> p k r", r=NPRM)
        S_.op("act", lambda a: a.activation(out=c_act[:, :], in_=cols3[:, :, R_C], func=AF.Silu),
              reads=[b_cols], writes=[b_cact])
        S_.op("pool", lambda g: g.memset(onesF[:, :], 1.0))
        b_onesF = Buf()
        b_onesF.w = (S_.tl["pool"], S_.cnt["pool"], "pool")

        def mk_lhs(v):
            ins = None
            for kc in range(NJ):
                ins = v.tensor_scalar(out=lhsT_bc[:, kc, :], in0=onesF[:, :], scalar1=c_act[:, kc:kc + 1],
                                      scalar2=None, op0=ALU.mult)
            return ins

        S_.op("dve", mk_lhs, reads=[b_cact, b_onesF], writes=[b_lhs])
        sl_misc = S_.slot("sl_misc")
        S_.dma("sp", sl_misc, mod_sb[:, :], b_ada.partition_broadcast(128), writes=[b_mod])
        S_.dma("sp", sl_misc, ng_bc[:, :], norm_gain.partition_broadcast(128), writes=[b_ng])
        b_mod.w = (sl_misc.sem, sl_misc.count, None)
        b_ng.w = (sl_misc.sem, sl_misc.count, None)

        sl_wada = [S_.slot("sl_wada%d" % i) for i in range(4)]
        b_wada = [Buf() for _ in range(4)]
        q = 0
        for kc in range(NJ):
            for h in range(2):
                p = q % 4
                q += 1
                S_.dma("pool", sl_wada[p], wada[p][:, :], w_ada[kc * 128:(kc + 1) * 128, h * 1536:(h + 1) * 1536],
                       writes=[b_wada[p]])

                def mm(pe, kc=kc, h=h, p=p):
                    ins = None
                    for n in range(3):
                        ins = pe.matmul(banks[1 + h * 3 + n][:, :], lhsT=lhsT_bc[:, kc, :],
                                        rhs=wada[p][:, n * 512:(n + 1) * 512], start=(kc == 0), stop=(kc == NJ - 1))
                    return ins

                S_.op("pe", mm, reads=[b_wada[p], b_lhs], writes=[bankB[1 + h * 3 + n] for n in range(3)])
        for n in range(6):
            S_.op("dve", lambda v, n=n: v.tensor_tensor(out=mod_sb[:, n * 512:(n + 1) * 512],
                                                        in0=banks[1 + n][:, :], in1=mod_sb[:, n * 512:(n + 1) * 512],
                                                        op=ALU.add),
                  reads=[bankB[1 + n]], writes=[b_mod])
        S_.op("dve", lambda v: v.scalar_tensor_tensor(out=gprime[:, :], in0=mod_sb[:, D:2 * D], scalar=1.0,
                                                      in1=ng_bc[:, :], op0=ALU.add, op1=ALU.mult),
              reads=[b_mod, b_ng], writes=[b_gp])
        S_.op("dve", lambda v: v.tensor_copy(out=gate_bc[:, :], in_=mod_sb[:, 2 * D:3 * D]),
              reads=[b_mod], writes=[b_gate])

        load_wj(0)

        NXT = 4
        sl_xt = [S_.slot("sl_xt%d" % i) for i in range(NXT)]
        b_xt = [Buf() for _ in range(NXT)]
        b_t1 = [Buf(), Buf()]
        b_hb = [Buf(), Buf()]
        b_rstd = [Buf() for _ in range(32)]

        def p0_load(tc):
            s4 = tc % NXT
            S_.dma("sp", sl_xt[s4], xt[s4][:, :], x[tc * 128:(tc + 1) * 128, :], writes=[b_xt[s4]])

        def p0_stats_act(tc):
            s4 = tc % NXT
            tss = S_.op("act", lambda a, s4=s4, tc=tc: a.activation(out=junk[:, :], in_=xt[s4][:, :], func=AF.Square,
                                                                    accum_out=ss[:, tc:tc + 1]),
                        reads=[b_xt[s4], b_ss], writes=[])
            return S_.op("act", lambda a, tc=tc: a.activation(out=rstd[:, tc:tc + 1], in_=ss[:, tc:tc + 1],
                                                              func=AF.Sqrt, scale=1.0 / D, bias=epsc[:, 0:1]),
                         reads=[b_eps], deps=[tss])

        def p0_recip(tc, tsq):
            S_.op("dve", lambda v, tc=tc: v.reciprocal(out=rstd[:, tc:tc + 1], in_=rstd[:, tc:tc + 1]),
                  deps=[tsq], writes=[b_rstd[tc]])

        def p0_main(tc):
            s4 = tc % NXT
            s2 = tc % 2
            S_.op("dve", lambda v, s4=s4, s2=s2, tc=tc: v.scalar_tensor_tensor(
                out=t1[s2][:, :], in0=xt[s4][:, :], scalar=rstd[:, tc:tc + 1], in1=gprime[:, :],
                op0=ALU.mult, op1=ALU.mult),
                reads=[b_xt[s4], b_rstd[tc], b_gp], writes=[b_t1[s2]])
            S_.op("dve", lambda g, s2=s2: g.tensor_tensor(out=hb[s2][:, :], in0=t1[s2][:, :], in1=mod_sb[:, 0:D],
                                                          op=ALU.add),
                  reads=[b_t1[s2], b_mod], writes=[b_hb[s2]])
            bk = 6 + s2
            bbf = banks[bk][:, :].bitcast(BF16)

            def trs(pe, s2=s2, bbf=bbf):
                ins = None
                for kc in range(NJ):
                    ins = pe.transpose(bbf[:, kc * 128:(kc + 1) * 128], hb[s2][:, kc * 128:(kc + 1) * 128],
                                       ident_bf[:, :])
                return ins

            S_.op("pe", trs, reads=[b_hb[s2], b_ident_bf], writes=[bankB[bk]])

        def p0_evac(tc):
            bk = 6 + tc % 2
            bbf = banks[bk][:, :].bitcast(BF16)
            S_.op("act", lambda a, tc=tc, bbf=bbf: a.activation(
                out=hT[:, :, tc * 128:(tc + 1) * 128], in_=bbf.rearrange("p (k t) -> p k t", k=NJ),
                func=AF.Identity),
                reads=[bankB[bk]], writes=[b_hT[tc]])

        p0_load(0)
        p0_load(1)
        tsqs = {0: p0_stats_act(0)}
        p0_recip(0, tsqs[0])
        for tc in range(33):
            if tc + 2 < 32:
                p0_load(tc + 2)
            if tc + 1 < 32:
                tsqs[tc + 1] = p0_stats_act(tc + 1)
            if tc < 32:
                p0_main(tc)
            if tc + 1 < 32:
                p0_recip(tc + 1, tsqs[tc + 1])
            if tc >= 1:
                p0_evac(tc - 1)

        S_.barrier(bar[:, 0:1])
        A0.close()

        A1 = ExitStack()
        diag = [sb(A1, "diag%d" % i, [128, 31, 128], BF16) for i in range(2)]
        cvb = sb(A1, "cvb", [128, S + 2], F32)
        ub = sb(A1, "ub", [128, S + 30], BF16)
        sz_sb = [sb(A1, "sz%d" % i, [128, TT], F32) for i in range(2)]
        ca_sb = [sb(A1, "ca%d" % i, [128, TT], F32) for i in range(2)]
        sg_sb = [sb(A1, "sg%d" % i, [128, TT], F32) for i in range(2)]
        ga = [sb(A1, "ga%d" % i, [128, TT], F32) for i in range(3)]
        acc = [sb(A1, "acc%d" % i, [128, TT], F32) for i in range(2)]
        cacc = [sb(A1, "cacc%d" % i, [128, TT], F32) for i in range(2)]
        b_cacc = [Buf(), Buf()]
        st_names = ("gz", "sa", "sb", "ya", "uc")
        stg = {n: [sb(A1, "st_%s%d" % (n, i), [128, TT], BF16) for i in range(2)] for n in st_names}
        b_stg = {n: [Buf(), Buf()] for n in st_names}
        sl_stg = {n: [S_.slot("sl_%s%d" % (n, i)) for i in range(2)] for n in st_names}
        sp_of = {"gz": sp_gz, "sa": sp_sa, "sb": sp_sb, "ya": sp_ya, "uc": sp_uc}
        b_sp = {n: [Buf() for _ in range(NT)] for n in st_names}
        b_diag = [Buf(), Buf()]
        b_cv = [Buf() for _ in range(NT)]
        b_u = [Buf() for _ in range(NT)]
        b_sz, b_ca, b_sg = [Buf(), Buf()], [Buf(), Buf()], [Buf(), Buf()]
        b_ga = [Buf(), Buf(), Buf()]
        b_acc = [Buf(), Buf()]

        S_.op("pool", lambda g: g.memset(cvb[:, 0:1], 0.0))
        S_.op("pool", lambda g: g.memset(cvb[:, S + 1:S + 2], 0.0))
        S_.op("pool", lambda g: g.memset(ub[:, 0:15], 0.0))
        S_.op("pool", lambda g: g.memset(ub[:, S + 15:S + 30], 0.0))
        tpad = (S_.tl["pool"], S_.cnt["pool"], "pool")

        b_scw = {"woa": Buf(), "wob": Buf(), "wo": Buf()}
        sl_scw = S_.slot("sl_scw")
        S_.dma("pool", sl_scw, sc_woa[:, :, :], w_out_a.rearrange("(kc p) n -> p kc n", p=128))
        S_.dma("pool", sl_scw, sc_wob[:, :, :], w_out_b.rearrange("(kc p) n -> p kc n", p=128))
        tscw = (sl_scw.sem, sl_scw.count, None)
        b_scw["woa"].w = tscw
        b_scw["wob"].w = tscw
        wos_f = [sb(A1, "wos_f%d" % i, [128, D], F32) for i in range(2)]
        wos_b = [sb(A1, "wos_b%d" % i, [128, D], BF16) for i in range(2)]
        b_wosf, b_wosb = [Buf(), Buf()], [Buf(), Buf()]
        sl_wosf = [S_.slot("sl_wosf%d" % i) for i in range(2)]
        sl_wosb = [S_.slot("sl_wosb%d" % i) for i in range(2)]
        for kc in range(NJ):
            p = kc % 2
            S_.dma("sp", sl_wosf[p], wos_f[p][:, :], w_o[kc * 128:(kc + 1) * 128, :], writes=[b_wosf[p]])
            S_.op("dve", lambda v, p=p: v.tensor_tensor(out=wos_b[p][:, :], in0=wos_f[p][:, :],
                                                        in1=gate_bc[:, :], op=ALU.mult),
                  reads=[b_wosf[p], b_gate], writes=[b_wosb[p]])
            S_.dma("sp", sl_wosb[p], sc_wo[:, kc, :], wos_b[p][:, :], reads=[b_wosb[p]], writes=[b_scw["wo"]])

        ring = [0]
        convring = [0]

        def next_bank():
            b = ring[0] % 6
            ring[0] += 1
            return b

        def proj(j, i, blk):
            p = j % 2
            bk = next_bank()

            def fn(pe, p=p, blk=blk, i=i, bk=bk):
                ins = None
                for kc in range(NJ):
                    ins = pe.matmul(banks[bk][:, :], lhsT=wj[p][:, blk, kc, :], rhs=hT[:, kc, i * TT:(i + 1) * TT],
                                    start=(kc == 0), stop=(kc == NJ - 1))
                return ins

            S_.op("pe", fn, reads=[b_wj[p]] + b_hT[i * 4:(i + 1) * 4], writes=[bankB[bk]])
            return bk

        def store(name, slot_i, il, j):
            S_.dma("sp", sl_stg[name][slot_i], sp_of[name][il, :, j, :], stg[name][slot_i][:, :],
                   reads=[b_stg[name][slot_i]], writes=[b_sp[name][il]])

        stc = [0]

        def lagged_parts(j, il, n):
            p = j % 2
            c0 = il * TT
            s = stc[0] % 2
            stc[0] += 1
            cb = 6 + (convring[0] % 2)
            convring[0] += 1
            ureads = [b_u[t] for t in (il - 1, il, il + 1) if 0 <= t < NT]
            cvreads = [b_cv[t] for t in (il - 1, il, il + 1) if 0 <= t < NT]
            gslot = n % 3

            def part_pe():
                def cfn(pe):
                    ins = None
                    for k in range(N_PE_TAPS):
                        ins = pe.matmul(banks[cb][:, :], lhsT=diag[p][:, k, :], rhs=ub[:, c0 + k:c0 + k + TT],
                                        start=(k == 0), stop=(k == N_PE_TAPS - 1))
                    return ins

                S_.op("pe", cfn, reads=[b_diag[p]] + ureads, writes=[bankB[cb]], deps=[tpad])
                S_.op("act", lambda a: a.activation(out=cacc[s][:, :], in_=banks[cb][:, :],
                                                    func=AF.Identity, bias=col(j, R_CBB)),
                      reads=[bankB[cb], b_cols], writes=[b_cacc[s]])

            def taps(k0, k1):
                def emit():
                    for k in range(k0, k1):
                        last = (k == 30)
                        dst = stg["uc"][s] if last else cacc[s]
                        S_.op("dve", lambda v, k=k, dst=dst: v.scalar_tensor_tensor(
                            out=dst[:, :], in0=ub[:, c0 + k:c0 + k + TT], scalar=col(j, R_CBW + k),
                            in1=cacc[s][:, :], op0=ALU.mult, op1=ALU.add),
                            reads=ureads + [b_cacc[s], b_cols],
                            writes=[b_stg["uc"][s]] if last else [b_cacc[s]])
                    if k1 == 31:
                        store("uc", s, il, j)
                return emit

            def part_a():
                S_.op("dve", lambda g: g.tensor_scalar(
                    out=acc[s][:, :], in0=cvb[:, c0:c0 + TT], scalar1=col(j, R_CAW), scalar2=None, op0=ALU.mult),
                    reads=cvreads + [b_cols], writes=[b_acc[s]], deps=[tpad])
                for k in (1, 2):
                    S_.op("dve", lambda g, k=k: g.scalar_tensor_tensor(
                        out=acc[s][:, :], in0=cvb[:, c0 + k:c0 + k + TT], scalar=col(j, R_CAW + k),
                        in1=acc[s][:, :], op0=ALU.mult, op1=ALU.add),
                        reads=cvreads, writes=[b_acc[s]])
                S_.op("dve", lambda g: g.tensor_tensor(
                    out=stg["ya"][s][:, :], in0=acc[s][:, :], in1=ga[gslot][:, :], op=ALU.mult),
                    reads=[b_acc[s], b_ga[gslot]], writes=[b_stg["ya"][s]])
                store("ya", s, il, j)

            nd = 31 - N_PE_TAPS
            c1 = N_PE_TAPS + nd // 3
            c2 = N_PE_TAPS + (2 * nd) // 3
            return [part_pe, taps(N_PE_TAPS, c1), taps(c1, c2), taps(c2, 31), part_a]

        def build_diag(j):
            p = j % 2
            S_.op("dve", lambda v, p=p, j=j: v.tensor_tensor(
                out=diag[p][:, :, :],
                in0=ident_bf[:, :].unsqueeze(1).to_broadcast([128, 31, 128]),
                in1=cols[:, j * NPRM + R_CBW:j * NPRM + R_CBW + 31].unsqueeze(2).to_broadcast([128, 31, 128]),
                op=ALU.mult),
                reads=[b_ident_bf, b_cols], writes=[b_diag[p]])

        build_diag(0)
        NIT = NJ * NT
        for n in range(NIT):
            j, i = divmod(n, NT)
            p = j % 2
            if i == 0 and j + 1 < NJ:
                load_wj(j + 1)
            if i == 2 and j + 1 < NJ:
                build_diag(j + 1)
            c0 = i * TT
            s = n % 2
            parts = None
            if n >= 2:
                jl, il = divmod(n - 2, NT)
                parts = lagged_parts(jl, il, n - 2)
                parts[0]()
            bk_za = proj(j, i, BLK_ZA)
            bk_zb = proj(j, i, BLK_ZB)
            bk_ba = proj(j, i, BLK_BA)
            bk_gb = proj(j, i, BLK_GB)
            bk_ma = proj(j, i, BLK_MA)
            bk_mb = proj(j, i, BLK_MB)
            S_.op("act", lambda a, s=s, bk=bk_za: a.activation(out=sz_sb[s][:, :], in_=banks[bk][:, :],
                                                               func=AF.Silu),
                  reads=[bankB[bk_za]], writes=[b_sz[s]])
            S_.op("act", lambda a, s=s, bk=bk_zb: a.activation(out=stg["gz"][s][:, :], in_=banks[bk][:, :],
                                                               func=AF.Silu),
                  reads=[bankB[bk_zb]], writes=[b_stg["gz"][s]])
            store("gz", s, i, j)
            S_.op("dve", lambda v, s=s, bk=bk_ba, n=n: v.tensor_tensor(
                out=ga[n % 3][:, :], in0=banks[bk][:, :], in1=sz_sb[s][:, :], op=ALU.mult),
                reads=[bankB[bk_ba], b_sz[s]], writes=[b_ga[n % 3]])
            if parts:
                parts[1]()
            S_.op("act", lambda a, s=s, bk=bk_gb: a.activation(out=sg_sb[s][:, :], in_=banks[bk][:, :],
                                                               func=AF.Sigmoid),
                  reads=[bankB[bk_gb]], writes=[b_sg[s]])
            S_.op("act", lambda a, s=s, bk=bk_ma, j=j: a.activation(
                out=stg["sa"][s][:, :], in_=banks[bk][:, :], func=AF.Sigmoid, bias=col(j, R_BMA)),
                reads=[bankB[bk_ma], b_cols], writes=[b_stg["sa"][s]])
            store("sa", s, i, j)
            S_.op("act", lambda a, s=s, bk=bk_mb, j=j: a.activation(
                out=stg["sb"][s][:, :], in_=banks[bk][:, :], func=AF.Sigmoid, bias=col(j, R_BMB)),
                reads=[bankB[bk_mb], b_cols], writes=[b_stg["sb"][s]])
            store("sb", s, i, j)
            bk_ab = proj(j, i, BLK_AB)
            bk_ca = proj(j, i, BLK_CA)
            bk_va = proj(j, i, BLK_VA)
            S_.op("dve", lambda v, s=s, bk=bk_ab, c0=c0: v.tensor_tensor(
                out=ub[:, 15 + c0:15 + c0 + TT], in0=banks[bk][:, :], in1=sg_sb[s][:, :], op=ALU.mult),
                reads=[bankB[bk_ab], b_sg[s]], writes=[b_u[i]])
            if parts:
                parts[2]()
            S_.op("act", lambda a, s=s, bk=bk_ca: a.activation(out=ca_sb[s][:, :], in_=banks[bk][:, :],
                                                               func=AF.Identity),
                  reads=[bankB[bk_ca]], writes=[b_ca[s]])
            S_.op("dve", lambda v, s=s, bk=bk_va, c0=c0: v.tensor_tensor(
                out=cvb[:, 1 + c0:1 + c0 + TT], in0=banks[bk][:, :], in1=ca_sb[s][:, :], op=ALU.mult),
                reads=[bankB[bk_va], b_ca[s]], writes=[b_cv[i]])
            if parts:
                parts[3]()
                parts[4]()
        for n in (NIT - 2, NIT - 1):
            jl, il = divmod(n, NT)
            for part in lagged_parts(jl, il, n):
                part()

        S_.barrier(bar[:, 0:1])
        A1.close()
        A.close()

        B = ExitStack()
        woa_bf = sb(B, "woa_bf", [128, NJ, D], BF16)
        wob_bf = sb(B, "wob_bf", [128, NJ, D], BF16)
        wo_bf = sb(B, "wo_bf", [128, NJ, D], BF16)
        fg_bc = sb(B, "fg_bc", [128, D], F32)
        ld_names = ("uc", "ya", "gz", "sa", "sb")
        nring = {"uc": 2, "ya": 2, "gz": 1, "sa": 1, "sb": 1}
        CHUNKED = ("gz", "sa", "sb")
        ldt = {n: [sb(B, "ld_%s%d" % (n, i), [128, NJ, TT], BF16) for i in range(nring[n])] for n in ld_names}
        b_ldt = {n: [Buf() for _ in range(nring[n])] for n in ("uc", "ya")}
        sl_ldt = {n: [S_.slot("sl_ld_%s%d" % (n, i)) for i in range(nring[n])] for n in ("uc", "ya")}
        b_ldc = {n: [Buf() for _ in range(NJ)] for n in CHUNKED}
        sl_ldc = {n: [S_.slot("sl_ldc_%s%d" % (n, k)) for k in range(NJ)] for n in CHUNKED}
        sq = [sb(B, "sq%d" % i, [128, TT], BF16) for i in range(2)]
        mean_sb = [sb(B, "mean_sb%d" % i, [128, TT], F32) for i in range(2)]
        m2 = sb(B, "m2", [128, TT], F32)
        rstd_t = [sb(B, "rstd_t%d" % i, [128, TT], F32) for i in range(2)]
        tL = sb(B, "tL", [128, 4, TT], F32)
        b_tL = Buf()
        ubp = [sb(B, "ubp%d" % i, [128, NJ, TT], BF16) for i in range(2)]
        u1 = [sb(B, "u1_%d" % i, [128, TT], F32) for i in range(2)]
        u2 = [sb(B, "u2_%d" % i, [128, TT], F32) for i in range(2)]
        mg = sb(B, "mg", [128, NJ, TT], BF16)
        NX2 = 3
        xt2 = [sb(B, "xt2_%d" % i, [128, D], F32) for i in range(NX2)]
        xn = [sb(B, "xn%d" % i, [128, D], F32) for i in range(2)]
        ot = [sb(B, "ot%d" % i, [128, D], F32) for i in range(2)]
        junk2 = sb(B, "junk2", [128, D], BF16)

        b_woa, b_wob, b_wo, b_fg = Buf(), Buf(), Buf(), Buf()
        b_sq = [Buf(), Buf()]
        b_mean, b_rstdt = [Buf(), Buf()], [Buf(), Buf()]
        b_m2 = Buf()
        b_ubp = [[Buf() for _ in range(NJ)] for _ in range(2)]
        b_u1, b_u2 = [Buf(), Buf()], [Buf(), Buf()]
        b_mg = [Buf() for _ in range(NJ)]
        b_xt2 = [Buf() for _ in range(NX2)]
        sl_xt2 = [S_.slot("sl_xt2_%d" % i) for i in range(NX2)]
        b_xn, b_ot = [Buf(), Buf()], [Buf(), Buf()]
        sl_ot = [S_.slot("sl_ot%d" % i) for i in range(2)]
        b_rstd2 = [Buf() for _ in range(32)]

        def load_tile(i, names):
            for n in names:
                if n in CHUNKED:
                    for k in range(NJ):
                        load_chunk(n, i, k)
                    continue
                r = i % nring[n]
                S_.dma("sp", sl_ldt[n][r], ldt[n][r][:, :, :], sp_of[n][i, :, :, :],
                       reads=[b_sp[n][i]], writes=[b_ldt[n][r]])

        def load_chunk(n, i, k):
            S_.dma("sp", sl_ldc[n][k], ldt[n][0][:, k, :], sp_of[n][i, :, k, :],
                   reads=[b_sp[n][i]], writes=[b_ldc[n][k]])

        def T(n, i):
            return ldt[n][i % nring[n]]

        def BT(n, i):
            return b_ldt[n][i % nring[n]]

        REST = ("gz", "ya", "sa", "sb")
        load_tile(0, ("uc",))
        load_tile(1, ("uc",))
        load_tile(0, REST)
        sl_fg = S_.slot("sl_fg")
        S_.dma("sp", sl_fg, fg_bc[:, :], final_gain.partition_broadcast(128), writes=[b_fg])
        sl_w3 = [S_.slot("sl_w3_%d" % i) for i in range(3)]
        S_.dma("sp", sl_w3[0], woa_bf[:, :, :], sc_woa[:, :, :], reads=[b_scw["woa"]], writes=[b_woa])
        S_.dma("sp", sl_w3[1], wob_bf[:, :, :], sc_wob[:, :, :], reads=[b_scw["wob"]], writes=[b_wob])
        S_.dma("sp", sl_w3[2], wo_bf[:, :, :], sc_wo[:, :, :], reads=[b_scw["wo"]], writes=[b_wo])

        BK_MEAN, BK_MSQ = 0, 1
        yring = [0]
        oring = [0]

        def s1_sq(i, k):
            s = k % 2
            S_.op("act", lambda a, s=s, k=k, t=T("uc", i): a.activation(out=sq[s][:, :], in_=t[:, k, :],
                                                                        func=AF.Square),
                  reads=[BT("uc", i)], writes=[b_sq[s]])

        def s1_mm(i, k):
            s = k % 2

            def stat(pe, s=s, k=k, t=T("uc", i)):
                pe.matmul(banks[BK_MEAN][:, :], lhsT=ones_bf[:, :], rhs=t[:, k, :], start=(k == 0),
                          stop=(k == NJ - 1))
                return pe.matmul(banks[BK_MSQ][:, :], lhsT=ones_bf[:, :], rhs=sq[s][:, :], start=(k == 0),
                                 stop=(k == NJ - 1))

            S_.op("pe", stat, reads=[BT("uc", i), b_sq[s], b_ones], writes=[bankB[BK_MEAN], bankB[BK_MSQ]])

        def s1_fin(i):
            r = i % 2
            S_.op("act", lambda a, r=r: a.activation(out=mean_sb[r][:, :], in_=banks[BK_MEAN][:, :],
                                                     func=AF.Identity),
                  reads=[bankB[BK_MEAN]], writes=[b_mean[r]])
            S_.op("dve", lambda v, r=r: v.tensor_tensor(out=m2[:, :], in0=mean_sb[r][:, :], in1=mean_sb[r][:, :],
                                                        op=ALU.mult),
                  reads=[b_mean[r]], writes=[b_m2])
            S_.op("dve", lambda v: v.tensor_tensor(out=m2[:, :], in0=banks[BK_MSQ][:, :], in1=m2[:, :],
                                                   op=ALU.subtract),
                  reads=[bankB[BK_MSQ], b_m2], writes=[b_m2])
            S_.op("dve", lambda v: v.tensor_scalar(out=m2[:, :], in0=m2[:, :], scalar1=0.0, scalar2=None,
                                                   op0=ALU.max),
                  reads=[b_m2], writes=[b_m2])
            S_.op("act", lambda a, r=r: a.activation(out=rstd_t[r][:, :], in_=m2[:, :], func=AF.Sqrt,
                                                     bias=epsc[:, 0:1]),
                  reads=[b_m2, b_eps], writes=[b_rstdt[r]])
            S_.op("dve", lambda v, r=r: v.reciprocal(out=rstd_t[r][:, :], in_=rstd_t[r][:, :]),
                  reads=[b_rstdt[r]], writes=[b_rstdt[r]])

        def s2_sub(i, h):
            r = i % 2
            S_.op("dve", lambda v, r=r, h=h, t=T("uc", i): v.tensor_tensor(
                out=tL[:, :, :], in0=t[:, 4 * h:4 * h + 4, :],
                in1=mean_sb[r][:, :].unsqueeze(1).to_broadcast([128, 4, TT]), op=ALU.subtract),
                reads=[BT("uc", i), b_mean[r]], writes=[b_tL])

        def s2_mul(i, h):
            r = i % 2
            S_.op("dve", lambda v, r=r: v.tensor_tensor(
                out=tL[:, :, :], in0=tL[:, :, :],
                in1=rstd_t[r][:, :].unsqueeze(1).to_broadcast([128, 4, TT]), op=ALU.mult),
                reads=[b_tL, b_rstdt[r]], writes=[b_tL])
            for kk in range(4):
                k = 4 * h + kk
                S_.op("act", lambda a, kk=kk, k=k: a.activation(out=tL[:, kk, :], in_=tL[:, kk, :], func=AF.Silu,
                                                                scale=col(k, R_LNG), bias=col(k, R_LNB)),
                      reads=[b_tL, b_cols], writes=[b_tL])

        def s2_gate(i, h):
            r = i % 2
            ks = list(range(4 * h, 4 * h + 4))
            S_.op("dve", lambda v, r=r, h=h, t=T("gz", i): v.tensor_tensor(
                out=ubp[r][:, 4 * h:4 * h + 4, :], in0=tL[:, :, :], in1=t[:, 4 * h:4 * h + 4, :], op=ALU.mult),
                reads=[b_tL] + [b_ldc["gz"][k] for k in ks], writes=[b_ubp[r][k] for k in ks])

        def s3(i, dj):
            s = dj % 2
            r = i % 2
            bka = 2 + (yring[0] % 4)
            yring[0] += 1
            bkb = 2 + (yring[0] % 4)
            yring[0] += 1

            def mma(pe, dj=dj, bka=bka, t=T("ya", i)):
                ins = None
                for kc in range(NJ):
                    ins = pe.matmul(banks[bka][:, :], lhsT=woa_bf[:, kc, dj * 128:(dj + 1) * 128],
                                    rhs=t[:, kc, :], start=(kc == 0), stop=(kc == NJ - 1))
                return ins

            def mmb(pe, dj=dj, bkb=bkb, r=r):
                ins = None
                for kc in range(NJ):
                    ins = pe.matmul(banks[bkb][:, :], lhsT=wob_bf[:, kc, dj * 128:(dj + 1) * 128],
                                    rhs=ubp[r][:, kc, :], start=(kc == 0), stop=(kc == NJ - 1))
                return ins

            S_.op("pe", mma, reads=[b_woa, BT("ya", i)], writes=[bankB[bka]])
            S_.op("pe", mmb, reads=[b_wob] + b_ubp[r], writes=[bankB[bkb]])
            S_.op("dve", lambda v, s=s, dj=dj, bka=bka, t=T("sa", i): v.tensor_tensor(
                out=u1[s][:, :], in0=banks[bka][:, :], in1=t[:, dj, :], op=ALU.mult),
                reads=[bankB[bka], b_ldc["sa"][dj]], writes=[b_u1[s]])
            S_.op("dve", lambda v, s=s, dj=dj, bkb=bkb, t=T("sb", i): v.scalar_tensor_tensor(
                out=u2[s][:, :], in0=banks[bkb][:, :], scalar=col(dj, R_BOB), in1=t[:, dj, :],
                op0=ALU.add, op1=ALU.mult),
                reads=[bankB[bkb], b_ldc["sb"][dj], b_cols], writes=[b_u2[s]])
            S_.op("dve", lambda g, s=s, dj=dj: g.tensor_tensor(out=mg[:, dj, :], in0=u1[s][:, :],
                                                               in1=u2[s][:, :], op=ALU.add),
                  reads=[b_u1[s], b_u2[s]], writes=[b_mg[dj]])

        def s4_load(i, tq):
            tcg = i * 4 + tq
            s4 = tcg % NX2
            S_.dma("sp", sl_xt2[s4], xt2[s4][:, :], x[tcg * 128:(tcg + 1) * 128, :], writes=[b_xt2[s4]])

        def s4_a(i, tq):
            tcg = i * 4 + tq
            s4 = tcg % NX2
            s2 = tcg % 2
            for dh in range(2):
                bko = 6 + (oring[0] % 2)
                oring[0] += 1

                def mmo(pe, tq=tq, dh=dh, bko=bko):
                    ins = None
                    for kc in range(NJ):
                        ins = pe.matmul(banks[bko][:, :], lhsT=mg[:, kc, tq * 128:(tq + 1) * 128],
                                        rhs=wo_bf[:, kc, dh * 512:(dh + 1) * 512], start=(kc == 0),
                                        stop=(kc == NJ - 1))
                    return ins

                S_.op("pe", mmo, reads=[b_wo] + b_mg, writes=[bankB[bko]])
                S_.op("dve", lambda v, s2=s2, s4=s4, dh=dh, bko=bko: v.tensor_tensor(
                    out=xn[s2][:, dh * 512:(dh + 1) * 512], in0=banks[bko][:, :],
                    in1=xt2[s4][:, dh * 512:(dh + 1) * 512], op=ALU.add),
                    reads=[bankB[bko], b_xt2[s4]], writes=[b_xn[s2]])
            tss = S_.op("act", lambda a, s2=s2, tcg=tcg: a.activation(out=junk2[:, :], in_=xn[s2][:, :],
                                                                      func=AF.Square, accum_out=ss2[:, tcg:tcg + 1]),
                        reads=[b_xn[s2], b_ss2], writes=[])
            return S_.op("act", lambda a, tcg=tcg: a.activation(out=rstd2[:, tcg:tcg + 1], in_=ss2[:, tcg:tcg + 1],
                                                                func=AF.Sqrt, scale=1.0 / D, bias=epsc[:, 0:1]),
                         reads=[b_eps], deps=[tss])

        def s4_b(i, tq, tsq):
            tcg = i * 4 + tq
            s2 = tcg % 2
            S_.op("dve", lambda v, tcg=tcg: v.reciprocal(out=rstd2[:, tcg:tcg + 1], in_=rstd2[:, tcg:tcg + 1]),
                  deps=[tsq], writes=[b_rstd2[tcg]])
            S_.op("dve", lambda v, s2=s2, tcg=tcg: v.scalar_tensor_tensor(
                out=ot[s2][:, :], in0=xn[s2][:, :], scalar=rstd2[:, tcg:tcg + 1], in1=fg_bc[:, :],
                op0=ALU.mult, op1=ALU.mult),
                reads=[b_xn[s2], b_rstd2[tcg], b_fg], writes=[b_ot[s2]])
            S_.dma("sp", sl_ot[s2], y[tcg * 128:(tcg + 1) * 128, :], ot[s2][:, :], reads=[b_ot[s2]])

        for k in range(NJ):
            s1_sq(0, k)
            s1_mm(0, k)
        s1_fin(0)
        for h in range(2):
            s2_sub(0, h)
            s2_mul(0, h)
            s2_gate(0, h)
        load_tile(1, ("gz",))
        for k in range(NJ):
            s1_sq(1, k)
            s1_mm(1, k)
        s1_fin(1)

        for i in range(NT):
            if i + 2 < NT:
                load_tile(i + 2, ("uc",))
            if i + 1 < NT:
                load_tile(i + 1, ("ya",))
            for tq in range(3):
                s4_load(i, tq)
            for dj in range(NJ):
                s3(i, dj)
                if i + 1 < NT:
                    load_chunk("sa", i + 1, dj)
                    load_chunk("sb", i + 1, dj)
                    h = dj // 4
                    if dj % 4 == 0:
                        s2_sub(i + 1, h)
                    elif dj % 4 == 1:
                        s2_mul(i + 1, h)
                    elif dj % 4 == 3:
                        s2_gate(i + 1, h)
                        if i + 2 < NT:
                            for k in range(4 * h, 4 * h + 4):
                                load_chunk("gz", i + 2, k)
            if i + 2 < NT:
                for k in range(NJ):
                    s1_sq(i + 2, k)
                    s1_mm(i + 2, k)
            prev = None
            for tq in range(4):
                tsq = s4_a(i, tq)
                if tq == 0:
                    s4_load(i, 3)
                if prev is not None:
                    s4_b(i, tq - 1, prev)
                prev = tsq
            s4_b(i, 3, prev)
            if i + 2 < NT:
                s1_fin(i + 2)

        S_.finish()
        B.close()

        with nc.Block() as block:
            @block.tensor
            def _(e):
                for f in S_.ops["pe"]:
                    f(e)

            @block.scalar
            def _(e):
                for f in S_.ops["act"]:
                    f(e)

            @block.vector
            def _(e):
                for f in S_.ops["dve"]:
                    f(e)

            @block.gpsimd
            def _(e):
                for f in S_.ops["pool"]:
                    f(e)

            @block.sync
            def _(e):
                for f in S_.ops["sp"]:
                    f(e)
    return nc


_NC = None


def kernel(x, c, norm_gain, w_ada, b_ada, w_in, b_merge, conv_a_w, w_out_a, conv_b_w, conv_b_bias,
           ln_b_gain, ln_b_bias, w_out_b, b_out_b, w_o, final_gain):
    global _NC
    if _NC is None:
        _NC = build_nc()
    nc = _NC

    vals = {
        "norm_gain": norm_gain[0], "w_ada": w_ada[0], "b_ada": b_ada[0], "w_in": w_in[0],
        "b_merge": b_merge[0], "conv_a_w": conv_a_w[0], "w_out_a": w_out_a[0],
        "conv_b_w": conv_b_w[0], "conv_b_bias": conv_b_bias[0], "ln_b_gain": ln_b_gain[0],
        "ln_b_bias": ln_b_bias[0], "w_out_b": w_out_b[0], "b_out_b": b_out_b[0], "w_o": w_o[0],
        "final_gain": final_gain,
    }
    x = np.asarray(x)
    c = np.asarray(c)
    base = np.empty((BLOB_N,), dtype=np.float32)
    for name, shape in BLOB_SHAPES:
        if name in ("x", "c"):
            continue
        off, _ = BLOB_OFF[name]
        a = np.asarray(vals[name], dtype=np.float32).reshape(-1)
        base[off:off + a.size] = a
    in_maps = []
    ox, _ = BLOB_OFF["x"]
    oc, _ = BLOB_OFF["c"]
    for b in range(NCORES):
        m = base.copy()
        m[ox:ox + S * D] = np.asarray(x[b], dtype=np.float32).reshape(-1)
        m[oc:oc + D] = np.asarray(c[b], dtype=np.float32).reshape(-1)
        in_maps.append({"blob": m})
    res = run_bass_kernel_spmd(nc, in_maps, core_ids=list(range(NCORES)))
    out = np.stack([np.asarray(res.results[b]["y"], dtype=np.float32) for b in range(NCORES)], axis=0)
    return out
```

```python
import numpy as np
from contextlib import ExitStack

import concourse.bass as bass
import concourse.mybir as mybir
from concourse.bass_utils import run_bass_kernel_spmd

F32 = mybir.dt.float32
BF16 = mybir.dt.bfloat16
AF = mybir.ActivationFunctionType
ALU = mybir.AluOpType

D = 1024
S = 4096
NCORES = 8
DIN = 9216
EPS = 1e-6
TT = 512
NT = S // TT
NJ = 8
NPRM = 41
N_PE_TAPS = 17
R_BMA, R_BMB, R_CAW, R_CBW, R_CBB, R_LNG, R_LNB, R_BOB, R_C = 0, 1, 2, 5, 36, 37, 38, 39, 40
BLK_BA, BLK_CA, BLK_VA, BLK_ZA, BLK_AB, BLK_GB, BLK_ZB, BLK_MA, BLK_MB = range(9)


BLOB_SHAPES = [
    ("x", (S, D)), ("c", (D,)), ("norm_gain", (D,)), ("w_ada", (D, 3 * D)), ("b_ada", (3 * D,)),
    ("w_in", (D, DIN)), ("b_merge", (2 * D,)), ("conv_a_w", (3, D)), ("w_out_a", (D, D)),
    ("conv_b_w", (31, D)), ("conv_b_bias", (D,)), ("ln_b_gain", (D,)), ("ln_b_bias", (D,)),
    ("w_out_b", (D, D)), ("b_out_b", (D,)), ("w_o", (D, D)), ("final_gain", (D,)),
]
BLOB_OFF = {}
_o = 0
for _n, _s in BLOB_SHAPES:
    BLOB_OFF[_n] = (_o, _s)
    _o += int(np.prod(_s))
BLOB_N = _o


class Buf:
    __slots__ = ("w", "r")

    def __init__(self):
        self.w = None
        self.r = []


class Slot:
    __slots__ = ("sem", "count")

    def __init__(self, sem):
        self.sem = sem
        self.count = 0


class Sched:
    ENG = ("pe", "act", "dve", "pool", "sp")

    def __init__(self, nc, stack):
        self.nc = nc
        self.stack = stack
        self.ops = {e: [] for e in self.ENG}
        self.tl = {e: stack.enter_context(nc.semaphore("tl_" + e)) for e in ("pe", "act", "dve", "pool")}
        self.cnt = {e: 0 for e in self.tl}
        self.waited = {e: {} for e in self.ENG}
        self.slots = []
        self.nsem = 4

    def slot(self, name):
        s = Slot(self.stack.enter_context(self.nc.semaphore(name)))
        self.slots.append(s)
        self.nsem += 1
        return s

    def _waits(self, engine, deps):
        best = {}
        for tok in deps:
            if tok is None:
                continue
            sem, val, prod = tok
            if prod == engine and engine == "pe":
                continue
            key = sem.num
            if key not in best or best[key][1] < val:
                best[key] = (sem, val)
        for key, (sem, val) in best.items():
            if self.waited[engine].get(key, 0) >= val:
                continue
            self.waited[engine][key] = val
            self.ops[engine].append(lambda eng, sem=sem, val=val: eng.wait_ge(sem, val))

    def _collect(self, reads, writes, deps):
        out = list(deps)
        for b in reads:
            out.append(b.w)
        for b in writes:
            out.extend(b.r)
            out.append(b.w)
        return out

    def _update(self, tok, reads, writes):
        for b in reads:
            b.r.append(tok)
        for b in writes:
            b.w = tok
            b.r = []

    def op(self, engine, fn, reads=(), writes=(), deps=()):
        self._waits(engine, self._collect(reads, writes, deps))
        self.cnt[engine] += 1
        sem = self.tl[engine]
        tok = (sem, self.cnt[engine], engine)
        self.ops[engine].append(lambda eng, fn=fn, sem=sem: fn(eng).then_inc(sem, 1))
        self._update(tok, reads, writes)
        return tok

    def dma(self, queue, slot, out, in_, reads=(), writes=(), deps=(), noncontig=False):
        self._waits(queue, self._collect(reads, writes, deps))
        slot.count += 16
        tok = (slot.sem, slot.count, None)
        nc = self.nc

        def fn(eng, out=out, in_=in_, sem=slot.sem):
            if noncontig:
                with nc.allow_non_contiguous_dma(reason="small one-time parameter load"):
                    eng.dma_start(out=out, in_=in_).then_inc(sem, 16)
            else:
                eng.dma_start(out=out, in_=in_).then_inc(sem, 16)

        self.ops[queue].append(fn)
        self._update(tok, reads, writes)
        return tok

    def barrier(self, scratch_ap):
        toks = [(self.tl[e], self.cnt[e], e) for e in self.tl if self.cnt[e] > 0]
        toks += [(s.sem, s.count, None) for s in self.slots if s.count > 0]
        t = self.op("act", lambda a: a.activation(out=scratch_ap, in_=scratch_ap, func=AF.Identity), deps=toks)
        for e in self.ENG:
            if e != "act":
                self._waits(e, [t])
        return t

    def finish(self):
        toks = [(s.sem, s.count, None) for s in self.slots if s.count > 0]
        self._waits("sp", toks)


def build_nc():
    nc = bass.Bass("TRN2", target_bir_lowering=False)

    blob = nc.dram_tensor("blob", [BLOB_N], F32, kind="ExternalInput").ap()

    def view(name):
        off, shape = BLOB_OFF[name]
        n = int(np.prod(shape))
        v = blob[off:off + n]
        if len(shape) == 2:
            v = v.rearrange("(a b) -> a b", b=shape[1])
        return v

    x = view("x")
    c = view("c")
    norm_gain = view("norm_gain")
    w_ada = view("w_ada")
    b_ada = view("b_ada")
    w_in = view("w_in")
    b_merge = view("b_merge")
    conv_a_w = view("conv_a_w")
    w_out_a = view("w_out_a")
    conv_b_w = view("conv_b_w")
    conv_b_bias = view("conv_b_bias")
    ln_b_gain = view("ln_b_gain")
    ln_b_bias = view("ln_b_bias")
    w_out_b = view("w_out_b")
    b_out_b = view("b_out_b")
    w_o = view("w_o")
    final_gain = view("final_gain")
    y = nc.dram_tensor("y", [S, D], F32, kind="ExternalOutput").ap()

    sp_ya = nc.dram_tensor("sp_ya", [NT, 128, NJ, TT], BF16).ap()
    sp_uc = nc.dram_tensor("sp_uc", [NT, 128, NJ, TT], BF16).ap()
    sp_gz = nc.dram_tensor("sp_gz", [NT, 128, NJ, TT], BF16).ap()
    sp_sa = nc.dram_tensor("sp_sa", [NT, 128, NJ, TT], BF16).ap()
    sp_sb = nc.dram_tensor("sp_sb", [NT, 128, NJ, TT], BF16).ap()

    sc_woa = nc.dram_tensor("sc_woa", [128, NJ, D], BF16).ap()
    sc_wob = nc.dram_tensor("sc_wob", [128, NJ, D], BF16).ap()
    sc_wo = nc.dram_tensor("sc_wo", [128, NJ, D], BF16).ap()

    with ExitStack() as G:
        S_ = Sched(nc, G)

        def sb(stack, name, shape, dt):
            return stack.enter_context(nc.sbuf_tensor(name, shape, dt))

        ident_bf = sb(G, "ident_bf", [128, 128], BF16)
        ident_f = sb(G, "ident_f", [128, 128], F32)
        ones_bf = sb(G, "ones_bf", [128, 128], BF16)
        cols = sb(G, "cols", [128, NJ * NPRM], F32)
        gate_bc = sb(G, "gate_bc", [128, D], F32)
        ss = sb(G, "ss", [128, 32], F32)
        rstd = sb(G, "rstd", [128, 32], F32)
        ss2 = sb(G, "ss2", [128, 32], F32)
        rstd2 = sb(G, "rstd2", [128, 32], F32)
        bar = sb(G, "bar", [128, 2], F32)
        epsc = sb(G, "epsc", [128, 1], F32)
        banks = [nc.alloc_psum_tensor("bank%d" % i, [128, 512], F32) for i in range(8)]
        bankB = [Buf() for _ in range(8)]

        def col(j, r):
            return cols[:, j * NPRM + r: j * NPRM + r + 1]

        b_ident_bf, b_ident_f, b_ones, b_cols, b_gate = Buf(), Buf(), Buf(), Buf(), Buf()
        b_ss, b_ss2 = Buf(), Buf()

        def mk_ident(t, b):
            S_.op("pool", lambda g: g.memset(t[:, :], 1.0), writes=[b])
            S_.op("pool", lambda g: g.affine_select(out=t[:, :], in_=t[:, :], pattern=[[-1, 128]],
                                                    compare_op=ALU.is_equal, fill=0.0, base=0,
                                                    channel_multiplier=1),
                  reads=[b], writes=[b])

        mk_ident(ident_bf, b_ident_bf)
        mk_ident(ident_f, b_ident_f)
        S_.op("pool", lambda g: g.memset(ones_bf[:, :], 1.0 / 1024.0), writes=[b_ones])
        S_.op("pool", lambda g: g.memset(ss[:, :], 0.0), writes=[b_ss])
        S_.op("pool", lambda g: g.memset(ss2[:, :], 0.0), writes=[b_ss2])
        S_.op("pool", lambda g: g.memset(bar[:, :], 0.0))
        b_eps = Buf()
        S_.op("pool", lambda g: g.memset(epsc[:, :], EPS), writes=[b_eps])

        A = ExitStack()
        hT = sb(A, "hT", [128, NJ, S], BF16)
        wj = [sb(A, "wj%d" % i, [128, 9, NJ, 128], BF16) for i in range(2)]
        b_wj = [Buf(), Buf()]
        sl_wj = [S_.slot("sl_wj%d" % i) for i in range(2)]
        b_hT = [Buf() for _ in range(32)]

        def load_wj(j):
            p = j % 2
            src = w_in.rearrange("(kc p) (blk j m) -> j p blk kc m", p=128, blk=9, j=NJ, m=128)
            t = None
            for blk in range(9):
                t = S_.dma("pool", sl_wj[p], wj[p][:, blk, :, :], src[j, :, blk, :, :],
                           writes=[b_wj[p]] if blk == 0 else [], deps=[] if blk == 0 else [])
            b_wj[p].w = t
            return t

        A0 = ExitStack()
        prm = sb(A0, "prm", [NPRM, D], F32)
        onesF = sb(A0, "onesF", [128, 128], F32)
        lhsT_bc = sb(A0, "lhsT_bc", [128, NJ, 128], BF16)
        c_act = sb(A0, "c_act", [128, NJ], F32)
        wada = [sb(A0, "wada%d" % i, [128, 1536], BF16) for i in range(4)]
        mod_sb = sb(A0, "mod_sb", [128, 3 * D], F32)
        ng_bc = sb(A0, "ng_bc", [128, D], F32)
        gprime = sb(A0, "gprime", [128, D], F32)
        xt = [sb(A0, "xt%d" % i, [128, D], F32) for i in range(4)]
        t1 = [sb(A0, "t1_%d" % i, [128, D], F32) for i in range(2)]
        hb = [sb(A0, "hb%d" % i, [128, D], BF16) for i in range(2)]
        junk = sb(A0, "junk", [128, D], F32)

        sl_prm = S_.slot("sl_prm")
        b_prm = Buf()
        rows = [
            (R_BMA, 2, b_merge.rearrange("(r n) -> r n", r=2)),
            (R_CAW, 3, conv_a_w),
            (R_CBW, 31, conv_b_w),
            (R_CBB, 1, conv_b_bias.rearrange("(r n) -> r n", r=1)),
            (R_LNG, 1, ln_b_gain.rearrange("(r n) -> r n", r=1)),
            (R_LNB, 1, ln_b_bias.rearrange("(r n) -> r n", r=1)),
            (R_BOB, 1, b_out_b.rearrange("(r n) -> r n", r=1)),
            (R_C, 1, c.rearrange("(r n) -> r n", r=1)),
        ]
        t = None
        for (r0, n, src) in rows:
            t = S_.dma("sp", sl_prm, prm[r0:r0 + n, :], src)
        b_prm.w = t

        def tr_prm(pe):
            ins = None
            for kc in range(NJ):
                ins = pe.transpose(banks[0][:, kc * NPRM:(kc + 1) * NPRM], prm[0:NPRM, kc * 128:(kc + 1) * 128],
                                   ident_f[0:NPRM, 0:NPRM])
            return ins

        S_.op("pe", tr_prm, reads=[b_prm, b_ident_f], writes=[bankB[0]])
        S_.op("dve", lambda v: v.tensor_copy(out=cols[:, :], in_=banks[0][:, 0:NJ * NPRM]),
              reads=[bankB[0]], writes=[b_cols])

        b_cact, b_lhs, b_mod, b_ng, b_gp = Buf(), Buf(), Buf(), Buf(), Buf()
        cols3 = cols[:, :].rearrange("p (k r) -> p k r", r=NPRM)
        S_.op("act", lambda a: a.activation(out=c_act[:, :], in_=cols3[:, :, R_C], func=AF.Silu),
              reads=[b_cols], writes=[b_cact])
        S_.op("pool", lambda g: g.memset(onesF[:, :], 1.0))
        b_onesF = Buf()
        b_onesF.w = (S_.tl["pool"], S_.cnt["pool"], "pool")

        def mk_lhs(v):
            ins = None
            for kc in range(NJ):
                ins = v.tensor_scalar(out=lhsT_bc[:, kc, :], in0=onesF[:, :], scalar1=c_act[:, kc:kc + 1],
                                      scalar2=None, op0=ALU.mult)
            return ins

        S_.op("dve", mk_lhs, reads=[b_cact, b_onesF], writes=[b_lhs])
        sl_misc = S_.slot("sl_misc")
        S_.dma("sp", sl_misc, mod_sb[:, :], b_ada.partition_broadcast(128), writes=[b_mod])
        S_.dma("sp", sl_misc, ng_bc[:, :], norm_gain.partition_broadcast(128), writes=[b_ng])
        b_mod.w = (sl_misc.sem, sl_misc.count, None)
        b_ng.w = (sl_misc.sem, sl_misc.count, None)

        sl_wada = [S_.slot("sl_wada%d" % i) for i in range(4)]
        b_wada = [Buf() for _ in range(4)]
        q = 0
        for kc in range(NJ):
            for h in range(2):
                p = q % 4
                q += 1
                S_.dma("pool", sl_wada[p], wada[p][:, :], w_ada[kc * 128:(kc + 1) * 128, h * 1536:(h + 1) * 1536],
                       writes=[b_wada[p]])

                def mm(pe, kc=kc, h=h, p=p):
                    ins = None
                    for n in range(3):
                        ins = pe.matmul(banks[1 + h * 3 + n][:, :], lhsT=lhsT_bc[:, kc, :],
                                        rhs=wada[p][:, n * 512:(n + 1) * 512], start=(kc == 0), stop=(kc == NJ - 1))
                    return ins

                S_.op("pe", mm, reads=[b_wada[p], b_lhs], writes=[bankB[1 + h * 3 + n] for n in range(3)])
        for n in range(6):
            S_.op("dve", lambda v, n=n: v.tensor_tensor(out=mod_sb[:, n * 512:(n + 1) * 512],
                                                        in0=banks[1 + n][:, :], in1=mod_sb[:, n * 512:(n + 1) * 512],
                                                        op=ALU.add),
                  reads=[bankB[1 + n]], writes=[b_mod])
        S_.op("dve", lambda v: v.scalar_tensor_tensor(out=gprime[:, :], in0=mod_sb[:, D:2 * D], scalar=1.0,
                                                      in1=ng_bc[:, :], op0=ALU.add, op1=ALU.mult),
              reads=[b_mod, b_ng], writes=[b_gp])
        S_.op("dve", lambda v: v.tensor_copy(out=gate_bc[:, :], in_=mod_sb[:, 2 * D:3 * D]),
              reads=[b_mod], writes=[b_gate])

        load_wj(0)

        NXT = 4
        sl_xt = [S_.slot("sl_xt%d" % i) for i in range(NXT)]
        b_xt = [Buf() for _ in range(NXT)]
        b_t1 = [Buf(), Buf()]
        b_hb = [Buf(), Buf()]
        b_rstd = [Buf() for _ in range(32)]

        def p0_load(tc):
            s4 = tc % NXT
            S_.dma("sp", sl_xt[s4], xt[s4][:, :], x[tc * 128:(tc + 1) * 128, :], writes=[b_xt[s4]])

        def p0_stats_act(tc):
            s4 = tc % NXT
            tss = S_.op("act", lambda a, s4=s4, tc=tc: a.activation(out=junk[:, :], in_=xt[s4][:, :], func=AF.Square,
                                                                    accum_out=ss[:, tc:tc + 1]),
                        reads=[b_xt[s4], b_ss], writes=[])
            return S_.op("act", lambda a, tc=tc: a.activation(out=rstd[:, tc:tc + 1], in_=ss[:, tc:tc + 1],
                                                              func=AF.Sqrt, scale=1.0 / D, bias=epsc[:, 0:1]),
                         reads=[b_eps], deps=[tss])

        def p0_recip(tc, tsq):
            S_.op("dve", lambda v, tc=tc: v.reciprocal(out=rstd[:, tc:tc + 1], in_=rstd[:, tc:tc + 1]),
                  deps=[tsq], writes=[b_rstd[tc]])

        def p0_main(tc):
            s4 = tc % NXT
            s2 = tc % 2
            S_.op("dve", lambda v, s4=s4, s2=s2, tc=tc: v.scalar_tensor_tensor(
                out=t1[s2][:, :], in0=xt[s4][:, :], scalar=rstd[:, tc:tc + 1], in1=gprime[:, :],
                op0=ALU.mult, op1=ALU.mult),
                reads=[b_xt[s4], b_rstd[tc], b_gp], writes=[b_t1[s2]])
            S_.op("dve", lambda g, s2=s2: g.tensor_tensor(out=hb[s2][:, :], in0=t1[s2][:, :], in1=mod_sb[:, 0:D],
                                                          op=ALU.add),
                  reads=[b_t1[s2], b_mod], writes=[b_hb[s2]])
            bk = 6 + s2
            bbf = banks[bk][:, :].bitcast(BF16)

            def trs(pe, s2=s2, bbf=bbf):
                ins = None
                for kc in range(NJ):
                    ins = pe.transpose(bbf[:, kc * 128:(kc + 1) * 128], hb[s2][:, kc * 128:(kc + 1) * 128],
                                       ident_bf[:, :])
                return ins

            S_.op("pe", trs, reads=[b_hb[s2], b_ident_bf], writes=[bankB[bk]])

        def p0_evac(tc):
            bk = 6 + tc % 2
            bbf = banks[bk][:, :].bitcast(BF16)
            S_.op("act", lambda a, tc=tc, bbf=bbf: a.activation(
                out=hT[:, :, tc * 128:(tc + 1) * 128], in_=bbf.rearrange("p (k t) -> p k t", k=NJ),
                func=AF.Identity),
                reads=[bankB[bk]], writes=[b_hT[tc]])

        p0_load(0)
        p0_load(1)
        tsqs = {0: p0_stats_act(0)}
        p0_recip(0, tsqs[0])
        for tc in range(33):
            if tc + 2 < 32:
                p0_load(tc + 2)
            if tc + 1 < 32:
                tsqs[tc + 1] = p0_stats_act(tc + 1)
            if tc < 32:
                p0_main(tc)
            if tc + 1 < 32:
                p0_recip(tc + 1, tsqs[tc + 1])
            if tc >= 1:
                p0_evac(tc - 1)

        S_.barrier(bar[:, 0:1])
        A0.close()

        A1 = ExitStack()
        diag = [sb(A1, "diag%d" % i, [128, 31, 128], BF16) for i in range(2)]
        cvb = sb(A1, "cvb", [128, S + 2], F32)
        ub = sb(A1, "ub", [128, S + 30], BF16)
        sz_sb = [sb(A1, "sz%d" % i, [128, TT], F32) for i in range(2)]
        ca_sb = [sb(A1, "ca%d" % i, [128, TT], F32) for i in range(2)]
        sg_sb = [sb(A1, "sg%d" % i, [128, TT], F32) for i in range(2)]
        ga = [sb(A1, "ga%d" % i, [128, TT], F32) for i in range(3)]
        acc = [sb(A1, "acc%d" % i, [128, TT], F32) for i in range(2)]
        cacc = [sb(A1, "cacc%d" % i, [128, TT], F32) for i in range(2)]
        b_cacc = [Buf(), Buf()]
        st_names = ("gz", "sa", "sb", "ya", "uc")
        stg = {n: [sb(A1, "st_%s%d" % (n, i), [128, TT], BF16) for i in range(2)] for n in st_names}
        b_stg = {n: [Buf(), Buf()] for n in st_names}
        sl_stg = {n: [S_.slot("sl_%s%d" % (n, i)) for i in range(2)] for n in st_names}
        sp_of = {"gz": sp_gz, "sa": sp_sa, "sb": sp_sb, "ya": sp_ya, "uc": sp_uc}
        b_sp = {n: [Buf() for _ in range(NT)] for n in st_names}
        b_diag = [Buf(), Buf()]
        b_cv = [Buf() for _ in range(NT)]
        b_u = [Buf() for _ in range(NT)]
        b_sz, b_ca, b_sg = [Buf(), Buf()], [Buf(), Buf()], [Buf(), Buf()]
        b_ga = [Buf(), Buf(), Buf()]
        b_acc = [Buf(), Buf()]

        S_.op("pool", lambda g: g.memset(cvb[:, 0:1], 0.0))
        S_.op("pool", lambda g: g.memset(cvb[:, S + 1:S + 2], 0.0))
        S_.op("pool", lambda g: g.memset(ub[:, 0:15], 0.0))
        S_.op("pool", lambda g: g.memset(ub[:, S + 15:S + 30], 0.0))
        tpad = (S_.tl["pool"], S_.cnt["pool"], "pool")

        b_scw = {"woa": Buf(), "wob": Buf(), "wo": Buf()}
        sl_scw = S_.slot("sl_scw")
        S_.dma("pool", sl_scw, sc_woa[:, :, :], w_out_a.rearrange("(kc p) n -> p kc n", p=128))
        S_.dma("pool", sl_scw, sc_wob[:, :, :], w_out_b.rearrange("(kc p) n -> p kc n", p=128))
        tscw = (sl_scw.sem, sl_scw.count, None)
        b_scw["woa"].w = tscw
        b_scw["wob"].w = tscw
        wos_f = [sb(A1, "wos_f%d" % i, [128, D], F32) for i in range(2)]
        wos_b = [sb(A1, "wos_b%d" % i, [128, D], BF16) for i in range(2)]
        b_wosf, b_wosb = [Buf(), Buf()], [Buf(), Buf()]
        sl_wosf = [S_.slot("sl_wosf%d" % i) for i in range(2)]
        sl_wosb = [S_.slot("sl_wosb%d" % i) for i in range(2)]
        def fold_load(kc):
            p = kc % 2
            S_.dma("sp", sl_wosf[p], wos_f[p][:, :], w_o[kc * 128:(kc + 1) * 128, :], writes=[b_wosf[p]])

        def fold_op(kc):
            p = kc % 2
            S_.op("dve", lambda v, p=p: v.tensor_tensor(out=wos_b[p][:, :], in0=wos_f[p][:, :],
                                                        in1=gate_bc[:, :], op=ALU.mult),
                  reads=[b_wosf[p], b_gate], writes=[b_wosb[p]])
            S_.dma("sp", sl_wosb[p], sc_wo[:, kc, :], wos_b[p][:, :], reads=[b_wosb[p]], writes=[b_scw["wo"]])

        ring = [0]
        convring = [0]

        def next_bank():
            b = ring[0] % 6
            ring[0] += 1
            return b

        def proj(j, i, blk):
            p = j % 2
            bk = next_bank()

            def fn(pe, p=p, blk=blk, i=i, bk=bk):
                ins = None
                for kc in range(NJ):
                    ins = pe.matmul(banks[bk][:, :], lhsT=wj[p][:, blk, kc, :], rhs=hT[:, kc, i * TT:(i + 1) * TT],
                                    start=(kc == 0), stop=(kc == NJ - 1))
                return ins

            S_.op("pe", fn, reads=[b_wj[p]] + b_hT[i * 4:(i + 1) * 4], writes=[bankB[bk]])
            return bk

        def store(name, slot_i, il, j):
            S_.dma("sp", sl_stg[name][slot_i], sp_of[name][il, :, j, :], stg[name][slot_i][:, :],
                   reads=[b_stg[name][slot_i]], writes=[b_sp[name][il]])

        stc = [0]

        def lagged_parts(j, il, n, n_pe=N_PE_TAPS):
            p = j % 2
            c0 = il * TT
            s = stc[0] % 2
            stc[0] += 1
            cb = 6 + (convring[0] % 2)
            convring[0] += 1
            ureads = [b_u[t] for t in (il - 1, il, il + 1) if 0 <= t < NT]
            cvreads = [b_cv[t] for t in (il - 1, il, il + 1) if 0 <= t < NT]
            gslot = n % 3

            def part_pe():
                def cfn(pe):
                    ins = None
                    for k in range(n_pe):
                        ins = pe.matmul(banks[cb][:, :], lhsT=diag[p][:, k, :], rhs=ub[:, c0 + k:c0 + k + TT],
                                        start=(k == 0), stop=(k == n_pe - 1))
                    return ins

                S_.op("pe", cfn, reads=[b_diag[p]] + ureads, writes=[bankB[cb]], deps=[tpad])
                if n_pe == 31:
                    S_.op("act", lambda a: a.activation(out=stg["uc"][s][:, :], in_=banks[cb][:, :],
                                                        func=AF.Identity, bias=col(j, R_CBB)),
                          reads=[bankB[cb], b_cols], writes=[b_stg["uc"][s]])
                    store("uc", s, il, j)
                else:
                    S_.op("act", lambda a: a.activation(out=cacc[s][:, :], in_=banks[cb][:, :],
                                                        func=AF.Identity, bias=col(j, R_CBB)),
                          reads=[bankB[cb], b_cols], writes=[b_cacc[s]])

            def taps(k0, k1):
                def emit():
                    for k in range(k0, k1):
                        last = (k == 30)
                        dst = stg["uc"][s] if last else cacc[s]
                        S_.op("dve", lambda v, k=k, dst=dst: v.scalar_tensor_tensor(
                            out=dst[:, :], in0=ub[:, c0 + k:c0 + k + TT], scalar=col(j, R_CBW + k),
                            in1=cacc[s][:, :], op0=ALU.mult, op1=ALU.add),
                            reads=ureads + [b_cacc[s], b_cols],
                            writes=[b_stg["uc"][s]] if last else [b_cacc[s]])
                    if k1 == 31:
                        store("uc", s, il, j)
                return emit

            def part_a():
                S_.op("dve", lambda g: g.tensor_scalar(
                    out=acc[s][:, :], in0=cvb[:, c0:c0 + TT], scalar1=col(j, R_CAW), scalar2=None, op0=ALU.mult),
                    reads=cvreads + [b_cols], writes=[b_acc[s]], deps=[tpad])
                for k in (1, 2):
                    S_.op("dve", lambda g, k=k: g.scalar_tensor_tensor(
                        out=acc[s][:, :], in0=cvb[:, c0 + k:c0 + k + TT], scalar=col(j, R_CAW + k),
                        in1=acc[s][:, :], op0=ALU.mult, op1=ALU.add),
                        reads=cvreads, writes=[b_acc[s]])
                S_.op("dve", lambda g: g.tensor_tensor(
                    out=stg["ya"][s][:, :], in0=acc[s][:, :], in1=ga[gslot][:, :], op=ALU.mult),
                    reads=[b_acc[s], b_ga[gslot]], writes=[b_stg["ya"][s]])
                store("ya", s, il, j)

            nd = 31 - n_pe
            if nd == 0:
                return [part_pe, part_a]
            c1 = n_pe + nd // 3
            c2 = n_pe + (2 * nd) // 3
            return [part_pe, taps(n_pe, c1), taps(c1, c2), taps(c2, 31), part_a]

        def build_diag(j):
            p = j % 2
            S_.op("dve", lambda v, p=p, j=j: v.tensor_tensor(
                out=diag[p][:, :, :],
                in0=ident_bf[:, :].unsqueeze(1).to_broadcast([128, 31, 128]),
                in1=cols[:, j * NPRM + R_CBW:j * NPRM + R_CBW + 31].unsqueeze(2).to_broadcast([128, 31, 128]),
                op=ALU.mult),
                reads=[b_ident_bf, b_cols], writes=[b_diag[p]])

        build_diag(0)
        NIT = NJ * NT
        for n in range(NIT):
            j, i = divmod(n, NT)
            p = j % 2
            if i == 0 and j + 1 < NJ:
                load_wj(j + 1)
            if i == 2 and j + 1 < NJ:
                build_diag(j + 1)
            c0 = i * TT
            s = n % 2
            parts = None
            if n >= 2:
                jl, il = divmod(n - 2, NT)
                parts = lagged_parts(jl, il, n - 2)
                parts[0]()
            bk_za = proj(j, i, BLK_ZA)
            bk_zb = proj(j, i, BLK_ZB)
            bk_ba = proj(j, i, BLK_BA)
            bk_gb = proj(j, i, BLK_GB)
            bk_ma = proj(j, i, BLK_MA)
            bk_mb = proj(j, i, BLK_MB)
            S_.op("act", lambda a, s=s, bk=bk_za: a.activation(out=sz_sb[s][:, :], in_=banks[bk][:, :],
                                                               func=AF.Silu),
                  reads=[bankB[bk_za]], writes=[b_sz[s]])
            S_.op("act", lambda a, s=s, bk=bk_zb: a.activation(out=stg["gz"][s][:, :], in_=banks[bk][:, :],
                                                               func=AF.Silu),
                  reads=[bankB[bk_zb]], writes=[b_stg["gz"][s]])
            store("gz", s, i, j)
            S_.op("dve", lambda v, s=s, bk=bk_ba, n=n: v.tensor_tensor(
                out=ga[n % 3][:, :], in0=banks[bk][:, :], in1=sz_sb[s][:, :], op=ALU.mult),
                reads=[bankB[bk_ba], b_sz[s]], writes=[b_ga[n % 3]])
            if parts:
                parts[1]()
            S_.op("act", lambda a, s=s, bk=bk_gb: a.activation(out=sg_sb[s][:, :], in_=banks[bk][:, :],
                                                               func=AF.Sigmoid),
                  reads=[bankB[bk_gb]], writes=[b_sg[s]])
            S_.op("act", lambda a, s=s, bk=bk_ma, j=j: a.activation(
                out=stg["sa"][s][:, :], in_=banks[bk][:, :], func=AF.Sigmoid, bias=col(j, R_BMA)),
                reads=[bankB[bk_ma], b_cols], writes=[b_stg["sa"][s]])
            store("sa", s, i, j)
            S_.op("act", lambda a, s=s, bk=bk_mb, j=j: a.activation(
                out=stg["sb"][s][:, :], in_=banks[bk][:, :], func=AF.Sigmoid, bias=col(j, R_BMB)),
                reads=[bankB[bk_mb], b_cols], writes=[b_stg["sb"][s]])
            store("sb", s, i, j)
            bk_ab = proj(j, i, BLK_AB)
            bk_ca = proj(j, i, BLK_CA)
            bk_va = proj(j, i, BLK_VA)
            S_.op("dve", lambda v, s=s, bk=bk_ab, c0=c0: v.tensor_tensor(
                out=ub[:, 15 + c0:15 + c0 + TT], in0=banks[bk][:, :], in1=sg_sb[s][:, :], op=ALU.mult),
                reads=[bankB[bk_ab], b_sg[s]], writes=[b_u[i]])
            if parts:
                parts[2]()
            S_.op("act", lambda a, s=s, bk=bk_ca: a.activation(out=ca_sb[s][:, :], in_=banks[bk][:, :],
                                                               func=AF.Identity),
                  reads=[bankB[bk_ca]], writes=[b_ca[s]])
            S_.op("dve", lambda v, s=s, bk=bk_va, c0=c0: v.tensor_tensor(
                out=cvb[:, 1 + c0:1 + c0 + TT], in0=banks[bk][:, :], in1=ca_sb[s][:, :], op=ALU.mult),
                reads=[bankB[bk_va], b_ca[s]], writes=[b_cv[i]])
            if parts:
                parts[3]()
                parts[4]()
            if 4 <= n < 4 + NJ:
                fold_op(n - 4)
            if 2 <= n < 2 + NJ:
                fold_load(n - 2)
        for n in (NIT - 2, NIT - 1):
            jl, il = divmod(n, NT)
            for part in lagged_parts(jl, il, n, n_pe=31):
                part()

        S_.barrier(bar[:, 0:1])
        A1.close()
        A.close()

        B = ExitStack()
        woa_bf = sb(B, "woa_bf", [128, NJ, D], BF16)
        wob_bf = sb(B, "wob_bf", [128, NJ, D], BF16)
        wo_bf = sb(B, "wo_bf", [128, NJ, D], BF16)
        fg_bc = sb(B, "fg_bc", [128, D], F32)
        ld_names = ("uc", "ya", "gz", "sa", "sb")
        nring = {"uc": 2, "ya": 2, "gz": 1, "sa": 1, "sb": 1}
        CHUNKED = ("gz", "sa", "sb")
        ldt = {n: [sb(B, "ld_%s%d" % (n, i), [128, NJ, TT], BF16) for i in range(nring[n])] for n in ld_names}
        b_ldt = {n: [Buf() for _ in range(nring[n])] for n in ("uc", "ya")}
        sl_ldt = {n: [S_.slot("sl_ld_%s%d" % (n, i)) for i in range(nring[n])] for n in ("uc", "ya")}
        b_ldc = {n: [Buf() for _ in range(NJ)] for n in CHUNKED}
        sl_ldc = {n: [S_.slot("sl_ldc_%s%d" % (n, k)) for k in range(NJ)] for n in CHUNKED}
        sq = [sb(B, "sq%d" % i, [128, TT], BF16) for i in range(2)]
        mean_sb = [sb(B, "mean_sb%d" % i, [128, TT], F32) for i in range(2)]
        m2 = sb(B, "m2", [128, TT], F32)
        rstd_t = [sb(B, "rstd_t%d" % i, [128, TT], F32) for i in range(2)]
        tL = sb(B, "tL", [128, 4, TT], F32)
        b_tL = Buf()
        ubp = [sb(B, "ubp%d" % i, [128, NJ, TT], BF16) for i in range(2)]
        u1 = [sb(B, "u1_%d" % i, [128, TT], F32) for i in range(2)]
        u2 = [sb(B, "u2_%d" % i, [128, TT], F32) for i in range(2)]
        mg = sb(B, "mg", [128, NJ, TT], BF16)
        NX2 = 3
        xt2 = [sb(B, "xt2_%d" % i, [128, D], F32) for i in range(NX2)]
        xn = [sb(B, "xn%d" % i, [128, D], F32) for i in range(2)]
        ot = [sb(B, "ot%d" % i, [128, D], F32) for i in range(2)]
        junk2 = sb(B, "junk2", [128, D], BF16)

        b_woa, b_wob, b_wo, b_fg = Buf(), Buf(), Buf(), Buf()
        b_sq = [Buf(), Buf()]
        b_mean, b_rstdt = [Buf(), Buf()], [Buf(), Buf()]
        b_m2 = Buf()
        b_ubp = [[Buf() for _ in range(NJ)] for _ in range(2)]
        b_u1, b_u2 = [Buf(), Buf()], [Buf(), Buf()]
        b_mg = [Buf() for _ in range(NJ)]
        b_xt2 = [Buf() for _ in range(NX2)]
        sl_xt2 = [S_.slot("sl_xt2_%d" % i) for i in range(NX2)]
        b_xn, b_ot = [Buf(), Buf()], [Buf(), Buf()]
        sl_ot = [S_.slot("sl_ot%d" % i) for i in range(2)]
        b_rstd2 = [Buf() for _ in range(32)]

        def load_tile(i, names):
            for n in names:
                if n in CHUNKED:
                    for k in range(NJ):
                        load_chunk(n, i, k)
                    continue
                r = i % nring[n]
                S_.dma("sp", sl_ldt[n][r], ldt[n][r][:, :, :], sp_of[n][i, :, :, :],
                       reads=[b_sp[n][i]], writes=[b_ldt[n][r]])

        def load_chunk(n, i, k):
            S_.dma("sp", sl_ldc[n][k], ldt[n][0][:, k, :], sp_of[n][i, :, k, :],
                   reads=[b_sp[n][i]], writes=[b_ldc[n][k]])

        def T(n, i):
            return ldt[n][i % nring[n]]

        def BT(n, i):
            return b_ldt[n][i % nring[n]]

        REST = ("gz", "ya", "sa", "sb")
        load_tile(0, ("uc",))
        load_tile(1, ("uc",))
        load_tile(0, REST)
        sl_fg = S_.slot("sl_fg")
        S_.dma("sp", sl_fg, fg_bc[:, :], final_gain.partition_broadcast(128), writes=[b_fg])
        sl_w3 = [S_.slot("sl_w3_%d" % i) for i in range(3)]
        S_.dma("sp", sl_w3[0], woa_bf[:, :, :], sc_woa[:, :, :], reads=[b_scw["woa"]], writes=[b_woa])
        S_.dma("sp", sl_w3[1], wob_bf[:, :, :], sc_wob[:, :, :], reads=[b_scw["wob"]], writes=[b_wob])
        S_.dma("sp", sl_w3[2], wo_bf[:, :, :], sc_wo[:, :, :], reads=[b_scw["wo"]], writes=[b_wo])

        BK_MEAN, BK_MSQ = 0, 1
        yring = [0]
        oring = [0]

        def s1_sq(i, k):
            s = k % 2
            S_.op("act", lambda a, s=s, k=k, t=T("uc", i): a.activation(out=sq[s][:, :], in_=t[:, k, :],
                                                                        func=AF.Square),
                  reads=[BT("uc", i)], writes=[b_sq[s]])

        def s1_mm(i, k):
            s = k % 2

            def stat(pe, s=s, k=k, t=T("uc", i)):
                pe.matmul(banks[BK_MEAN][:, :], lhsT=ones_bf[:, :], rhs=t[:, k, :], start=(k == 0),
                          stop=(k == NJ - 1))
                return pe.matmul(banks[BK_MSQ][:, :], lhsT=ones_bf[:, :], rhs=sq[s][:, :], start=(k == 0),
                                 stop=(k == NJ - 1))

            S_.op("pe", stat, reads=[BT("uc", i), b_sq[s], b_ones], writes=[bankB[BK_MEAN], bankB[BK_MSQ]])

        def s1_fin(i):
            r = i % 2
            S_.op("act", lambda a, r=r: a.activation(out=mean_sb[r][:, :], in_=banks[BK_MEAN][:, :],
                                                     func=AF.Identity),
                  reads=[bankB[BK_MEAN]], writes=[b_mean[r]])
            S_.op("dve", lambda v, r=r: v.tensor_tensor(out=m2[:, :], in0=mean_sb[r][:, :], in1=mean_sb[r][:, :],
                                                        op=ALU.mult),
                  reads=[b_mean[r]], writes=[b_m2])
            S_.op("dve", lambda v: v.tensor_tensor(out=m2[:, :], in0=banks[BK_MSQ][:, :], in1=m2[:, :],
                                                   op=ALU.subtract),
                  reads=[bankB[BK_MSQ], b_m2], writes=[b_m2])
            S_.op("dve", lambda v: v.tensor_scalar(out=m2[:, :], in0=m2[:, :], scalar1=0.0, scalar2=None,
                                                   op0=ALU.max),
                  reads=[b_m2], writes=[b_m2])
            S_.op("act", lambda a, r=r: a.activation(out=rstd_t[r][:, :], in_=m2[:, :], func=AF.Sqrt,
                                                     bias=epsc[:, 0:1]),
                  reads=[b_m2, b_eps], writes=[b_rstdt[r]])
            S_.op("dve", lambda v, r=r: v.reciprocal(out=rstd_t[r][:, :], in_=rstd_t[r][:, :]),
                  reads=[b_rstdt[r]], writes=[b_rstdt[r]])

        def s2_sub(i, h):
            r = i % 2
            S_.op("dve", lambda v, r=r, h=h, t=T("uc", i): v.tensor_tensor(
                out=tL[:, :, :], in0=t[:, 4 * h:4 * h + 4, :],
                in1=mean_sb[r][:, :].unsqueeze(1).to_broadcast([128, 4, TT]), op=ALU.subtract),
                reads=[BT("uc", i), b_mean[r]], writes=[b_tL])

        def s2_mul(i, h):
            r = i % 2
            S_.op("dve", lambda v, r=r: v.tensor_tensor(
                out=tL[:, :, :], in0=tL[:, :, :],
                in1=rstd_t[r][:, :].unsqueeze(1).to_broadcast([128, 4, TT]), op=ALU.mult),
                reads=[b_tL, b_rstdt[r]], writes=[b_tL])
            for kk in range(4):
                k = 4 * h + kk
                S_.op("act", lambda a, kk=kk, k=k: a.activation(out=tL[:, kk, :], in_=tL[:, kk, :], func=AF.Silu,
                                                                scale=col(k, R_LNG), bias=col(k, R_LNB)),
                      reads=[b_tL, b_cols], writes=[b_tL])

        def s2_gate(i, h):
            r = i % 2
            ks = list(range(4 * h, 4 * h + 4))
            S_.op("dve", lambda v, r=r, h=h, t=T("gz", i): v.tensor_tensor(
                out=ubp[r][:, 4 * h:4 * h + 4, :], in0=tL[:, :, :], in1=t[:, 4 * h:4 * h + 4, :], op=ALU.mult),
                reads=[b_tL] + [b_ldc["gz"][k] for k in ks], writes=[b_ubp[r][k] for k in ks])

        def s3(i, dj):
            s = dj % 2
            r = i % 2
            bka = 2 + (yring[0] % 4)
            yring[0] += 1
            bkb = 2 + (yring[0] % 4)
            yring[0] += 1

            def mma(pe, dj=dj, bka=bka, t=T("ya", i)):
                ins = None
                for kc in range(NJ):
                    ins = pe.matmul(banks[bka][:, :], lhsT=woa_bf[:, kc, dj * 128:(dj + 1) * 128],
                                    rhs=t[:, kc, :], start=(kc == 0), stop=(kc == NJ - 1))
                return ins

            def mmb(pe, dj=dj, bkb=bkb, r=r):
                ins = None
                for kc in range(NJ):
                    ins = pe.matmul(banks[bkb][:, :], lhsT=wob_bf[:, kc, dj * 128:(dj + 1) * 128],
                                    rhs=ubp[r][:, kc, :], start=(kc == 0), stop=(kc == NJ - 1))
                return ins

            S_.op("pe", mma, reads=[b_woa, BT("ya", i)], writes=[bankB[bka]])
            S_.op("pe", mmb, reads=[b_wob] + b_ubp[r], writes=[bankB[bkb]])
            S_.op("dve", lambda v, s=s, dj=dj, bka=bka, t=T("sa", i): v.tensor_tensor(
                out=u1[s][:, :], in0=banks[bka][:, :], in1=t[:, dj, :], op=ALU.mult),
                reads=[bankB[bka], b_ldc["sa"][dj]], writes=[b_u1[s]])
            S_.op("dve", lambda v, s=s, dj=dj, bkb=bkb, t=T("sb", i): v.scalar_tensor_tensor(
                out=u2[s][:, :], in0=banks[bkb][:, :], scalar=col(dj, R_BOB), in1=t[:, dj, :],
                op0=ALU.add, op1=ALU.mult),
                reads=[bankB[bkb], b_ldc["sb"][dj], b_cols], writes=[b_u2[s]])
            S_.op("dve", lambda g, s=s, dj=dj: g.tensor_tensor(out=mg[:, dj, :], in0=u1[s][:, :],
                                                               in1=u2[s][:, :], op=ALU.add),
                  reads=[b_u1[s], b_u2[s]], writes=[b_mg[dj]])

        def s4_load(i, tq):
            tcg = i * 4 + tq
            s4 = tcg % NX2
            S_.dma("sp", sl_xt2[s4], xt2[s4][:, :], x[tcg * 128:(tcg + 1) * 128, :], writes=[b_xt2[s4]])

        def s4_a(i, tq):
            tcg = i * 4 + tq
            s4 = tcg % NX2
            s2 = tcg % 2
            for dh in range(2):
                bko = 6 + (oring[0] % 2)
                oring[0] += 1

                def mmo(pe, tq=tq, dh=dh, bko=bko):
                    ins = None
                    for kc in range(NJ):
                        ins = pe.matmul(banks[bko][:, :], lhsT=mg[:, kc, tq * 128:(tq + 1) * 128],
                                        rhs=wo_bf[:, kc, dh * 512:(dh + 1) * 512], start=(kc == 0),
                                        stop=(kc == NJ - 1))
                    return ins

                S_.op("pe", mmo, reads=[b_wo] + b_mg, writes=[bankB[bko]])
                S_.op("dve", lambda v, s2=s2, s4=s4, dh=dh, bko=bko: v.tensor_tensor(
                    out=xn[s2][:, dh * 512:(dh + 1) * 512], in0=banks[bko][:, :],
                    in1=xt2[s4][:, dh * 512:(dh + 1) * 512], op=ALU.add),
                    reads=[bankB[bko], b_xt2[s4]], writes=[b_xn[s2]])
            tss = S_.op("act", lambda a, s2=s2, tcg=tcg: a.activation(out=junk2[:, :], in_=xn[s2][:, :],
                                                                      func=AF.Square, accum_out=ss2[:, tcg:tcg + 1]),
                        reads=[b_xn[s2], b_ss2], writes=[])
            return S_.op("act", lambda a, tcg=tcg: a.activation(out=rstd2[:, tcg:tcg + 1], in_=ss2[:, tcg:tcg + 1],
                                                                func=AF.Sqrt, scale=1.0 / D, bias=epsc[:, 0:1]),
                         reads=[b_eps], deps=[tss])

        def s4_b(i, tq, tsq):
            tcg = i * 4 + tq
            s2 = tcg % 2
            S_.op("dve", lambda v, tcg=tcg: v.reciprocal(out=rstd2[:, tcg:tcg + 1], in_=rstd2[:, tcg:tcg + 1]),
                  deps=[tsq], writes=[b_rstd2[tcg]])
            S_.op("dve", lambda v, s2=s2, tcg=tcg: v.scalar_tensor_tensor(
                out=ot[s2][:, :], in0=xn[s2][:, :], scalar=rstd2[:, tcg:tcg + 1], in1=fg_bc[:, :],
                op0=ALU.mult, op1=ALU.mult),
                reads=[b_xn[s2], b_rstd2[tcg], b_fg], writes=[b_ot[s2]])
            S_.dma("sp", sl_ot[s2], y[tcg * 128:(tcg + 1) * 128, :], ot[s2][:, :], reads=[b_ot[s2]])

        for k in range(NJ):
            s1_sq(0, k)
            s1_mm(0, k)
        s1_fin(0)
        for h in range(2):
            s2_sub(0, h)
            s2_mul(0, h)
            s2_gate(0, h)
        load_tile(1, ("gz",))
        for k in range(NJ):
            s1_sq(1, k)
            s1_mm(1, k)
        s1_fin(1)

        for i in range(NT):
            if i + 2 < NT:
                load_tile(i + 2, ("uc",))
            if i + 1 < NT:
                load_tile(i + 1, ("ya",))
            for tq in range(3):
                s4_load(i, tq)
            for dj in range(NJ):
                s3(i, dj)
                if i + 1 < NT:
                    load_chunk("sa", i + 1, dj)
                    load_chunk("sb", i + 1, dj)
                    h = dj // 4
                    if dj % 4 == 0:
                        s2_sub(i + 1, h)
                    elif dj % 4 == 1:
                        s2_mul(i + 1, h)
                    elif dj % 4 == 3:
                        s2_gate(i + 1, h)
                        if i + 2 < NT:
                            for k in range(4 * h, 4 * h + 4):
                                load_chunk("gz", i + 2, k)
            if i + 2 < NT:
                for k in range(NJ):
                    s1_sq(i + 2, k)
                    s1_mm(i + 2, k)
            prev = None
            for tq in range(4):
                tsq = s4_a(i, tq)
                if tq == 0:
                    s4_load(i, 3)
                if prev is not None:
                    s4_b(i, tq - 1, prev)
                prev = tsq
            s4_b(i, 3, prev)
            if i + 2 < NT:
                s1_fin(i + 2)

        S_.finish()
        B.close()

        with nc.Block() as block:
            @block.tensor
            def _(e):
                for f in S_.ops["pe"]:
                    f(e)

            @block.scalar
            def _(e):
                for f in S_.ops["act"]:
                    f(e)

            @block.vector
            def _(e):
                for f in S_.ops["dve"]:
                    f(e)

            @block.gpsimd
            def _(e):
                for f in S_.ops["pool"]:
                    f(e)

            @block.sync
            def _(e):
                for f in S_.ops["sp"]:
                    f(e)
    return nc


_NC = None


def kernel(x, c, norm_gain, w_ada, b_ada, w_in, b_merge, conv_a_w, w_out_a, conv_b_w, conv_b_bias,
           ln_b_gain, ln_b_bias, w_out_b, b_out_b, w_o, final_gain):
    global _NC
    if _NC is None:
        _NC = build_nc()
    nc = _NC

    vals = {
        "norm_gain": norm_gain[0], "w_ada": w_ada[0], "b_ada": b_ada[0], "w_in": w_in[0],
        "b_merge": b_merge[0], "conv_a_w": conv_a_w[0], "w_out_a": w_out_a[0],
        "conv_b_w": conv_b_w[0], "conv_b_bias": conv_b_bias[0], "ln_b_gain": ln_b_gain[0],
        "ln_b_bias": ln_b_bias[0], "w_out_b": w_out_b[0], "b_out_b": b_out_b[0], "w_o": w_o[0],
        "final_gain": final_gain,
    }
    x = np.asarray(x)
    c = np.asarray(c)
    base = np.empty((BLOB_N,), dtype=np.float32)
    for name, shape in BLOB_SHAPES:
        if name in ("x", "c"):
            continue
        off, _ = BLOB_OFF[name]
        a = np.asarray(vals[name], dtype=np.float32).reshape(-1)
        base[off:off + a.size] = a
    in_maps = []
    ox, _ = BLOB_OFF["x"]
    oc, _ = BLOB_OFF["c"]
    for b in range(NCORES):
        m = base.copy()
        m[ox:ox + S * D] = np.asarray(x[b], dtype=np.float32).reshape(-1)
        m[oc:oc + D] = np.asarray(c[b], dtype=np.float32).reshape(-1)
        in_maps.append({"blob": m})
    res = run_bass_kernel_spmd(nc, in_maps, core_ids=list(range(NCORES)))
    out = np.stack([np.asarray(res.results[b]["y"], dtype=np.float32) for b in range(NCORES)], axis=0)
    return out
```

```python
import numpy as np
from contextlib import ExitStack

import concourse.bass as bass
import concourse.mybir as mybir
from concourse.bass_utils import run_bass_kernel_spmd

F32 = mybir.dt.float32
BF16 = mybir.dt.bfloat16
AF = mybir.ActivationFunctionType
ALU = mybir.AluOpType

D = 1024
S = 4096
NCORES = 8
DIN = 9216
EPS = 1e-6
TT = 512
NT = S // TT
NJ = 8
NPRM = 41
N_PE_TAPS = 16
R_BMA, R_BMB, R_CAW, R_CBW, R_CBB, R_LNG, R_LNB, R_BOB, R_C = 0, 1, 2, 5, 36, 37, 38, 39, 40
BLK_BA, BLK_CA, BLK_VA, BLK_ZA, BLK_AB, BLK_GB, BLK_ZB, BLK_MA, BLK_MB = range(9)


BLOB_SHAPES = [
    ("x", (S, D)), ("c", (D,)), ("norm_gain", (D,)), ("w_ada", (D, 3 * D)), ("b_ada", (3 * D,)),
    ("w_in", (D, DIN)), ("b_merge", (2 * D,)), ("conv_a_w", (3, D)), ("w_out_a", (D, D)),
    ("conv_b_w", (31, D)), ("conv_b_bias", (D,)), ("ln_b_gain", (D,)), ("ln_b_bias", (D,)),
    ("w_out_b", (D, D)), ("b_out_b", (D,)), ("w_o", (D, D)), ("final_gain", (D,)),
]
BLOB_OFF = {}
_o = 0
for _n, _s in BLOB_SHAPES:
    BLOB_OFF[_n] = (_o, _s)
    _o += int(np.prod(_s))
BLOB_N = _o


class Buf:
    __slots__ = ("w", "r")

    def __init__(self):
        self.w = None
        self.r = []


class Slot:
    __slots__ = ("sem", "count")

    def __init__(self, sem):
        self.sem = sem
        self.count = 0


class Sched:
    ENG = ("pe", "act", "dve", "pool", "sp")

    def __init__(self, nc, stack):
        self.nc = nc
        self.stack = stack
        self.ops = {e: [] for e in self.ENG}
        self.tl = {e: stack.enter_context(nc.semaphore("tl_" + e)) for e in ("pe", "act", "dve", "pool")}
        self.cnt = {e: 0 for e in self.tl}
        self.waited = {e: {} for e in self.ENG}
        self.slots = []
        self.nsem = 4

    def slot(self, name):
        s = Slot(self.stack.enter_context(self.nc.semaphore(name)))
        self.slots.append(s)
        self.nsem += 1
        return s

    def _waits(self, engine, deps):
        best = {}
        for tok in deps:
            if tok is None:
                continue
            sem, val, prod = tok
            if prod == engine and engine == "pe":
                continue
            key = sem.num
            if key not in best or best[key][1] < val:
                best[key] = (sem, val)
        for key, (sem, val) in best.items():
            if self.waited[engine].get(key, 0) >= val:
                continue
            self.waited[engine][key] = val
            self.ops[engine].append(lambda eng, sem=sem, val=val: eng.wait_ge(sem, val))

    def _collect(self, reads, writes, deps):
        out = list(deps)
        for b in reads:
            out.append(b.w)
        for b in writes:
            out.extend(b.r)
            out.append(b.w)
        return out

    def _update(self, tok, reads, writes):
        for b in reads:
            b.r.append(tok)
        for b in writes:
            b.w = tok
            b.r = []

    def op(self, engine, fn, reads=(), writes=(), deps=()):
        self._waits(engine, self._collect(reads, writes, deps))
        self.cnt[engine] += 1
        sem = self.tl[engine]
        tok = (sem, self.cnt[engine], engine)
        self.ops[engine].append(lambda eng, fn=fn, sem=sem: fn(eng).then_inc(sem, 1))
        self._update(tok, reads, writes)
        return tok

    def dma(self, queue, slot, out, in_, reads=(), writes=(), deps=(), noncontig=False):
        self._waits(queue, self._collect(reads, writes, deps))
        slot.count += 16
        tok = (slot.sem, slot.count, None)
        nc = self.nc

        def fn(eng, out=out, in_=in_, sem=slot.sem):
            if noncontig:
                with nc.allow_non_contiguous_dma(reason="small one-time parameter load"):
                    eng.dma_start(out=out, in_=in_).then_inc(sem, 16)
            else:
                eng.dma_start(out=out, in_=in_).then_inc(sem, 16)

        self.ops[queue].append(fn)
        self._update(tok, reads, writes)
        return tok

    def barrier(self, scratch_ap):
        toks = [(self.tl[e], self.cnt[e], e) for e in self.tl if self.cnt[e] > 0]
        toks += [(s.sem, s.count, None) for s in self.slots if s.count > 0]
        t = self.op("act", lambda a: a.activation(out=scratch_ap, in_=scratch_ap, func=AF.Identity), deps=toks)
        for e in self.ENG:
            if e != "act":
                self._waits(e, [t])
        return t

    def finish(self):
        toks = [(s.sem, s.count, None) for s in self.slots if s.count > 0]
        self._waits("sp", toks)


def build_nc():
    nc = bass.Bass("TRN2", target_bir_lowering=False)

    blob = nc.dram_tensor("blob", [BLOB_N], F32, kind="ExternalInput").ap()

    def view(name):
        off, shape = BLOB_OFF[name]
        n = int(np.prod(shape))
        v = blob[off:off + n]
        if len(shape) == 2:
            v = v.rearrange("(a b) -> a b", b=shape[1])
        return v

    x = view("x")
    c = view("c")
    norm_gain = view("norm_gain")
    w_ada = view("w_ada")
    b_ada = view("b_ada")
    w_in = view("w_in")
    b_merge = view("b_merge")
    conv_a_w = view("conv_a_w")
    w_out_a = view("w_out_a")
    conv_b_w = view("conv_b_w")
    conv_b_bias = view("conv_b_bias")
    ln_b_gain = view("ln_b_gain")
    ln_b_bias = view("ln_b_bias")
    w_out_b = view("w_out_b")
    b_out_b = view("b_out_b")
    w_o = view("w_o")
    final_gain = view("final_gain")
    y = nc.dram_tensor("y", [S, D], F32, kind="ExternalOutput").ap()

    sp_ya = nc.dram_tensor("sp_ya", [NT, 128, NJ, TT], BF16).ap()
    sp_uc = nc.dram_tensor("sp_uc", [NT, 128, NJ, TT], BF16).ap()
    sp_gz = nc.dram_tensor("sp_gz", [NT, 128, NJ, TT], BF16).ap()
    sp_sa = nc.dram_tensor("sp_sa", [NT, 128, NJ, TT], BF16).ap()
    sp_sb = nc.dram_tensor("sp_sb", [NT, 128, NJ, TT], BF16).ap()

    sc_woa = nc.dram_tensor("sc_woa", [128, NJ, D], BF16).ap()
    sc_wob = nc.dram_tensor("sc_wob", [128, NJ, D], BF16).ap()
    sc_wo = nc.dram_tensor("sc_wo", [128, NJ, D], BF16).ap()

    with ExitStack() as G:
        S_ = Sched(nc, G)

        def sb(stack, name, shape, dt):
            return stack.enter_context(nc.sbuf_tensor(name, shape, dt))

        ident_bf = sb(G, "ident_bf", [128, 128], BF16)
        ident_f = sb(G, "ident_f", [128, 128], F32)
        ones_bf = sb(G, "ones_bf", [128, 128], BF16)
        cols = sb(G, "cols", [128, NJ * NPRM], F32)
        gate_bc = sb(G, "gate_bc", [128, D], F32)
        ss = sb(G, "ss", [128, 32], F32)
        rstd = sb(G, "rstd", [128, 32], F32)
        ss2 = sb(G, "ss2", [128, 32], F32)
        rstd2 = sb(G, "rstd2", [128, 32], F32)
        bar = sb(G, "bar", [128, 2], F32)
        epsc = sb(G, "epsc", [128, 1], F32)
        banks = [nc.alloc_psum_tensor("bank%d" % i, [128, 512], F32) for i in range(8)]
        bankB = [Buf() for _ in range(8)]

        def col(j, r):
            return cols[:, j * NPRM + r: j * NPRM + r + 1]

        b_ident_bf, b_ident_f, b_ones, b_cols, b_gate = Buf(), Buf(), Buf(), Buf(), Buf()
        b_ss, b_ss2 = Buf(), Buf()

        def mk_ident(t, b):
            S_.op("pool", lambda g: g.memset(t[:, :], 1.0), writes=[b])
            S_.op("pool", lambda g: g.affine_select(out=t[:, :], in_=t[:, :], pattern=[[-1, 128]],
                                                    compare_op=ALU.is_equal, fill=0.0, base=0,
                                                    channel_multiplier=1),
                  reads=[b], writes=[b])

        mk_ident(ident_bf, b_ident_bf)
        mk_ident(ident_f, b_ident_f)
        S_.op("pool", lambda g: g.memset(ones_bf[:, :], 1.0 / 1024.0), writes=[b_ones])
        S_.op("pool", lambda g: g.memset(ss[:, :], 0.0), writes=[b_ss])
        S_.op("pool", lambda g: g.memset(ss2[:, :], 0.0), writes=[b_ss2])
        S_.op("pool", lambda g: g.memset(bar[:, :], 0.0))
        b_eps = Buf()
        S_.op("pool", lambda g: g.memset(epsc[:, :], EPS), writes=[b_eps])

        A = ExitStack()
        hT = sb(A, "hT", [128, NJ, S], BF16)
        wj = [sb(A, "wj%d" % i, [128, 9, NJ, 128], BF16) for i in range(2)]
        b_wj = [Buf(), Buf()]
        sl_wj = [S_.slot("sl_wj%d" % i) for i in range(2)]
        b_hT = [Buf() for _ in range(32)]

        def load_wj(j):
            p = j % 2
            src = w_in.rearrange("(kc p) (blk j m) -> j p blk kc m", p=128, blk=9, j=NJ, m=128)
            t = None
            for blk in range(9):
                t = S_.dma("pool", sl_wj[p], wj[p][:, blk, :, :], src[j, :, blk, :, :],
                           writes=[b_wj[p]] if blk == 0 else [], deps=[] if blk == 0 else [])
            b_wj[p].w = t
            return t

        A0 = ExitStack()
        prm = sb(A0, "prm", [NPRM, D], F32)
        onesF = sb(A0, "onesF", [128, 128], F32)
        lhsT_bc = sb(A0, "lhsT_bc", [128, NJ, 128], BF16)
        c_act = sb(A0, "c_act", [128, NJ], F32)
        wada = [sb(A0, "wada%d" % i, [128, 1536], BF16) for i in range(4)]
        mod_sb = sb(A0, "mod_sb", [128, 3 * D], F32)
        ng_bc = sb(A0, "ng_bc", [128, D], F32)
        gprime = sb(A0, "gprime", [128, D], F32)
        xt = [sb(A0, "xt%d" % i, [128, D], F32) for i in range(4)]
        t1 = [sb(A0, "t1_%d" % i, [128, D], F32) for i in range(2)]
        hb = [sb(A0, "hb%d" % i, [128, D], BF16) for i in range(2)]
        junk = sb(A0, "junk", [128, D], F32)

        sl_prm = S_.slot("sl_prm")
        b_prm = Buf()
        rows = [
            (R_BMA, 2, b_merge.rearrange("(r n) -> r n", r=2)),
            (R_CAW, 3, conv_a_w),
            (R_CBW, 31, conv_b_w),
            (R_CBB, 1, conv_b_bias.rearrange("(r n) -> r n", r=1)),
            (R_LNG, 1, ln_b_gain.rearrange("(r n) -> r n", r=1)),
            (R_LNB, 1, ln_b_bias.rearrange("(r n) -> r n", r=1)),
            (R_BOB, 1, b_out_b.rearrange("(r n) -> r n", r=1)),
            (R_C, 1, c.rearrange("(r n) -> r n", r=1)),
        ]
        t = None
        for (r0, n, src) in rows:
            t = S_.dma("sp", sl_prm, prm[r0:r0 + n, :], src)
        b_prm.w = t

        def tr_prm(pe):
            ins = None
            for kc in range(NJ):
                ins = pe.transpose(banks[0][:, kc * NPRM:(kc + 1) * NPRM], prm[0:NPRM, kc * 128:(kc + 1) * 128],
                                   ident_f[0:NPRM, 0:NPRM])
            return ins

        S_.op("pe", tr_prm, reads=[b_prm, b_ident_f], writes=[bankB[0]])
        S_.op("dve", lambda v: v.tensor_copy(out=cols[:, :], in_=banks[0][:, 0:NJ * NPRM]),
              reads=[bankB[0]], writes=[b_cols])

        b_cact, b_lhs, b_mod, b_ng, b_gp = Buf(), Buf(), Buf(), Buf(), Buf()
        cols3 = cols[:, :].rearrange("p (k r) -> p k r", r=NPRM)
        S_.op("act", lambda a: a.activation(out=c_act[:, :], in_=cols3[:, :, R_C], func=AF.Silu),
              reads=[b_cols], writes=[b_cact])
        S_.op("pool", lambda g: g.memset(onesF[:, :], 1.0))
        b_onesF = Buf()
        b_onesF.w = (S_.tl["pool"], S_.cnt["pool"], "pool")

        def mk_lhs(v):
            ins = None
            for kc in range(NJ):
                ins = v.tensor_scalar(out=lhsT_bc[:, kc, :], in0=onesF[:, :], scalar1=c_act[:, kc:kc + 1],
                                      scalar2=None, op0=ALU.mult)
            return ins

        S_.op("dve", mk_lhs, reads=[b_cact, b_onesF], writes=[b_lhs])
        sl_misc = S_.slot("sl_misc")
        S_.dma("sp", sl_misc, mod_sb[:, :], b_ada.partition_broadcast(128), writes=[b_mod])
        S_.dma("sp", sl_misc, ng_bc[:, :], norm_gain.partition_broadcast(128), writes=[b_ng])
        b_mod.w = (sl_misc.sem, sl_misc.count, None)
        b_ng.w = (sl_misc.sem, sl_misc.count, None)

        sl_wada = [S_.slot("sl_wada%d" % i) for i in range(4)]
        b_wada = [Buf() for _ in range(4)]
        q = 0
        for kc in range(NJ):
            for h in range(2):
                p = q % 4
                q += 1
                S_.dma("pool", sl_wada[p], wada[p][:, :], w_ada[kc * 128:(kc + 1) * 128, h * 1536:(h + 1) * 1536],
                       writes=[b_wada[p]])

                def mm(pe, kc=kc, h=h, p=p):
                    ins = None
                    for n in range(3):
                        ins = pe.matmul(banks[1 + h * 3 + n][:, :], lhsT=lhsT_bc[:, kc, :],
                                        rhs=wada[p][:, n * 512:(n + 1) * 512], start=(kc == 0), stop=(kc == NJ - 1))
                    return ins

                S_.op("pe", mm, reads=[b_wada[p], b_lhs], writes=[bankB[1 + h * 3 + n] for n in range(3)])
        for n in range(6):
            S_.op("dve", lambda v, n=n: v.tensor_tensor(out=mod_sb[:, n * 512:(n + 1) * 512],
                                                        in0=banks[1 + n][:, :], in1=mod_sb[:, n * 512:(n + 1) * 512],
                                                        op=ALU.add),
                  reads=[bankB[1 + n]], writes=[b_mod])
        S_.op("dve", lambda v: v.scalar_tensor_tensor(out=gprime[:, :], in0=mod_sb[:, D:2 * D], scalar=1.0,
                                                      in1=ng_bc[:, :], op0=ALU.add, op1=ALU.mult),
              reads=[b_mod, b_ng], writes=[b_gp])
        S_.op("dve", lambda v: v.tensor_copy(out=gate_bc[:, :], in_=mod_sb[:, 2 * D:3 * D]),
              reads=[b_mod], writes=[b_gate])

        load_wj(0)

        NXT = 4
        sl_xt = [S_.slot("sl_xt%d" % i) for i in range(NXT)]
        b_xt = [Buf() for _ in range(NXT)]
        b_t1 = [Buf(), Buf()]
        b_hb = [Buf(), Buf()]
        b_rstd = [Buf() for _ in range(32)]

        def p0_load(tc):
            s4 = tc % NXT
            S_.dma("sp", sl_xt[s4], xt[s4][:, :], x[tc * 128:(tc + 1) * 128, :], writes=[b_xt[s4]])

        def p0_stats_act(tc):
            s4 = tc % NXT
            tss = S_.op("act", lambda a, s4=s4, tc=tc: a.activation(out=junk[:, :], in_=xt[s4][:, :], func=AF.Square,
                                                                    accum_out=ss[:, tc:tc + 1]),
                        reads=[b_xt[s4], b_ss], writes=[])
            return S_.op("act", lambda a, tc=tc: a.activation(out=rstd[:, tc:tc + 1], in_=ss[:, tc:tc + 1],
                                                              func=AF.Sqrt, scale=1.0 / D, bias=epsc[:, 0:1]),
                         reads=[b_eps], deps=[tss])

        def p0_recip(tc, tsq):
            S_.op("dve", lambda v, tc=tc: v.reciprocal(out=rstd[:, tc:tc + 1], in_=rstd[:, tc:tc + 1]),
                  deps=[tsq], writes=[b_rstd[tc]])

        def p0_main(tc):
            s4 = tc % NXT
            s2 = tc % 2
            S_.op("dve", lambda v, s4=s4, s2=s2, tc=tc: v.scalar_tensor_tensor(
                out=t1[s2][:, :], in0=xt[s4][:, :], scalar=rstd[:, tc:tc + 1], in1=gprime[:, :],
                op0=ALU.mult, op1=ALU.mult),
                reads=[b_xt[s4], b_rstd[tc], b_gp], writes=[b_t1[s2]])
            S_.op("dve", lambda g, s2=s2: g.tensor_tensor(out=hb[s2][:, :], in0=t1[s2][:, :], in1=mod_sb[:, 0:D],
                                                          op=ALU.add),
                  reads=[b_t1[s2], b_mod], writes=[b_hb[s2]])
            bk = 6 + s2
            bbf = banks[bk][:, :].bitcast(BF16)

            def trs(pe, s2=s2, bbf=bbf):
                ins = None
                for kc in range(NJ):
                    ins = pe.transpose(bbf[:, kc * 128:(kc + 1) * 128], hb[s2][:, kc * 128:(kc + 1) * 128],
                                       ident_bf[:, :])
                return ins

            S_.op("pe", trs, reads=[b_hb[s2], b_ident_bf], writes=[bankB[bk]])

        def p0_evac(tc):
            bk = 6 + tc % 2
            bbf = banks[bk][:, :].bitcast(BF16)
            S_.op("act", lambda a, tc=tc, bbf=bbf: a.activation(
                out=hT[:, :, tc * 128:(tc + 1) * 128], in_=bbf.rearrange("p (k t) -> p k t", k=NJ),
                func=AF.Identity),
                reads=[bankB[bk]], writes=[b_hT[tc]])

        p0_load(0)
        p0_load(1)
        tsqs = {0: p0_stats_act(0)}
        p0_recip(0, tsqs[0])
        for tc in range(33):
            if tc + 2 < 32:
                p0_load(tc + 2)
            if tc + 1 < 32:
                tsqs[tc + 1] = p0_stats_act(tc + 1)
            if tc < 32:
                p0_main(tc)
            if tc + 1 < 32:
                p0_recip(tc + 1, tsqs[tc + 1])
            if tc >= 1:
                p0_evac(tc - 1)

        S_.barrier(bar[:, 0:1])
        A0.close()

        A1 = ExitStack()
        diag = [sb(A1, "diag%d" % i, [128, 31, 128], BF16) for i in range(2)]
        cvb = sb(A1, "cvb", [128, S + 2], F32)
        ub = sb(A1, "ub", [128, S + 30], BF16)
        sz_sb = [sb(A1, "sz%d" % i, [128, TT], F32) for i in range(2)]
        ca_sb = [sb(A1, "ca%d" % i, [128, TT], F32) for i in range(2)]
        sg_sb = [sb(A1, "sg%d" % i, [128, TT], F32) for i in range(2)]
        ga = [sb(A1, "ga%d" % i, [128, TT], F32) for i in range(3)]
        acc = [sb(A1, "acc%d" % i, [128, TT], F32) for i in range(2)]
        cacc = [sb(A1, "cacc%d" % i, [128, TT], F32) for i in range(2)]
        b_cacc = [Buf(), Buf()]
        st_names = ("gz", "sa", "sb", "ya", "uc")
        stg = {n: [sb(A1, "st_%s%d" % (n, i), [128, TT], BF16) for i in range(2)] for n in st_names}
        b_stg = {n: [Buf(), Buf()] for n in st_names}
        sl_stg = {n: [S_.slot("sl_%s%d" % (n, i)) for i in range(2)] for n in st_names}
        sp_of = {"gz": sp_gz, "sa": sp_sa, "sb": sp_sb, "ya": sp_ya, "uc": sp_uc}
        b_sp = {n: [Buf() for _ in range(NT)] for n in st_names}
        b_diag = [Buf(), Buf()]
        b_cv = [Buf() for _ in range(NT)]
        b_u = [Buf() for _ in range(NT)]
        b_sz, b_ca, b_sg = [Buf(), Buf()], [Buf(), Buf()], [Buf(), Buf()]
        b_ga = [Buf(), Buf(), Buf()]
        b_acc = [Buf(), Buf()]

        S_.op("pool", lambda g: g.memset(cvb[:, 0:1], 0.0))
        S_.op("pool", lambda g: g.memset(cvb[:, S + 1:S + 2], 0.0))
        S_.op("pool", lambda g: g.memset(ub[:, 0:15], 0.0))
        S_.op("pool", lambda g: g.memset(ub[:, S + 15:S + 30], 0.0))
        tpad = (S_.tl["pool"], S_.cnt["pool"], "pool")

        b_scw = {"woa": Buf(), "wob": Buf(), "wo": Buf()}
        sl_scw = S_.slot("sl_scw")
        S_.dma("pool", sl_scw, sc_woa[:, :, :], w_out_a.rearrange("(kc p) n -> p kc n", p=128))
        S_.dma("pool", sl_scw, sc_wob[:, :, :], w_out_b.rearrange("(kc p) n -> p kc n", p=128))
        tscw = (sl_scw.sem, sl_scw.count, None)
        b_scw["woa"].w = tscw
        b_scw["wob"].w = tscw
        wos_f = [sb(A1, "wos_f%d" % i, [128, D], F32) for i in range(2)]
        wos_b = [sb(A1, "wos_b%d" % i, [128, D], BF16) for i in range(2)]
        b_wosf, b_wosb = [Buf(), Buf()], [Buf(), Buf()]
        sl_wosf = [S_.slot("sl_wosf%d" % i) for i in range(2)]
        sl_wosb = [S_.slot("sl_wosb%d" % i) for i in range(2)]
        def fold_load(kc):
            p = kc % 2
            S_.dma("sp", sl_wosf[p], wos_f[p][:, :], w_o[kc * 128:(kc + 1) * 128, :], writes=[b_wosf[p]])

        def fold_op(kc):
            p = kc % 2
            S_.op("dve", lambda v, p=p: v.tensor_tensor(out=wos_b[p][:, :], in0=wos_f[p][:, :],
                                                        in1=gate_bc[:, :], op=ALU.mult),
                  reads=[b_wosf[p], b_gate], writes=[b_wosb[p]])
            S_.dma("sp", sl_wosb[p], sc_wo[:, kc, :], wos_b[p][:, :], reads=[b_wosb[p]], writes=[b_scw["wo"]])

        ring = [0]
        convring = [0]

        def next_bank():
            b = ring[0] % 6
            ring[0] += 1
            return b

        def proj(j, i, blk):
            p = j % 2
            bk = next_bank()

            def fn(pe, p=p, blk=blk, i=i, bk=bk):
                ins = None
                for kc in range(NJ):
                    ins = pe.matmul(banks[bk][:, :], lhsT=wj[p][:, blk, kc, :], rhs=hT[:, kc, i * TT:(i + 1) * TT],
                                    start=(kc == 0), stop=(kc == NJ - 1))
                return ins

            S_.op("pe", fn, reads=[b_wj[p]] + b_hT[i * 4:(i + 1) * 4], writes=[bankB[bk]])
            return bk

        def store(name, slot_i, il, j):
            S_.dma("sp", sl_stg[name][slot_i], sp_of[name][il, :, j, :], stg[name][slot_i][:, :],
                   reads=[b_stg[name][slot_i]], writes=[b_sp[name][il]])

        stc = [0]

        def lagged_parts(j, il, n, n_pe=N_PE_TAPS):
            p = j % 2
            c0 = il * TT
            s = stc[0] % 2
            stc[0] += 1
            cb = 6 + (convring[0] % 2)
            convring[0] += 1
            ureads = [b_u[t] for t in (il - 1, il, il + 1) if 0 <= t < NT]
            cvreads = [b_cv[t] for t in (il - 1, il, il + 1) if 0 <= t < NT]
            gslot = n % 3

            def part_pe():
                def cfn(pe):
                    ins = None
                    for k in range(n_pe):
                        ins = pe.matmul(banks[cb][:, :], lhsT=diag[p][:, k, :], rhs=ub[:, c0 + k:c0 + k + TT],
                                        start=(k == 0), stop=(k == n_pe - 1))
                    return ins

                S_.op("pe", cfn, reads=[b_diag[p]] + ureads, writes=[bankB[cb]], deps=[tpad])
                if n_pe == 31:
                    S_.op("act", lambda a: a.activation(out=stg["uc"][s][:, :], in_=banks[cb][:, :],
                                                        func=AF.Identity, bias=col(j, R_CBB)),
                          reads=[bankB[cb], b_cols], writes=[b_stg["uc"][s]])
                    store("uc", s, il, j)
                else:
                    S_.op("act", lambda a: a.activation(out=cacc[s][:, :], in_=banks[cb][:, :],
                                                        func=AF.Identity, bias=col(j, R_CBB)),
                          reads=[bankB[cb], b_cols], writes=[b_cacc[s]])

            def taps(k0, k1):
                def emit():
                    for k in range(k0, k1):
                        last = (k == 30)
                        dst = stg["uc"][s] if last else cacc[s]
                        S_.op("dve", lambda v, k=k, dst=dst: v.scalar_tensor_tensor(
                            out=dst[:, :], in0=ub[:, c0 + k:c0 + k + TT], scalar=col(j, R_CBW + k),
                            in1=cacc[s][:, :], op0=ALU.mult, op1=ALU.add),
                            reads=ureads + [b_cacc[s], b_cols],
                            writes=[b_stg["uc"][s]] if last else [b_cacc[s]])
                    if k1 == 31:
                        store("uc", s, il, j)
                return emit

            def part_a():
                S_.op("dve", lambda g: g.tensor_scalar(
                    out=acc[s][:, :], in0=cvb[:, c0:c0 + TT], scalar1=col(j, R_CAW), scalar2=None, op0=ALU.mult),
                    reads=cvreads + [b_cols], writes=[b_acc[s]], deps=[tpad])
                for k in (1, 2):
                    S_.op("dve", lambda g, k=k: g.scalar_tensor_tensor(
                        out=acc[s][:, :], in0=cvb[:, c0 + k:c0 + k + TT], scalar=col(j, R_CAW + k),
                        in1=acc[s][:, :], op0=ALU.mult, op1=ALU.add),
                        reads=cvreads, writes=[b_acc[s]])
                S_.op("dve", lambda g: g.tensor_tensor(
                    out=stg["ya"][s][:, :], in0=acc[s][:, :], in1=ga[gslot][:, :], op=ALU.mult),
                    reads=[b_acc[s], b_ga[gslot]], writes=[b_stg["ya"][s]])
                store("ya", s, il, j)

            nd = 31 - n_pe
            if nd == 0:
                return [part_pe, part_a]
            c1 = n_pe + nd // 3
            c2 = n_pe + (2 * nd) // 3
            return [part_pe, taps(n_pe, c1), taps(c1, c2), taps(c2, 31), part_a]

        def build_diag(j):
            p = j % 2
            S_.op("dve", lambda v, p=p, j=j: v.tensor_tensor(
                out=diag[p][:, :, :],
                in0=ident_bf[:, :].unsqueeze(1).to_broadcast([128, 31, 128]),
                in1=cols[:, j * NPRM + R_CBW:j * NPRM + R_CBW + 31].unsqueeze(2).to_broadcast([128, 31, 128]),
                op=ALU.mult),
                reads=[b_ident_bf, b_cols], writes=[b_diag[p]])

        build_diag(0)
        NIT = NJ * NT
        for n in range(NIT):
            j, i = divmod(n, NT)
            p = j % 2
            if i == 0 and j + 1 < NJ:
                load_wj(j + 1)
            if i == 2 and j + 1 < NJ:
                build_diag(j + 1)
            c0 = i * TT
            s = n % 2
            parts = None
            if n >= 2:
                jl, il = divmod(n - 2, NT)
                parts = lagged_parts(jl, il, n - 2)
                parts[0]()
            bk_za = proj(j, i, BLK_ZA)
            bk_zb = proj(j, i, BLK_ZB)
            bk_ba = proj(j, i, BLK_BA)
            bk_gb = proj(j, i, BLK_GB)
            bk_ma = proj(j, i, BLK_MA)
            bk_mb = proj(j, i, BLK_MB)
            S_.op("act", lambda a, s=s, bk=bk_za: a.activation(out=sz_sb[s][:, :], in_=banks[bk][:, :],
                                                               func=AF.Silu),
                  reads=[bankB[bk_za]], writes=[b_sz[s]])
            S_.op("act", lambda a, s=s, bk=bk_zb: a.activation(out=stg["gz"][s][:, :], in_=banks[bk][:, :],
                                                               func=AF.Silu),
                  reads=[bankB[bk_zb]], writes=[b_stg["gz"][s]])
            store("gz", s, i, j)
            S_.op("dve", lambda v, s=s, bk=bk_ba, n=n: v.tensor_tensor(
                out=ga[n % 3][:, :], in0=banks[bk][:, :], in1=sz_sb[s][:, :], op=ALU.mult),
                reads=[bankB[bk_ba], b_sz[s]], writes=[b_ga[n % 3]])
            if parts:
                parts[1]()
            S_.op("act", lambda a, s=s, bk=bk_gb: a.activation(out=sg_sb[s][:, :], in_=banks[bk][:, :],
                                                               func=AF.Sigmoid),
                  reads=[bankB[bk_gb]], writes=[b_sg[s]])
            S_.op("act", lambda a, s=s, bk=bk_ma, j=j: a.activation(
                out=stg["sa"][s][:, :], in_=banks[bk][:, :], func=AF.Sigmoid, bias=col(j, R_BMA)),
                reads=[bankB[bk_ma], b_cols], writes=[b_stg["sa"][s]])
            store("sa", s, i, j)
            S_.op("act", lambda a, s=s, bk=bk_mb, j=j: a.activation(
                out=stg["sb"][s][:, :], in_=banks[bk][:, :], func=AF.Sigmoid, bias=col(j, R_BMB)),
                reads=[bankB[bk_mb], b_cols], writes=[b_stg["sb"][s]])
            store("sb", s, i, j)
            bk_ab = proj(j, i, BLK_AB)
            bk_ca = proj(j, i, BLK_CA)
            bk_va = proj(j, i, BLK_VA)
            S_.op("dve", lambda v, s=s, bk=bk_ab, c0=c0: v.tensor_tensor(
                out=ub[:, 15 + c0:15 + c0 + TT], in0=banks[bk][:, :], in1=sg_sb[s][:, :], op=ALU.mult),
                reads=[bankB[bk_ab], b_sg[s]], writes=[b_u[i]])
            if parts:
                parts[2]()
            S_.op("act", lambda a, s=s, bk=bk_ca: a.activation(out=ca_sb[s][:, :], in_=banks[bk][:, :],
                                                               func=AF.Identity),
                  reads=[bankB[bk_ca]], writes=[b_ca[s]])
            S_.op("dve", lambda v, s=s, bk=bk_va, c0=c0: v.tensor_tensor(
                out=cvb[:, 1 + c0:1 + c0 + TT], in0=banks[bk][:, :], in1=ca_sb[s][:, :], op=ALU.mult),
                reads=[bankB[bk_va], b_ca[s]], writes=[b_cv[i]])
            if parts:
                parts[3]()
                parts[4]()
            if 4 <= n < 4 + NJ:
                fold_op(n - 4)
            if 2 <= n < 2 + NJ:
                fold_load(n - 2)
        for n in (NIT - 2, NIT - 1):
            jl, il = divmod(n, NT)
            for part in lagged_parts(jl, il, n, n_pe=31):
                part()

        S_.barrier(bar[:, 0:1])
        A1.close()
        A.close()

        B = ExitStack()
        woa_bf = sb(B, "woa_bf", [128, NJ, D], BF16)
        wob_bf = sb(B, "wob_bf", [128, NJ, D], BF16)
        wo_bf = sb(B, "wo_bf", [128, NJ, D], BF16)
        fg_bc = sb(B, "fg_bc", [128, D], F32)
        ld_names = ("uc", "ya", "gz", "sa", "sb")
        nring = {"uc": 2, "ya": 2, "gz": 1, "sa": 1, "sb": 1}
        CHUNKED = ("gz", "sa", "sb")
        ldt = {n: [sb(B, "ld_%s%d" % (n, i), [128, NJ, TT], BF16) for i in range(nring[n])] for n in ld_names}
        b_ldt = {n: [Buf() for _ in range(nring[n])] for n in ("uc", "ya")}
        sl_ldt = {n: [S_.slot("sl_ld_%s%d" % (n, i)) for i in range(nring[n])] for n in ("uc", "ya")}
        b_ldc = {n: [Buf() for _ in range(NJ)] for n in CHUNKED}
        sl_ldc = {n: [S_.slot("sl_ldc_%s%d" % (n, k)) for k in range(NJ)] for n in CHUNKED}
        sq = [sb(B, "sq%d" % i, [128, TT], BF16) for i in range(2)]
        mean_sb = [sb(B, "mean_sb%d" % i, [128, TT], F32) for i in range(2)]
        m2 = sb(B, "m2", [128, TT], F32)
        rstd_t = [sb(B, "rstd_t%d" % i, [128, TT], F32) for i in range(2)]
        tL = sb(B, "tL", [128, 4, TT], F32)
        b_tL = Buf()
        ubp = [sb(B, "ubp%d" % i, [128, NJ, TT], BF16) for i in range(2)]
        u1 = [sb(B, "u1_%d" % i, [128, TT], F32) for i in range(2)]
        u2 = [sb(B, "u2_%d" % i, [128, TT], F32) for i in range(2)]
        mg = sb(B, "mg", [128, NJ, TT], BF16)
        NX2 = 3
        xt2 = [sb(B, "xt2_%d" % i, [128, D], F32) for i in range(NX2)]
        xn = [sb(B, "xn%d" % i, [128, D], F32) for i in range(2)]
        ot = [sb(B, "ot%d" % i, [128, D], F32) for i in range(2)]
        junk2 = sb(B, "junk2", [128, D], BF16)

        b_woa, b_wob, b_wo, b_fg = Buf(), Buf(), Buf(), Buf()
        b_sq = [Buf(), Buf()]
        b_mean, b_rstdt = [Buf(), Buf()], [Buf(), Buf()]
        b_m2 = Buf()
        b_ubp = [[Buf() for _ in range(NJ)] for _ in range(2)]
        b_u1, b_u2 = [Buf(), Buf()], [Buf(), Buf()]
        b_mg = [Buf() for _ in range(NJ)]
        b_xt2 = [Buf() for _ in range(NX2)]
        sl_xt2 = [S_.slot("sl_xt2_%d" % i) for i in range(NX2)]
        b_xn, b_ot = [Buf(), Buf()], [Buf(), Buf()]
        sl_ot = [S_.slot("sl_ot%d" % i) for i in range(2)]
        b_rstd2 = [Buf() for _ in range(32)]

        def load_tile(i, names):
            for n in names:
                if n in CHUNKED:
                    for k in range(NJ):
                        load_chunk(n, i, k)
                    continue
                r = i % nring[n]
                S_.dma("sp", sl_ldt[n][r], ldt[n][r][:, :, :], sp_of[n][i, :, :, :],
                       reads=[b_sp[n][i]], writes=[b_ldt[n][r]])

        def load_chunk(n, i, k):
            S_.dma("sp", sl_ldc[n][k], ldt[n][0][:, k, :], sp_of[n][i, :, k, :],
                   reads=[b_sp[n][i]], writes=[b_ldc[n][k]])

        def T(n, i):
            return ldt[n][i % nring[n]]

        def BT(n, i):
            return b_ldt[n][i % nring[n]]

        REST = ("gz", "ya", "sa", "sb")
        load_tile(0, ("uc",))
        load_tile(1, ("uc",))
        load_tile(0, REST)
        sl_fg = S_.slot("sl_fg")
        S_.dma("sp", sl_fg, fg_bc[:, :], final_gain.partition_broadcast(128), writes=[b_fg])
        sl_w3 = [S_.slot("sl_w3_%d" % i) for i in range(3)]
        S_.dma("sp", sl_w3[0], woa_bf[:, :, :], sc_woa[:, :, :], reads=[b_scw["woa"]], writes=[b_woa])
        S_.dma("sp", sl_w3[1], wob_bf[:, :, :], sc_wob[:, :, :], reads=[b_scw["wob"]], writes=[b_wob])
        S_.dma("sp", sl_w3[2], wo_bf[:, :, :], sc_wo[:, :, :], reads=[b_scw["wo"]], writes=[b_wo])

        BK_MEAN, BK_MSQ = 0, 1
        yring = [0]
        oring = [0]

        def s1_sq(i, k):
            s = k % 2
            S_.op("act", lambda a, s=s, k=k, t=T("uc", i): a.activation(out=sq[s][:, :], in_=t[:, k, :],
                                                                        func=AF.Square),
                  reads=[BT("uc", i)], writes=[b_sq[s]])

        def s1_mm(i, k):
            s = k % 2

            def stat(pe, s=s, k=k, t=T("uc", i)):
                pe.matmul(banks[BK_MEAN][:, :], lhsT=ones_bf[:, :], rhs=t[:, k, :], start=(k == 0),
                          stop=(k == NJ - 1))
                return pe.matmul(banks[BK_MSQ][:, :], lhsT=ones_bf[:, :], rhs=sq[s][:, :], start=(k == 0),
                                 stop=(k == NJ - 1))

            S_.op("pe", stat, reads=[BT("uc", i), b_sq[s], b_ones], writes=[bankB[BK_MEAN], bankB[BK_MSQ]])

        def s1_fin(i):
            r = i % 2
            S_.op("act", lambda a, r=r: a.activation(out=mean_sb[r][:, :], in_=banks[BK_MEAN][:, :],
                                                     func=AF.Identity),
                  reads=[bankB[BK_MEAN]], writes=[b_mean[r]])
            S_.op("act", lambda a: a.activation(out=m2[:, :], in_=banks[BK_MEAN][:, :], func=AF.Square),
                  reads=[bankB[BK_MEAN]], writes=[b_m2])
            S_.op("dve", lambda v: v.tensor_tensor(out=m2[:, :], in0=banks[BK_MSQ][:, :], in1=m2[:, :],
                                                   op=ALU.subtract),
                  reads=[bankB[BK_MSQ], b_m2], writes=[b_m2])
            S_.op("dve", lambda v: v.tensor_scalar(out=m2[:, :], in0=m2[:, :], scalar1=0.0, scalar2=None,
                                                   op0=ALU.max),
                  reads=[b_m2], writes=[b_m2])
            S_.op("act", lambda a, r=r: a.activation(out=rstd_t[r][:, :], in_=m2[:, :], func=AF.Sqrt,
                                                     bias=epsc[:, 0:1]),
                  reads=[b_m2, b_eps], writes=[b_rstdt[r]])
            S_.op("dve", lambda v, r=r: v.reciprocal(out=rstd_t[r][:, :], in_=rstd_t[r][:, :]),
                  reads=[b_rstdt[r]], writes=[b_rstdt[r]])

        def s2_sub(i, h):
            r = i % 2
            S_.op("dve", lambda v, r=r, h=h, t=T("uc", i): v.tensor_tensor(
                out=tL[:, :, :], in0=t[:, 4 * h:4 * h + 4, :],
                in1=mean_sb[r][:, :].unsqueeze(1).to_broadcast([128, 4, TT]), op=ALU.subtract),
                reads=[BT("uc", i), b_mean[r]], writes=[b_tL])

        def s2_mul(i, h):
            r = i % 2
            S_.op("dve", lambda v, r=r: v.tensor_tensor(
                out=tL[:, :, :], in0=tL[:, :, :],
                in1=rstd_t[r][:, :].unsqueeze(1).to_broadcast([128, 4, TT]), op=ALU.mult),
                reads=[b_tL, b_rstdt[r]], writes=[b_tL])
            for kk in range(4):
                k = 4 * h + kk
                S_.op("act", lambda a, kk=kk, k=k: a.activation(out=tL[:, kk, :], in_=tL[:, kk, :], func=AF.Silu,
                                                                scale=col(k, R_LNG), bias=col(k, R_LNB)),
                      reads=[b_tL, b_cols], writes=[b_tL])

        def s2_gate(i, h):
            r = i % 2
            ks = list(range(4 * h, 4 * h + 4))
            S_.op("dve", lambda v, r=r, h=h, t=T("gz", i): v.tensor_tensor(
                out=ubp[r][:, 4 * h:4 * h + 4, :], in0=tL[:, :, :], in1=t[:, 4 * h:4 * h + 4, :], op=ALU.mult),
                reads=[b_tL] + [b_ldc["gz"][k] for k in ks], writes=[b_ubp[r][k] for k in ks])

        def s3(i, dj):
            s = dj % 2
            r = i % 2
            bka = 2 + (yring[0] % 4)
            yring[0] += 1
            bkb = 2 + (yring[0] % 4)
            yring[0] += 1

            def mma(pe, dj=dj, bka=bka, t=T("ya", i)):
                ins = None
                for kc in range(NJ):
                    ins = pe.matmul(banks[bka][:, :], lhsT=woa_bf[:, kc, dj * 128:(dj + 1) * 128],
                                    rhs=t[:, kc, :], start=(kc == 0), stop=(kc == NJ - 1))
                return ins

            def mmb(pe, dj=dj, bkb=bkb, r=r):
                ins = None
                for kc in range(NJ):
                    ins = pe.matmul(banks[bkb][:, :], lhsT=wob_bf[:, kc, dj * 128:(dj + 1) * 128],
                                    rhs=ubp[r][:, kc, :], start=(kc == 0), stop=(kc == NJ - 1))
                return ins

            S_.op("pe", mma, reads=[b_woa, BT("ya", i)], writes=[bankB[bka]])
            S_.op("pe", mmb, reads=[b_wob] + b_ubp[r], writes=[bankB[bkb]])
            S_.op("dve", lambda v, s=s, dj=dj, bka=bka, t=T("sa", i): v.tensor_tensor(
                out=u1[s][:, :], in0=banks[bka][:, :], in1=t[:, dj, :], op=ALU.mult),
                reads=[bankB[bka], b_ldc["sa"][dj]], writes=[b_u1[s]])
            S_.op("dve", lambda v, s=s, dj=dj, bkb=bkb, t=T("sb", i): v.scalar_tensor_tensor(
                out=u2[s][:, :], in0=banks[bkb][:, :], scalar=col(dj, R_BOB), in1=t[:, dj, :],
                op0=ALU.add, op1=ALU.mult),
                reads=[bankB[bkb], b_ldc["sb"][dj], b_cols], writes=[b_u2[s]])
            S_.op("dve", lambda g, s=s, dj=dj: g.tensor_tensor(out=mg[:, dj, :], in0=u1[s][:, :],
                                                               in1=u2[s][:, :], op=ALU.add),
                  reads=[b_u1[s], b_u2[s]], writes=[b_mg[dj]])

        def s4_load(i, tq):
            tcg = i * 4 + tq
            s4 = tcg % NX2
            S_.dma("sp", sl_xt2[s4], xt2[s4][:, :], x[tcg * 128:(tcg + 1) * 128, :], writes=[b_xt2[s4]])

        def s4_a(i, tq):
            tcg = i * 4 + tq
            s4 = tcg % NX2
            s2 = tcg % 2
            for dh in range(2):
                bko = 6 + (oring[0] % 2)
                oring[0] += 1

                def mmo(pe, tq=tq, dh=dh, bko=bko):
                    ins = None
                    for kc in range(NJ):
                        ins = pe.matmul(banks[bko][:, :], lhsT=mg[:, kc, tq * 128:(tq + 1) * 128],
                                        rhs=wo_bf[:, kc, dh * 512:(dh + 1) * 512], start=(kc == 0),
                                        stop=(kc == NJ - 1))
                    return ins

                S_.op("pe", mmo, reads=[b_wo] + b_mg, writes=[bankB[bko]])
                S_.op("dve", lambda v, s2=s2, s4=s4, dh=dh, bko=bko: v.tensor_tensor(
                    out=xn[s2][:, dh * 512:(dh + 1) * 512], in0=banks[bko][:, :],
                    in1=xt2[s4][:, dh * 512:(dh + 1) * 512], op=ALU.add),
                    reads=[bankB[bko], b_xt2[s4]], writes=[b_xn[s2]])
            tss = S_.op("act", lambda a, s2=s2, tcg=tcg: a.activation(out=junk2[:, :], in_=xn[s2][:, :],
                                                                      func=AF.Square, accum_out=ss2[:, tcg:tcg + 1]),
                        reads=[b_xn[s2], b_ss2], writes=[])
            return S_.op("act", lambda a, tcg=tcg: a.activation(out=rstd2[:, tcg:tcg + 1], in_=ss2[:, tcg:tcg + 1],
                                                                func=AF.Sqrt, scale=1.0 / D, bias=epsc[:, 0:1]),
                         reads=[b_eps], deps=[tss])

        def s4_b(i, tq, tsq):
            tcg = i * 4 + tq
            s2 = tcg % 2
            S_.op("dve", lambda v, tcg=tcg: v.reciprocal(out=rstd2[:, tcg:tcg + 1], in_=rstd2[:, tcg:tcg + 1]),
                  deps=[tsq], writes=[b_rstd2[tcg]])
            S_.op("dve", lambda v, s2=s2, tcg=tcg: v.scalar_tensor_tensor(
                out=ot[s2][:, :], in0=xn[s2][:, :], scalar=rstd2[:, tcg:tcg + 1], in1=fg_bc[:, :],
                op0=ALU.mult, op1=ALU.mult),
                reads=[b_xn[s2], b_rstd2[tcg], b_fg], writes=[b_ot[s2]])
            S_.dma("sp", sl_ot[s2], y[tcg * 128:(tcg + 1) * 128, :], ot[s2][:, :], reads=[b_ot[s2]])

        for k in range(NJ):
            s1_sq(0, k)
            s1_mm(0, k)
        s1_fin(0)
        for h in range(2):
            s2_sub(0, h)
            s2_mul(0, h)
            s2_gate(0, h)
        load_tile(1, ("gz",))
        for k in range(NJ):
            s1_sq(1, k)
            s1_mm(1, k)
        s1_fin(1)

        for i in range(NT):
            if i + 2 < NT:
                load_tile(i + 2, ("uc",))
            if i + 1 < NT:
                load_tile(i + 1, ("ya",))
            for tq in range(3):
                s4_load(i, tq)
            for dj in range(NJ):
                s3(i, dj)
                if i + 1 < NT:
                    load_chunk("sa", i + 1, dj)
                    load_chunk("sb", i + 1, dj)
                    h = dj // 4
                    if dj % 4 == 0:
                        s2_sub(i + 1, h)
                    elif dj % 4 == 1:
                        s2_mul(i + 1, h)
                    elif dj % 4 == 3:
                        s2_gate(i + 1, h)
                        if i + 2 < NT:
                            for k in range(4 * h, 4 * h + 4):
                                load_chunk("gz", i + 2, k)
            if i + 2 < NT:
                for k in range(NJ):
                    s1_sq(i + 2, k)
                    s1_mm(i + 2, k)
            prev = None
            for tq in range(4):
                tsq = s4_a(i, tq)
                if tq == 0:
                    s4_load(i, 3)
                if prev is not None:
                    s4_b(i, tq - 1, prev)
                prev = tsq
            s4_b(i, 3, prev)
            if i + 2 < NT:
                s1_fin(i + 2)

        S_.finish()
        B.close()

        with nc.Block() as block:
            @block.tensor
            def _(e):
                for f in S_.ops["pe"]:
                    f(e)

            @block.scalar
            def _(e):
                for f in S_.ops["act"]:
                    f(e)

            @block.vector
            def _(e):
                for f in S_.ops["dve"]:
                    f(e)

            @block.gpsimd
            def _(e):
                for f in S_.ops["pool"]:
                    f(e)

            @block.sync
            def _(e):
                for f in S_.ops["sp"]:
                    f(e)
    return nc


_NC = None


def kernel(x, c, norm_gain, w_ada, b_ada, w_in, b_merge, conv_a_w, w_out_a, conv_b_w, conv_b_bias,
           ln_b_gain, ln_b_bias, w_out_b, b_out_b, w_o, final_gain):
    global _NC
    if _NC is None:
        _NC = build_nc()
    nc = _NC

    vals = {
        "norm_gain": norm_gain[0], "w_ada": w_ada[0], "b_ada": b_ada[0], "w_in": w_in[0],
        "b_merge": b_merge[0], "conv_a_w": conv_a_w[0], "w_out_a": w_out_a[0],
        "conv_b_w": conv_b_w[0], "conv_b_bias": conv_b_bias[0], "ln_b_gain": ln_b_gain[0],
        "ln_b_bias": ln_b_bias[0], "w_out_b": w_out_b[0], "b_out_b": b_out_b[0], "w_o": w_o[0],
        "final_gain": final_gain,
    }
    x = np.asarray(x)
    c = np.asarray(c)
    base = np.empty((BLOB_N,), dtype=np.float32)
    for name, shape in BLOB_SHAPES:
        if name in ("x", "c"):
            continue
        off, _ = BLOB_OFF[name]
        a = np.asarray(vals[name], dtype=np.float32).reshape(-1)
        base[off:off + a.size] = a
    in_maps = []
    ox, _ = BLOB_OFF["x"]
    oc, _ = BLOB_OFF["c"]
    for b in range(NCORES):
        m = base.copy()
        m[ox:ox + S * D] = np.asarray(x[b], dtype=np.float32).reshape(-1)
        m[oc:oc + D] = np.asarray(c[b], dtype=np.float32).reshape(-1)
        in_maps.append({"blob": m})
    res = run_bass_kernel_spmd(nc, in_maps, core_ids=list(range(NCORES)))
    out = np.stack([np.asarray(res.results[b]["y"], dtype=np.float32) for b in range(NCORES)], axis=0)
    return out
```
